# Optimizing a Trainium2 kernel written in Bass

```python
import math
import jax
import jax.numpy as jnp
from jax import lax
import numpy as np

D_MODEL = 1024
BATCH = 32
SEQ = 2048
DEPTH = 4

GRID_W = 64
CTX_LEN = 256
Q_BLOCK = 128
ROPE_THETA = 10000.0
EPS = 1e-6

SSM_EXPAND = 2
SSM_D_INNER = SSM_EXPAND * D_MODEL
SSM_HEAD_DIM = 64
SSM_HEADS = SSM_D_INNER // SSM_HEAD_DIM
SSM_GROUPS = 4
SSM_STATE = 128
SSM_CONV = 5
SSM_CHUNK = 128
SSM_BC = SSM_GROUPS * SSM_STATE
SSM_CONV_DIM = SSM_D_INNER + 2 * SSM_BC

GQA_HEAD_DIM = 128
GQA_HEADS = D_MODEL // GQA_HEAD_DIM
GQA_KV_HEADS = 2
GQA_GROUP = GQA_HEADS // GQA_KV_HEADS
GQA_WIDTH = GQA_HEADS * GQA_HEAD_DIM
GQA_KV_WIDTH = GQA_KV_HEADS * GQA_HEAD_DIM

DIFF_HEAD_DIM = 64
DIFF_HEADS = D_MODEL // (2 * DIFF_HEAD_DIM)
DIFF_QK_WIDTH = DIFF_HEADS * 2 * DIFF_HEAD_DIM
DIFF_V_WIDTH = DIFF_HEADS * 2 * DIFF_HEAD_DIM

N_BRANCHES = 3
FFN_HIDDEN = int(math.ceil(8 * D_MODEL / 3 / 256)) * 256

DEEPNORM_ALPHA = (2 * DEPTH) ** 0.25
DEEPNORM_BETA = (8 * DEPTH) ** -0.25

IN_SIZES = (SSM_D_INNER, SSM_CONV_DIM, 2 * SSM_HEADS,
            GQA_WIDTH, GQA_KV_WIDTH, GQA_KV_WIDTH,
            DIFF_QK_WIDTH, DIFF_QK_WIDTH, DIFF_V_WIDTH,
            N_BRANCHES * D_MODEL)
IN_WIDTH = sum(IN_SIZES)
IN_OFFSETS = tuple(int(o) for o in np.cumsum(IN_SIZES)[:-1])

kernel_name = 'hybrid_ssd_gqa_diffattn_dit_trunk'


def layer_norm(x, g, b):
    xf = x.astype(jnp.float32)
    mu = jnp.mean(xf, axis=-1, keepdims=True)
    var = jnp.mean(jnp.square(xf - mu), axis=-1, keepdims=True)
    return ((xf - mu) * lax.rsqrt(var + EPS) * g + b).astype(x.dtype)


def rms_norm(x, g):
    xf = x.astype(jnp.float32)
    return (xf * lax.rsqrt(jnp.mean(xf * xf, axis=-1, keepdims=True) + EPS) * g).astype(x.dtype)


def modulate(x, shift, scale):
    return x * (1 + scale[:, None]) + shift[:, None]


def axial_rope_tables(pos_row, pos_col, head_dim):
    d_axis = head_dim // 2
    inv_freq = ROPE_THETA ** (-jnp.arange(0, d_axis, 2, dtype=jnp.float32) / d_axis)
    ang_r = pos_row.astype(jnp.float32)[:, None] * inv_freq
    ang_c = pos_col.astype(jnp.float32)[:, None] * inv_freq
    ang = jnp.concatenate([ang_r, ang_r, ang_c, ang_c], axis=-1)
    return jnp.cos(ang), jnp.sin(ang)


def rotate_half(u):
    u1, u2 = jnp.split(u, 2, axis=-1)
    return jnp.concatenate([-u2, u1], axis=-1)


def apply_rope(u, cos, sin):
    shape = (1, cos.shape[0]) + (1,) * (u.ndim - 3) + (cos.shape[1],)
    cos = cos.reshape(shape).astype(u.dtype)
    sin = sin.reshape(shape).astype(u.dtype)
    ur, uc = jnp.split(u, 2, axis=-1)
    rot = jnp.concatenate([rotate_half(ur), rotate_half(uc)], axis=-1)
    return u * cos + rot * sin


def depthwise_conv(u, w, bias):
    k, ch = w.shape
    out = lax.conv_general_dilated(
        u, w.astype(u.dtype)[:, None, :], window_strides=(1,),
        padding=[((k - 1) // 2, k // 2)],
        dimension_numbers=('NWC', 'WIO', 'NWC'), feature_group_count=ch)
    return out + bias


def _flip(u):
    return jnp.flip(u, axis=1)


def segsum_exp(a):
    t = a.shape[-1]
    cs = jnp.cumsum(a, axis=-1)
    diff = cs[..., :, None] - cs[..., None, :]
    mask = jnp.tril(jnp.ones((t, t), dtype=bool))
    return jnp.where(mask, jnp.exp(jnp.where(mask, diff, 0.0)), 0.0)


def ssd_scan(xs, dt, a, b, c, h0):
    bsz, seq, nh, hp = xs.shape
    ng, ns = b.shape[2], b.shape[3]
    nr = nh // ng
    nc = seq // SSM_CHUNK
    q = SSM_CHUNK
    xdt = (xs.astype(jnp.float32) * dt[..., None]).reshape(bsz, nc, q, ng, nr, hp)
    adt = jnp.moveaxis((dt * a).reshape(bsz, nc, q, ng, nr), 2, -1)
    bq = b.astype(jnp.float32).reshape(bsz, nc, q, ng, ns)
    cq = c.astype(jnp.float32).reshape(bsz, nc, q, ng, ns)
    a_cs = jnp.cumsum(adt, axis=-1)
    cb = jnp.einsum('bclgn,bcsgn->bcgls', cq, bq)
    y_diag = jnp.einsum('bcgls,bcgrls,bcsgrp->bclgrp', cb, segsum_exp(adt), xdt)
    decay_to_end = jnp.exp(a_cs[..., -1:] - a_cs)
    chunk_states = jnp.einsum('bclgn,bcgrl,bclgrp->bcgrpn', bq, decay_to_end, xdt)
    states = jnp.concatenate(
        [h0.astype(jnp.float32).reshape(bsz, 1, ng, nr, hp, ns), chunk_states], axis=1)
    totals = jnp.moveaxis(a_cs[..., -1], 1, -1)
    totals = jnp.pad(totals, [(0, 0), (0, 0), (0, 0), (1, 0)])
    new_states = jnp.einsum('bgrzc,bcgrpn->bzgrpn', segsum_exp(totals), states)
    states_in, h_final = new_states[:, :-1], new_states[:, -1]
    y_off = jnp.einsum('bclgn,bcgrpn,bcgrl->bclgrp', cq, states_in, jnp.exp(a_cs))
    y = (y_diag + y_off).reshape(bsz, seq, nh, hp).astype(xs.dtype)
    return y, h_final.reshape(bsz, nh, hp, ns)


def ssm_inputs(xbc, dt_raw, p):
    bsz, seq, _ = xbc.shape
    xbc = jax.nn.silu(depthwise_conv(xbc, p['ssm_conv_w'], p['ssm_conv_b']))
    xs, b, c = jnp.split(xbc, [SSM_D_INNER, SSM_D_INNER + SSM_BC], axis=-1)
    dt = jax.nn.softplus(dt_raw.astype(jnp.float32)
                         + p['ssm_dt_bias'].reshape(-1).astype(jnp.float32))
    return (xs.reshape(bsz, seq, SSM_HEADS, SSM_HEAD_DIM),
            b.reshape(bsz, seq, SSM_GROUPS, SSM_STATE),
            c.reshape(bsz, seq, SSM_GROUPS, SSM_STATE),
            dt.reshape(bsz, seq, 2, SSM_HEADS))


def ssm_output(y, xs, z, p):
    bsz, seq = y.shape[:2]
    y = y + p['ssm_d'][:, None] * xs
    y = y.reshape(bsz, seq, SSM_D_INNER) * jax.nn.silu(z)
    y = rms_norm(y.reshape(bsz, seq, SSM_GROUPS, SSM_D_INNER // SSM_GROUPS),
                 p['ssm_norm_w'].reshape(SSM_GROUPS, -1))
    return y.reshape(bsz, seq, SSM_D_INNER)


def map_query_blocks(fn, q):
    bsz, seq = q.shape[:2]
    nb = seq // Q_BLOCK
    qb = jnp.moveaxis(q.reshape((bsz, nb, Q_BLOCK) + q.shape[2:]), 1, 0)
    out = lax.map(fn, qb)
    return jnp.moveaxis(out, 0, 1).reshape((bsz, seq) + out.shape[3:])


def gqa_heads(q, k, v, p, rope):
    bsz, seq, _ = q.shape
    q = rms_norm(q.reshape(bsz, seq, GQA_HEADS, GQA_HEAD_DIM), p['gqa_q_norm'])
    k = rms_norm(k.reshape(bsz, seq, GQA_KV_HEADS, GQA_HEAD_DIM), p['gqa_k_norm'])
    v = v.reshape(bsz, seq, GQA_KV_HEADS, GQA_HEAD_DIM)
    if rope is not None:
        q = apply_rope(q, *rope)
        k = apply_rope(k, *rope)
    return q.reshape(bsz, seq, GQA_KV_HEADS, GQA_GROUP, GQA_HEAD_DIM), k, v


def gqa_attend(q, k, v):
    scale = GQA_HEAD_DIM ** -0.5

    def block(qb):
        s = jnp.einsum('bqgrd,bkgd->bgrqk', qb, k, preferred_element_type=jnp.float32) * scale
        pr = jax.nn.softmax(s, axis=-1).astype(v.dtype)
        return jnp.einsum('bgrqk,bkgd->bqgrd', pr, v)

    out = map_query_blocks(block, q)
    return out.reshape(out.shape[0], out.shape[1], GQA_WIDTH)


def diff_heads(q, k, v, rope):
    bsz, seq, _ = q.shape
    q = q.reshape(bsz, seq, DIFF_HEADS, 2, DIFF_HEAD_DIM)
    k = k.reshape(bsz, seq, DIFF_HEADS, 2, DIFF_HEAD_DIM)
    v = v.reshape(bsz, seq, DIFF_HEADS, 2 * DIFF_HEAD_DIM)
    if rope is not None:
        q = apply_rope(q, *rope)
        k = apply_rope(k, *rope)
    return q, k, v


def diff_attend(q, k, v, lam):
    scale = DIFF_HEAD_DIM ** -0.5

    def block(qb):
        s = jnp.einsum('bqhjd,bkhjd->bhjqk', qb, k, preferred_element_type=jnp.float32) * scale
        pr = jax.nn.softmax(s, axis=-1)
        w = (pr[:, :, 0] - lam * pr[:, :, 1]).astype(v.dtype)
        return jnp.einsum('bhqk,bkhe->bqhe', w, v)

    return map_query_blocks(block, q)


def diff_output(o, p, lam_init):
    bsz, seq = o.shape[:2]
    o = rms_norm(o, p['diff_norm_w']) * (1.0 - lam_init)
    return o.reshape(bsz, seq, DIFF_V_WIDTH)


def merge_branches(y_ssm, y_gqa, y_diff, gate_logits, p):
    bsz, seq = y_ssm.shape[:2]
    g = jax.nn.sigmoid((gate_logits + p['b_gate']).astype(jnp.float32))
    g = g.reshape(bsz, seq, N_BRANCHES, D_MODEL).astype(y_ssm.dtype)
    m = (g[:, :, 0] * (y_ssm @ p['w_ssm_out'])
         + g[:, :, 1] * (y_gqa @ p['w_gqa_out'])
         + g[:, :, 2] * (y_diff @ p['w_diff_out']))
    return m @ p['w_o']


def hybrid_mixer(h_lat, h_ctx, p, rope_gqa, rope_diff, lam_init, need_ctx):
    bsz = h_lat.shape[0]
    (z_l, xbc_l, dt_l, gq_l, gk_l, gv_l, dq_l, dk_l, dv_l, gate_l) = jnp.split(
        h_lat @ p['w_in'], IN_OFFSETS, axis=-1)
    (z_c, xbc_c, dt_c, gq_c, gk_c, gv_c, dq_c, dk_c, dv_c, gate_c) = jnp.split(
        h_ctx @ p['w_in'], IN_OFFSETS, axis=-1)

    xs_l, b_l, c_l, dtp_l = ssm_inputs(xbc_l, dt_l, p)
    xs_c, b_c, c_c, dtp_c = ssm_inputs(xbc_c, dt_c, p)
    a = -jnp.exp(p['ssm_a_log'].astype(jnp.float32))
    h0 = jnp.zeros((bsz, SSM_HEADS, SSM_HEAD_DIM, SSM_STATE), jnp.float32)
    yf_c, hf_c = ssd_scan(xs_c, dtp_c[:, :, 0], a[0], b_c, c_c, h0)
    yf_l, _ = ssd_scan(xs_l, dtp_l[:, :, 0], a[0], b_l, c_l, hf_c)
    yb_c, hb_c = ssd_scan(_flip(xs_c), _flip(dtp_c[:, :, 1]), a[1], _flip(b_c), _flip(c_c), h0)
    yb_l, _ = ssd_scan(_flip(xs_l), _flip(dtp_l[:, :, 1]), a[1], _flip(b_l), _flip(c_l), hb_c)
    ssm_l = ssm_output(yf_l + _flip(yb_l), xs_l, z_l, p)

    q_l, k_l, v_l = gqa_heads(gq_l, gk_l, gv_l, p, rope_gqa)
    q_c, k_c, v_c = gqa_heads(gq_c, gk_c, gv_c, p, None)
    gqa_l = gqa_attend(q_l, jnp.concatenate([k_c, k_l], axis=1),
                       jnp.concatenate([v_c, v_l], axis=1))

    lq1, lk1, lq2, lk2 = p['diff_lambda'].astype(jnp.float32)
    lam = jnp.exp(jnp.sum(lq1 * lk1)) - jnp.exp(jnp.sum(lq2 * lk2)) + lam_init
    dq_l, dk_l, dv_l = diff_heads(dq_l, dk_l, dv_l, rope_diff)
    dq_c, dk_c, dv_c = diff_heads(dq_c, dk_c, dv_c, None)
    diff_l = diff_output(diff_attend(dq_l, jnp.concatenate([dk_c, dk_l], axis=1),
                                     jnp.concatenate([dv_c, dv_l], axis=1), lam), p, lam_init)

    y_lat = merge_branches(ssm_l, gqa_l, diff_l, gate_l, p)
    if not need_ctx:
        return y_lat, None
    ssm_c = ssm_output(yf_c + _flip(yb_c), xs_c, z_c, p)
    gqa_c = gqa_attend(q_c, k_c, v_c)
    diff_c = diff_output(diff_attend(dq_c, dk_c, dv_c, lam), p, lam_init)
    y_ctx = merge_branches(ssm_c, gqa_c, diff_c, gate_c, p)
    return y_lat, y_ctx


def swiglu(h, w_in, w_out):
    a, b = jnp.split(h @ w_in, 2, axis=-1)
    return (jax.nn.silu(a) * b) @ w_out


def setup_inputs(seed: int = 0) -> dict:
    key = jax.random.key(seed)
    keys = iter(jax.random.split(key, 40))

    def nrm(shape, scale):
        return jax.random.normal(next(keys), shape, jnp.float32) * scale

    d = D_MODEL
    x = nrm((BATCH, SEQ, d), 1.0)
    c = nrm((BATCH, d), 1.0)
    ctx = nrm((BATCH, CTX_LEN, d), 1.0)
    c_ctx = nrm((d,), 1.0)
    ada_w = nrm((DEPTH, d, 6 * d), d ** -0.5)
    ada_b = nrm((DEPTH, 6 * d), 0.02)
    w_in = nrm((DEPTH, d, IN_WIDTH), d ** -0.5)
    b_gate = nrm((DEPTH, N_BRANCHES * d), 0.02)
    ssm_conv_w = nrm((DEPTH, SSM_CONV, SSM_CONV_DIM), SSM_CONV ** -0.5)
    ssm_conv_b = nrm((DEPTH, SSM_CONV_DIM), 0.02)
    dt0 = jnp.exp(jax.random.uniform(next(keys), (DEPTH, 2, SSM_HEADS), jnp.float32,
                                     minval=math.log(1e-3), maxval=math.log(1e-1)))
    ssm_dt_bias = dt0 + jnp.log(-jnp.expm1(-dt0))
    ssm_a_log = jnp.log(jax.random.uniform(next(keys), (DEPTH, 2, SSM_HEADS), jnp.float32,
                                           minval=1.0, maxval=16.0))
    ssm_d = 1.0 + nrm((DEPTH, SSM_HEADS), 0.02)
    ssm_norm_w = 1.0 + nrm((DEPTH, SSM_D_INNER), 0.02)
    w_ssm_out = nrm((DEPTH, SSM_D_INNER, d), SSM_D_INNER ** -0.5)
    gqa_q_norm = 1.0 + nrm((DEPTH, GQA_HEAD_DIM), 0.02)
    gqa_k_norm = 1.0 + nrm((DEPTH, GQA_HEAD_DIM), 0.02)
    w_gqa_out = nrm((DEPTH, GQA_WIDTH, d), GQA_WIDTH ** -0.5)
    diff_lambda = nrm((DEPTH, 4, DIFF_HEAD_DIM), 0.1)
    diff_norm_w = 1.0 + nrm((DEPTH, 2 * DIFF_HEAD_DIM), 0.02)
    w_diff_out = nrm((DEPTH, DIFF_V_WIDTH, d), DIFF_V_WIDTH ** -0.5)
    w_o = nrm((DEPTH, d, d), d ** -0.5 * DEEPNORM_BETA)
    ln1_g = 1.0 + nrm((DEPTH, d), 0.02)
    ln1_b = nrm((DEPTH, d), 0.02)
    ffn_w_in = nrm((DEPTH, d, 2 * FFN_HIDDEN), d ** -0.5)
    ffn_w_out = nrm((DEPTH, FFN_HIDDEN, d), FFN_HIDDEN ** -0.5 * DEEPNORM_BETA)
    ln2_g = 1.0 + nrm((DEPTH, d), 0.02)
    ln2_b = nrm((DEPTH, d), 0.02)
    return {'x': x, 'c': c, 'ctx': ctx, 'c_ctx': c_ctx,
            'ada_w': ada_w, 'ada_b': ada_b, 'w_in': w_in, 'b_gate': b_gate,
            'ssm_conv_w': ssm_conv_w, 'ssm_conv_b': ssm_conv_b, 'ssm_dt_bias': ssm_dt_bias,
            'ssm_a_log': ssm_a_log, 'ssm_d': ssm_d, 'ssm_norm_w': ssm_norm_w,
            'w_ssm_out': w_ssm_out, 'gqa_q_norm': gqa_q_norm, 'gqa_k_norm': gqa_k_norm,
            'w_gqa_out': w_gqa_out, 'diff_lambda': diff_lambda, 'diff_norm_w': diff_norm_w,
            'w_diff_out': w_diff_out, 'w_o': w_o, 'ln1_g': ln1_g, 'ln1_b': ln1_b,
            'ffn_w_in': ffn_w_in, 'ffn_w_out': ffn_w_out, 'ln2_g': ln2_g, 'ln2_b': ln2_b}


def reference(x, c, ctx, c_ctx, ada_w, ada_b, w_in, b_gate, ssm_conv_w, ssm_conv_b,
              ssm_dt_bias, ssm_a_log, ssm_d, ssm_norm_w, w_ssm_out, gqa_q_norm, gqa_k_norm,
              w_gqa_out, diff_lambda, diff_norm_w, w_diff_out, w_o, ln1_g, ln1_b,
              ffn_w_in, ffn_w_out, ln2_g, ln2_b):
    seq = x.shape[1]
    rows = seq // GRID_W
    t = jnp.arange(rows * GRID_W, dtype=jnp.int32)
    pos_row = t // GRID_W
    pos_col = t % GRID_W
    rope_gqa = axial_rope_tables(pos_row, pos_col, GQA_HEAD_DIM)
    rope_diff = axial_rope_tables(pos_row, pos_col, DIFF_HEAD_DIM)
    sc = jax.nn.silu(c)
    sc_ctx = jax.nn.silu(c_ctx)[None]
    for i in range(DEPTH):
        need_ctx = i < DEPTH - 1
        lam_init = 0.8 - 0.6 * math.exp(-0.3 * i)
        p = {'w_in': w_in[i], 'b_gate': b_gate[i], 'ssm_conv_w': ssm_conv_w[i],
             'ssm_conv_b': ssm_conv_b[i], 'ssm_dt_bias': ssm_dt_bias[i], 'ssm_a_log': ssm_a_log[i],
             'ssm_d': ssm_d[i], 'ssm_norm_w': ssm_norm_w[i], 'w_ssm_out': w_ssm_out[i],
             'gqa_q_norm': gqa_q_norm[i], 'gqa_k_norm': gqa_k_norm[i], 'w_gqa_out': w_gqa_out[i],
             'diff_lambda': diff_lambda[i], 'diff_norm_w': diff_norm_w[i],
             'w_diff_out': w_diff_out[i], 'w_o': w_o[i]}
        mod_l = jnp.split(sc @ ada_w[i] + ada_b[i], 6, axis=-1)
        mod_c = jnp.split(sc_ctx @ ada_w[i] + ada_b[i], 6, axis=-1)
        y_l, y_c = hybrid_mixer(modulate(x, mod_l[0], mod_l[1]),
                                modulate(ctx, mod_c[0], mod_c[1]),
                                p, rope_gqa, rope_diff, lam_init, need_ctx)
        x = layer_norm(DEEPNORM_ALPHA * x + mod_l[2][:, None] * y_l, ln1_g[i], ln1_b[i])
        x = layer_norm(DEEPNORM_ALPHA * x + mod_l[5][:, None]
                       * swiglu(modulate(x, mod_l[3], mod_l[4]), ffn_w_in[i], ffn_w_out[i]),
                       ln2_g[i], ln2_b[i])
        if need_ctx:
            ctx = layer_norm(DEEPNORM_ALPHA * ctx + mod_c[2][:, None] * y_c, ln1_g[i], ln1_b[i])
            ctx = layer_norm(DEEPNORM_ALPHA * ctx + mod_c[5][:, None]
                             * swiglu(modulate(ctx, mod_c[3], mod_c[4]), ffn_w_in[i], ffn_w_out[i]),
                             ln2_g[i], ln2_b[i])
    return x
```

```python
import math
from contextlib import ExitStack, contextmanager

import numpy as np
import concourse.bass as bass
import concourse.mybir as mybir
from concourse.bass_utils import run_bass_kernel_spmd

F32 = mybir.dt.float32
BF16 = mybir.dt.bfloat16
AF = mybir.ActivationFunctionType
ALU = mybir.AluOpType
AX = mybir.AxisListType

D = 1024
KC = 8
T = 2304
NT = 18
LAT = 2048
CTX = 256
DEPTH = 4
ALPHA = (2 * DEPTH) ** 0.25
EPS = 1e-6
FFH = 2816
O_Z, O_XBC, O_DT, O_GQ, O_GK, O_GV, O_DQ, O_DK, O_DV, O_GATE = (
    0, 2048, 5120, 5184, 6208, 6464, 6720, 7744, 8768, 9792)
TB = [(0, 256), (256, 768), (768, 1280), (1280, 1792), (1792, 2304)]
NSEM = 40


class Buf:
    def __init__(self, t, sem=None):
        self.t = t
        self.w = {}
        self.r = {}
        self.sem = sem

    def __getitem__(self, idx):
        return self.t[idx]


class DBuf:
    def __init__(self, ap):
        self.ap = ap
        self.w = {}
        self.r = {}


class Kx:
    def __init__(self, nc, es):
        self.nc = nc
        self.eng = {'pe': nc.tensor, 'act': nc.scalar, 'dve': nc.vector,
                    'pool': nc.gpsimd, 'sp': nc.sync}
        self.sem = {k: es.enter_context(nc.semaphore('sem_' + k)) for k in self.eng}
        self.cnt = {k: 0 for k in self.eng}
        self.seen = {k: {} for k in self.eng}
        self.sempool = [[es.enter_context(nc.semaphore('dsem%d' % i)), 'd%d' % i, 0]
                        for i in range(NSEM)]
        self.pending = {}
        self.dbufs = []
        self.stacks = [es]
        self.scope_sems = [[]]
        self.uid = 0

    def tile(self, name, shape, dtype, dma=False):
        self.uid += 1
        t = self.stacks[-1].enter_context(
            self.nc.sbuf_tensor('%s_%d' % (name, self.uid), list(shape), dtype))
        sem = None
        if dma:
            sem = self.sempool.pop()
            self.scope_sems[-1].append(sem)
        return Buf(t, sem)

    def psum(self, name, shape, dtype):
        self.uid += 1
        t = self.stacks[-1].enter_context(
            self.nc.psum_tensor('%s_%d' % (name, self.uid), list(shape), dtype))
        return Buf(t)

    def dbuf(self, ap):
        d = DBuf(ap)
        self.dbufs.append(d)
        return d

    @contextmanager
    def scope(self):
        es = ExitStack()
        self.stacks.append(es)
        self.scope_sems.append([])
        try:
            yield
        finally:
            self.barrier()
            for s in self.scope_sems.pop():
                self.sempool.append(s)
            self.stacks.pop()
            es.close()

    def _wait(self, e, deps):
        for (sem, key, val) in deps:
            if key == e and e == 'pe':
                continue
            if self.seen[e].get(key, 0) >= val:
                continue
            self.eng[e].wait_ge(sem, val)
            self.seen[e][key] = val

    def op(self, e, fn, reads=(), writes=()):
        deps = []
        for b in reads:
            deps += list(b.w.values())
        for b in writes:
            deps += [t for t in b.w.values() if t[1] != e]
            deps += [t for t in b.r.values() if t[1] != e]
        self._wait(e, deps)
        ins = fn(self.eng[e])
        self.cnt[e] += 1
        ins.then_inc(self.sem[e], 1)
        tok = (self.sem[e], e, self.cnt[e])
        for b in reads:
            b.r[e] = tok
        for b in writes:
            b.w = {e: tok}
            b.r = {}
        return ins

    def load(self, q, sb, out_ap, dr, in_ap, part=False):
        deps = list(dr.w.values()) + list(sb.r.values())
        deps += [t for t in sb.w.values() if not (part and t[1] == sb.sem[1])]
        self._wait(q, deps)
        ins = self.eng[q].dma_start(out=out_ap, in_=in_ap)
        sb.sem[2] += 16
        ins.then_inc(sb.sem[0], 16)
        tok = (sb.sem[0], sb.sem[1], sb.sem[2])
        self.pending[tok[1]] = tok
        dr.r[tok[1]] = tok
        if part:
            sb.w[tok[1]] = tok
        else:
            sb.w = {tok[1]: tok}
        sb.r = {}

    def store(self, q, dr, out_ap, sb, in_ap):
        deps = list(sb.w.values()) + list(dr.r.values())
        self._wait(q, deps)
        ins = self.eng[q].dma_start(out=out_ap, in_=in_ap)
        sb.sem[2] += 16
        ins.then_inc(sb.sem[0], 16)
        tok = (sb.sem[0], sb.sem[1], sb.sem[2])
        self.pending[tok[1]] = tok
        sb.r[tok[1]] = tok
        dr.w[tok[1]] = tok

    def barrier(self):
        sp = 'sp'
        deps = [(self.sem[e], e, self.cnt[e]) for e in ('pe', 'act', 'dve', 'pool')
                if self.cnt[e] > 0]
        deps += list(self.pending.values())
        self._wait(sp, deps)
        self.eng[sp].sem_inc(self.sem[sp], 1)
        self.cnt[sp] += 1
        tok = (self.sem[sp], sp, self.cnt[sp])
        for e in ('pe', 'act', 'dve', 'pool'):
            self._wait(e, [tok])
        self.pending = {}
        for d in self.dbufs:
            d.w = {}
            d.r = {}

    def mm(self, ps, out_ap, lhsT_b, lhsT_ap, rhs_b, rhs_ap, start, stop):
        return self.op('pe', lambda e: e.matmul(out_ap, lhsT_ap, rhs_ap, start=start, stop=stop),
                       reads=[lhsT_b, rhs_b], writes=[ps])

    def tr(self, ps, out_ap, in_b, in_ap, ident_b, ident_ap):
        return self.op('pe', lambda e: e.transpose(out_ap, in_ap, ident_ap),
                       reads=[in_b, ident_b], writes=[ps])

    def act(self, out_b, out_ap, in_b, in_ap, func, bias=None, scale=None, extra_reads=(),
            accum=None, accum_b=None):
        kw = {}
        if bias is not None:
            kw['bias'] = bias
        if scale is not None:
            kw['scale'] = scale
        if accum is not None:
            kw['accum_out'] = accum
        wr = [out_b] + ([accum_b] if accum_b is not None else [])
        return self.op('act', lambda e: e.activation(out=out_ap, in_=in_ap, func=func, **kw),
                       reads=[in_b] + list(extra_reads), writes=wr)

    def tt(self, e, out_b, out_ap, a_b, a_ap, b_b, b_ap, op):
        return self.op(e, lambda en: en.tensor_tensor(out=out_ap, in0=a_ap, in1=b_ap, op=op),
                       reads=[a_b, b_b], writes=[out_b])

    def ts(self, e, out_b, out_ap, a_b, a_ap, s1, s2, op0, op1=None, extra_reads=()):
        if op1 is None:
            f = lambda en: en.tensor_scalar(out=out_ap, in0=a_ap, scalar1=s1, scalar2=None, op0=op0)
        else:
            f = lambda en: en.tensor_scalar(out=out_ap, in0=a_ap, scalar1=s1, scalar2=s2,
                                            op0=op0, op1=op1)
        return self.op(e, f, reads=[a_b] + list(extra_reads), writes=[out_b])

    def stt(self, e, out_b, out_ap, a_b, a_ap, scalar, b_b, b_ap, op0, op1, extra_reads=()):
        return self.op(e, lambda en: en.scalar_tensor_tensor(out=out_ap, in0=a_ap, scalar=scalar,
                                                             in1=b_ap, op0=op0, op1=op1),
                       reads=[a_b, b_b] + list(extra_reads), writes=[out_b])

    def copy(self, e, out_b, out_ap, in_b, in_ap):
        if e == 'act':
            return self.op('act', lambda en: en.copy(out=out_ap, in_=in_ap),
                           reads=[in_b], writes=[out_b])
        return self.op(e, lambda en: en.tensor_copy(out=out_ap, in_=in_ap),
                       reads=[in_b], writes=[out_b])

    def rsqrt(self, b, ap):
        self.act(b, ap, b, ap, AF.Ln)
        self.act(b, ap, b, ap, AF.Exp, scale=-0.5)

    def memset(self, e, b, ap, val):
        return self.op(e, lambda en: en.memset(ap, val), reads=[], writes=[b])


def bc(ap, shape):
    return ap.broadcast_to(list(shape))


def build_program(NB, depth_run, debug=False):
    nc = bass.Bass("TRN2", target_bir_lowering=False)

    def din(name, shape, dt=F32):
        return nc.dram_tensor(name, list(shape), dt, kind="ExternalInput").ap()

    x_in = din("x", [NB, LAT, D])
    ctx_in = din("ctx", [NB, CTX, D])
    cT_in = din("cT", [128, KC, 5])
    ada_w = din("ada_w", [DEPTH, D, 6 * D])
    ada_b_tm = din("ada_b", [DEPTH, 6 * D])
    ada_b_fm = din("ada_b_fm", [DEPTH, 128, 48])
    w_in = din("w_in", [DEPTH, D, 12864])
    b_gate = din("b_gate", [DEPTH, 3072])
    conv_w = din("conv_w", [DEPTH, 128, 24, 5])
    conv_b = din("conv_b", [DEPTH, 128, 24])
    dt_bias = din("ssm_dt_bias", [DEPTH, 64])
    a_log = din("ssm_a_log", [DEPTH, 64])
    ssm_d = din("ssm_d", [DEPTH, 32])
    ssm_nw = din("ssm_norm_w", [DEPTH, 2048])
    w_sso = din("w_ssm_out", [DEPTH, 2048, D])
    gq_norm = din("gqa_q_norm", [DEPTH, 128])
    gk_norm = din("gqa_k_norm", [DEPTH, 128])
    w_gqo = din("w_gqa_out", [DEPTH, D, D])
    dlam = din("diff_lambda", [DEPTH, 256])
    dnw_fm = din("diff_norm_w", [DEPTH, 128, 1])
    w_dfo = din("w_diff_out", [DEPTH, D, D])
    w_o = din("w_o", [DEPTH, D, D])
    ln1_g = din("ln1_g", [DEPTH, D])
    ln1_b = din("ln1_b", [DEPTH, D])
    ffn_wi = din("ffn_w_in", [DEPTH, D, 2 * FFH])
    ffn_wo = din("ffn_w_out", [DEPTH, FFH, D])
    ln2_g = din("ln2_g", [DEPTH, D])
    ln2_b = din("ln2_b", [DEPTH, D])
    consts_in = din("consts", [128, 6, 128])
    ropeG_in = din("ropeG", [128, 16, 2, 128])
    ropeD_in = din("ropeD", [128, 16, 2, 64])
    out = nc.dram_tensor("out", [NB, LAT, D], F32, kind="ExternalOutput").ap()

    def scratch(name, shape, dt):
        return nc.dram_tensor(name, list(shape), dt, kind="Internal").ap()

    xA_d = scratch("xA", [T, D], F32)
    xB_d = scratch("xB", [T, D], F32)
    ybr_d = scratch("ybr", [32 * 128, T], BF16)
    mod_d = scratch("mod_tm", [DEPTH, 5, 6 * D], F32)

    with ExitStack() as es:
        K = Kx(nc, es)
        X_IN = K.dbuf(x_in)
        CTX_IN = K.dbuf(ctx_in)
        OUT = K.dbuf(out)
        XA = K.dbuf(xA_d)
        XB = K.dbuf(xB_d)
        YBR = K.dbuf(ybr_d)
        MODD = K.dbuf(mod_d)
        WD = K.dbuf(None)

        cst = K.tile('cst', [128, 6, 128], F32, dma=True)
        K.load('sp', cst, cst[:], WD, consts_in)
        IDF, MLE, MGT, MGE, MLT, ONESF = (cst[:, i, :] for i in range(6))
        cstb = K.tile('cstb', [128, 2, 128], BF16)
        K.copy('dve', cstb, cstb[:, 0, :], cst, cst[:, 0, :])
        K.copy('dve', cstb, cstb[:, 1, :], cst, cst[:, 5, :])
        IDB = cstb[:, 0, :]
        ONESB = cstb[:, 1, :]
        modT = K.tile('modT', [128, DEPTH, 48, 5], F32)

        with K.scope():
            cT = K.tile('cT', [128, KC, 5], F32, dma=True)
            K.load('sp', cT, cT[:], WD, cT_in)
            scT = K.tile('scT', [128, KC, 5], BF16)
            K.act(scT, scT[:], cT, cT[:], AF.Silu)
            abf = K.tile('abf', [128, DEPTH, 48], F32, dma=True)
            K.load('sp', abf, abf[:], WD, ada_b_fm.rearrange("l p c -> p l c"))
            wts = [K.tile('adaw%d' % i, [128, KC, 512], BF16, dma=True) for i in range(2)]
            abt = [K.tile('abt%d' % i, [5, 512], F32, dma=True) for i in range(2)]
            mo = [K.tile('mo%d' % i, [5, 512], F32, dma=True) for i in range(2)]
            ps_t = [K.psum('ps0t%d' % i, [128, 512], F32) for i in range(2)]
            ps_f = [K.psum('ps0f%d' % i, [128, 4, 5], F32) for i in range(2)]
            n = 0
            for L in range(depth_run):
                for j in range(12):
                    W = wts[n % 2]
                    K.load('pool', W, W[:], WD,
                           ada_w[L].rearrange("(kc p) f -> p kc f", p=128)[:, :, j * 512:(j + 1) * 512])
                    ab = abt[n % 2]
                    K.load('sp', ab, ab[:], WD,
                           ada_b_tm[L:L + 1, j * 512:(j + 1) * 512].broadcast_to([5, 512]))
                    pt = ps_t[n % 2]
                    for kc in range(KC):
                        K.mm(pt, pt[0:5, :], scT, scT[:, kc, :], W, W[:, kc, :], kc == 0, kc == KC - 1)
                    m = mo[n % 2]
                    K.tt('dve', m, m[:], pt, pt[0:5, :], ab, ab[:], ALU.add)
                    K.store('sp', MODD, mod_d[L, :, j * 512:(j + 1) * 512], m, m[:])
                    pf = ps_f[n % 2]
                    for f in range(4):
                        for kc in range(KC):
                            K.mm(pf, pf[:, f, :], W, W[:, kc, f * 128:(f + 1) * 128], scT, scT[:, kc, :],
                                 kc == 0, kc == KC - 1)
                    K.tt('dve', modT, modT[:, L, j * 4:(j + 1) * 4, :], pf, pf[:],
                         abf, bc(abf[:, L, j * 4:(j + 1) * 4, None], [128, 4, 5]), ALU.add)
                    n += 1

        for b in range(NB):
            for L in range(depth_run):
                last = (L == DEPTH - 1)
                lam_init = 0.8 - 0.6 * math.exp(-0.3 * L)
                if L == 0:
                    def xsrc(t0, nt, b=b):
                        if t0 < 2:
                            return CTX_IN, ctx_in[b, t0 * 128:(t0 + nt) * 128, :]
                        return X_IN, x_in[b, (t0 - 2) * 128:(t0 - 2 + nt) * 128, :]
                else:
                    def xsrc(t0, nt):
                        return XA, xA_d[t0 * 128:(t0 + nt) * 128, :]
                with K.scope():
                    layer(K, nc, locals())
    return nc


def layer(K, nc, G):
    b, L, last, lam_init, xsrc = G['b'], G['L'], G['last'], G['lam_init'], G['xsrc']
    WD, MODD, YBR, XA, XB, OUT = G['WD'], G['MODD'], G['YBR'], G['XA'], G['XB'], G['OUT']
    cst, cstb, modT = G['cst'], G['cstb'], G['modT']
    IDF, MLE, MGT, MGE, MLT, ONESF, IDB, ONESB = (G[k] for k in
                                                  ('IDF', 'MLE', 'MGT', 'MGE', 'MLT', 'ONESF', 'IDB', 'ONESB'))
    w_in = G['w_in']
    mod_d, ybr_d, xA_d, xB_d, out = G['mod_d'], G['ybr_d'], G['xA_d'], G['xB_d'], G['out']

    def wsl(c0, c1):
        return w_in[L].rearrange("(kc p) f -> p kc f", p=128)[:, :, c0:c1]

    sc1 = K.tile('sc1', [128, 2, 8], F32)
    sh1 = K.tile('sh1', [128, 2, 8], F32)
    sc2 = K.tile('sc2', [128, 2, 8], F32)
    sh2 = K.tile('sh2', [128, 2, 8], F32)
    for j, m in enumerate((b, 4)):
        K.copy('dve', sh1, sh1[:, j, :], modT, modT[:, L, 0:8, m])
        K.ts('dve', sc1, sc1[:, j, :], modT, modT[:, L, 8:16, m], 1.0, None, ALU.add)
        K.copy('dve', sh2, sh2[:, j, :], modT, modT[:, L, 24:32, m])
        K.ts('dve', sc2, sc2[:, j, :], modT, modT[:, L, 32:40, m], 1.0, None, ALU.add)
    hT = K.tile('hT', [128, KC, T], BF16)

    phase_A(K, G, hT, sc1, sh1, xsrc)
    phase_ssm(K, G, hT, wsl)
    phase_gqa(K, G, hT, wsl)
    for hh in range(2):
        phase_diff(K, G, hT, wsl, hh)
    for th in range(2):
        phase_merge(K, G, hT, wsl, th)
    phase_A(K, G, hT, sc2, sh2, lambda t0, nt: (XB, xB_d[t0 * 128:(t0 + nt) * 128, :]))
    phase_ffn(K, G, hT)


def load_rows(K, G, kmod, lg, lb):
    L, b = G['L'], G['b']
    rows = K.tile('rows', [128, 4, D], F32, dma=True)
    for j, m in enumerate((b, 4)):
        K.load('sp', rows, rows[:, j, :], G['MODD'],
               G['mod_d'][L, m:m + 1, kmod * D:(kmod + 1) * D].broadcast_to([128, D]), part=(j > 0))
    for j, src in enumerate((lg, lb)):
        K.load('sp', rows, rows[:, 2 + j, :], G['WD'], src[L:L + 1, :].broadcast_to([128, D]), part=True)
    return rows


def phase_A(K, G, hT, sc1, sh1, xsrc):
    IDF = G['IDF']
    cst = G['cst']
    with K.scope():
        ps = [K.psum('psA%d' % i, [128, 512], F32) for i in range(4)]
        xg = [K.tile('xg%d' % i, [128, 4, D], F32, dma=True) for i in range(2)]
        groups = [(0, 2, 1), (2, 4, 0), (6, 4, 0), (10, 4, 0), (14, 4, 0)]
        n = 0
        for gi, (t0, nt, isc) in enumerate(groups):
            xb = xg[gi % 2]
            db, ap = xsrc(t0, nt)
            K.load('sp', xb, xb[:, 0:nt, :], db, ap.rearrange("(n p) d -> p n d", p=128))
            for kc in range(KC):
                p = ps[n % 4]
                for j in range(nt):
                    K.tr(p, p[:, j * 128:(j + 1) * 128], xb, xb[:, j, kc * 128:(kc + 1) * 128], cst, IDF)
                o = hT[:, kc, t0 * 128:(t0 + nt) * 128]
                if n % 2 == 0:
                    K.ts('dve', hT, o, p, p[:, 0:nt * 128], sc1[:, isc, kc:kc + 1], sh1[:, isc, kc:kc + 1],
                         ALU.mult, ALU.add, extra_reads=[sc1, sh1])
                else:
                    K.act(hT, o, p, p[:, 0:nt * 128], AF.Identity, bias=sh1[:, isc, kc:kc + 1],
                          scale=sc1[:, isc, kc:kc + 1], extra_reads=[sc1, sh1])
                n += 1


def proj_tm(K, ps, hT, tt, W, c0, ncols):
    for kc in range(KC):
        K.mm(ps, ps[:, 0:ncols], hT, hT[:, kc, tt * 128:(tt + 1) * 128], W, W[:, kc, c0:c0 + ncols],
             kc == 0, kc == KC - 1)


def phase_ssm(K, G, hT, wsl):
    L = G['L']
    WD, YBR, ybr_d = G['WD'], G['YBR'], G['ybr_d']
    cst, cstb = G['cst'], G['cstb']
    IDB, ONESF = G['IDB'], G['ONESF']
    MLE, MGT, MGE, MLT = G['MLE'], G['MGT'], G['MGE'], G['MLT']
    with K.scope():
        dt = K.tile('dt', [128, NT, 64], F32)
        adt = K.tile('adt', [128, NT, 64], F32)
        prm = K.tile('prm', [128, 3, 64], F32, dma=True)
        K.load('sp', prm, prm[:, 0, :], WD, G['dt_bias'][L:L + 1, :].broadcast_to([128, 64]))
        K.load('sp', prm, prm[:, 1, :], WD, G['a_log'][L:L + 1, :].broadcast_to([128, 64]), part=True)
        K.load('sp', prm, prm[:, 2, 0:32], WD, G['ssm_d'][L:L + 1, :].broadcast_to([128, 32]), part=True)
        nw = K.tile('nw', [128, 2048], F32, dma=True)
        K.load('sp', nw, nw[:], WD, G['ssm_nw'][L:L + 1, :].broadcast_to([128, 2048]))
        cw = K.tile('cw', [128, 24, 5], F32, dma=True)
        K.load('sp', cw, cw[:], WD, G['conv_w'][L])
        cb = K.tile('cb', [128, 24], F32, dma=True)
        K.load('sp', cb, cb[:], WD, G['conv_b'][L])
        aneg = K.tile('aneg', [128, 64], F32)
        K.act(aneg, aneg[:], prm, prm[:, 1, :], AF.Exp)
        K.ts('dve', aneg, aneg[:], aneg, aneg[:], -1.0, None, ALU.mult)
        with K.scope():
            Wdt = K.tile('Wdt', [128, KC, 64], BF16, dma=True)
            K.load('pool', Wdt, Wdt[:], WD, wsl(O_DT, O_DT + 64))
            psd = [K.psum('psd%d' % i, [128, 64], F32) for i in range(2)]
            tmp = [K.tile('dtt%d' % i, [128, 64], F32) for i in range(2)]
            for tt in range(NT):
                p = psd[tt % 2]
                proj_tm(K, p, hT, tt, Wdt, 0, 64)
                t1 = tmp[tt % 2]
                K.tt('dve', t1, t1[:], p, p[:], prm, prm[:, 0, :], ALU.add)
                K.act(t1, t1[:], t1, t1[:], AF.Exp)
                K.act(dt, dt[:, tt, :], t1, t1[:], AF.Ln, bias=1.0)
            K.tt('dve', adt, adt[:], dt, dt[:], aneg, bc(aneg[:, None, :], [128, NT, 64]), ALU.mult)

        for g in range(4):
            with K.scope():
                ssm_group(K, G, hT, wsl, g, dt, adt, prm, nw, cw, cb)


def ssm_group(K, G, hT, wsl, g, dt, adt, prm, nw, cw, cb):
    WD, YBR, ybr_d = G['WD'], G['YBR'], G['ybr_d']
    cst, cstb = G['cst'], G['cstb']
    IDB, ONESF = G['IDB'], G['ONESF']
    MLE, MGT, MGE, MLT = G['MLE'], G['MGT'], G['MGE'], G['MLT']
    PADW = 2316
    xs_tm = K.tile('xs_tm', [128, NT, 512], BF16)
    B_tm = K.tile('B_tm', [128, NT, 128], BF16)
    BT = K.tile('BT', [128, T], BF16)
    CT = K.tile('CT', [128, T], BF16)
    with K.scope():
        W6 = K.tile('W6', [128, KC, 768], BF16, dma=True)
        K.load('pool', W6, W6[:, :, 0:512], WD, wsl(O_XBC + g * 512, O_XBC + (g + 1) * 512))
        K.load('pool', W6, W6[:, :, 512:640], WD,
               wsl(O_XBC + 2048 + g * 128, O_XBC + 2048 + (g + 1) * 128), part=True)
        K.load('pool', W6, W6[:, :, 640:768], WD,
               wsl(O_XBC + 2560 + g * 128, O_XBC + 2560 + (g + 1) * 128), part=True)
        cchunk = [g * 4 + 0, g * 4 + 1, g * 4 + 2, g * 4 + 3, 16 + g, 20 + g]
        xcT = K.tile('xcT', [128, 4, T], BF16)
        xpad = [K.tile('xpad%d' % i, [128, PADW], F32) for i in range(2)]
        cv = [K.tile('cv0', [128, PADW], F32)]
        for i in range(2):
            K.memset('pool', xpad[i], xpad[i][:], 0.0)
        psp = [K.psum('psp%d' % i, [128, 512], F32) for i in range(3)]
        pst = [K.psum('pst%d' % i, [128, 512], BF16) for i in range(2)]
        n = 0
        for fc in range(6):
            xp = xpad[fc % 2]
            c = cv[0]
            cc = cchunk[fc]
            for (a0, a1) in TB:
                p = psp[n % 3]
                n += 1
                for kc in range(KC):
                    K.mm(p, p[:, 0:a1 - a0], W6, W6[:, kc, fc * 128:(fc + 1) * 128], hT, hT[:, kc, a0:a1],
                         kc == 0, kc == KC - 1)
                off = 2 if a0 < 256 else 6
                K.copy('act', xp, xp[:, a0 + off:a1 + off], p, p[:, 0:a1 - a0])
            W = 2308
            K.ts('dve', c, c[:, 2:2 + W], xp, xp[:, 2:2 + W], cw[:, cc, 2:3], cb[:, cc:cc + 1],
                 ALU.mult, ALU.add, extra_reads=[cw, cb])
            for j in (0, 1, 3, 4):
                K.stt('dve', c, c[:, 2:2 + W], xp, xp[:, j:j + W], cw[:, cc, j:j + 1], c, c[:, 2:2 + W],
                      ALU.mult, ALU.add, extra_reads=[cw])
            if fc < 4:
                ob, o0, o1 = xcT, xcT[:, fc, 0:256], xcT[:, fc, 256:T]
            elif fc == 4:
                ob, o0, o1 = BT, BT[:, 0:256], BT[:, 256:T]
            else:
                ob, o0, o1 = CT, CT[:, 0:256], CT[:, 256:T]
            K.act(ob, o0, c, c[:, 2:258], AF.Silu)
            K.act(ob, o1, c, c[:, 262:2310], AF.Silu)
        for tt in range(NT):
            p = pst[tt % 2]
            for fc in range(4):
                K.tr(p, p[:, fc * 128:(fc + 1) * 128], xcT, xcT[:, fc, tt * 128:(tt + 1) * 128], cstb, IDB)
            K.copy('dve' if tt % 2 else 'act', xs_tm, xs_tm[:, tt, :], p, p[:])
        for t4 in range(0, NT, 4):
            nt = min(4, NT - t4)
            p = pst[(t4 // 4) % 2]
            for j in range(nt):
                K.tr(p, p[:, j * 128:(j + 1) * 128], BT, BT[:, (t4 + j) * 128:(t4 + j + 1) * 128], cstb, IDB)
            K.copy('dve', B_tm, B_tm[:, t4:t4 + nt, :], p,
                   p[:, 0:nt * 128].rearrange("p (n c) -> p n c", c=128))

    yacc = K.tile('yacc', [128, NT, 512], F32)
    with K.scope():
        S = K.tile('S', [128, 512], F32)
        Sb = K.tile('Sb', [128, 512], BF16)
        Xb = [K.tile('X%d' % i, [128, 8, 128], F32) for i in range(2)]
        Eb = [K.tile('E%d' % i, [128, 8, 128], F32) for i in range(2)]
        MTb = [K.tile('MT%d' % i, [128, 8, 128], BF16) for i in range(2)]
        Gmb = [K.tile('Gm%d' % i, [128, 128], F32) for i in range(2)]
        xdtb = [K.tile('xdt%d' % i, [128, 8, 64], BF16) for i in range(2)]
        xwb = [K.tile('xw%d' % i, [128, 8, 64], BF16) for i in range(2)]
        smb = [K.tile('sm%d' % i, [128, 16], F32) for i in range(2)]
        tmpb = [K.tile('yt%d' % i, [128, 512], F32) for i in range(2)]
        pG = K.psum('pG', [128, 128], F32)
        pD = [K.psum('pD%d' % i, [128, 512], F32) for i in range(2)]
        pY = K.psum('pY', [128, 512], F32)
        pYo = K.psum('pYo', [128, 512], F32)
        pS = K.psum('pS', [128, 512], F32)
        pc = K.psum('pc', [128, 16], F32)
        n = 0
        for d in range(2):
            Ma, Mb, Mg = (MLE, MGT, MLE) if d == 0 else (MGE, MLT, MGE)
            endcol = 127 if d == 0 else 0
            order = list(range(NT)) if d == 0 else [1, 0] + list(range(NT - 1, 1, -1))
            hs = slice(d * 32 + g * 8, d * 32 + g * 8 + 8)
            K.memset('dve', S, S[:], 0.0)
            K.memset('dve', Sb, Sb[:], 0.0)
            for c in order:
                i = n % 2
                n += 1
                tok = slice(c * 128, (c + 1) * 128)
                X, E, MT, Gm, xdt, xw, sm, ytmp = Xb[i], Eb[i], MTb[i], Gmb[i], xdtb[i], xwb[i], smb[i], tmpb[i]
                K.mm(pG, pG[:], BT, BT[:, tok], CT, CT[:, tok], True, True)
                K.tt('dve', Gm, Gm[:], pG, pG[:], cst, Mg, ALU.mult)
                K.tt('pool', X, X[:], adt, bc(adt[:, c, hs, None], [128, 8, 128]),
                     cst, bc(Ma[:, None, :], [128, 8, 128]), ALU.mult)
                for hh in range(2):
                    K.mm(pD[hh], pD[hh][:], cst, Mb, X,
                         X[:, hh * 4:(hh + 1) * 4, :].rearrange("p h l -> p (h l)"), True, True)
                    K.act(E, E[:, hh * 4:(hh + 1) * 4, :].rearrange("p h l -> p (h l)"), pD[hh], pD[hh][:], AF.Exp)
                K.tt('dve', MT, MT[:], E, E[:], Gm, bc(Gm[:, None, :], [128, 8, 128]), ALU.mult)
                K.mm(pc, pc[:, 0:8], cst, Ma, adt, adt[:, c, hs], True, True)
                K.mm(pc, pc[:, 8:16], cst, ONESF, adt, adt[:, c, hs], True, True)
                K.act(sm, sm[:], pc, pc[:], AF.Exp)
                K.tt('pool', xdt, xdt[:], xs_tm, xs_tm[:, c, :].rearrange("p (h q) -> p h q", q=64),
                     dt, bc(dt[:, c, hs, None], [128, 8, 64]), ALU.mult)
                K.tt('pool', xw, xw[:], xdt, xdt[:], E, bc(E[:, :, endcol:endcol + 1], [128, 8, 64]), ALU.mult)
                for h in range(8):
                    K.mm(pY, pY[:, h * 64:(h + 1) * 64], MT, MT[:, h, :], xdt, xdt[:, h, :], True, True)
                K.mm(pYo, pYo[:], CT, CT[:, tok], Sb, Sb[:], True, True)
                K.tt('dve', ytmp, ytmp[:].rearrange("p (h q) -> p h q", q=64),
                     pYo, pYo[:].rearrange("p (h q) -> p h q", q=64),
                     sm, bc(sm[:, 0:8, None], [128, 8, 64]), ALU.mult)
                if d == 0:
                    K.tt('dve', yacc, yacc[:, c, :], ytmp, ytmp[:], pY, pY[:], ALU.add)
                else:
                    K.tt('dve', ytmp, ytmp[:], ytmp, ytmp[:], pY, pY[:], ALU.add)
                    K.tt('pool', yacc, yacc[:, c, :], yacc, yacc[:, c, :], ytmp, ytmp[:], ALU.add)
                K.mm(pS, pS[:], B_tm, B_tm[:, c, :], xw, xw[:].rearrange("p h q -> p (h q)"), True, True)
                K.tt('dve', S, S[:].rearrange("p (h q) -> p h q", q=64),
                     S, S[:].rearrange("p (h q) -> p h q", q=64),
                     sm, bc(sm[:, 8:16, None], [128, 8, 64]), ALU.mult)
                K.tt('dve', S, S[:], S, S[:], pS, pS[:], ALU.add)
                K.copy('act', Sb, Sb[:], S, S[:])

    with K.scope():
        Wz = K.tile('Wz', [128, KC, 512], BF16, dma=True)
        K.load('pool', Wz, Wz[:], WD, wsl(O_Z + g * 512, O_Z + (g + 1) * 512))
        ysT = K.tile('ysT', [128, 4, T], BF16, dma=True)
        pz = [K.psum('pz%d' % i, [128, 512], F32) for i in range(2)]
        pt = [K.psum('pt%d' % i, [128, 512], BF16) for i in range(2)]
        zs = [K.tile('zs%d' % i, [128, 512], F32) for i in range(2)]
        yb = [K.tile('yb%d' % i, [128, 512], F32) for i in range(2)]
        ynb = [K.tile('yn%d' % i, [128, 512], BF16) for i in range(2)]
        jk = [K.tile('jk%d' % i, [128, 512], F32) for i in range(2)]
        ssb = [K.tile('ss%d' % i, [128, 2], F32) for i in range(2)]
        for tt in range(NT):
            i = tt % 2
            p, z, y, yn, ss = pz[i], zs[i], yb[i], ynb[i], ssb[i]
            proj_tm(K, p, hT, tt, Wz, 0, 512)
            K.act(z, z[:], p, p[:], AF.Silu)
            K.tt('pool', y, y[:].rearrange("p (h q) -> p h q", q=64),
                 xs_tm, xs_tm[:, tt, :].rearrange("p (h q) -> p h q", q=64),
                 prm, bc(prm[:, 2, g * 8:(g + 1) * 8, None], [128, 8, 64]), ALU.mult)
            K.tt('dve', y, y[:], y, y[:], yacc, yacc[:, tt, :], ALU.add)
            K.tt('dve', y, y[:], y, y[:], z, z[:], ALU.mult)
            K.act(jk[i], jk[i][:], y, y[:], AF.Square, accum=ss[:, 0:1], accum_b=ss)
            K.ts('dve', ss, ss[:, 1:2], ss, ss[:, 0:1], 1.0 / 512, EPS, ALU.mult, ALU.add)
            K.rsqrt(ss, ss[:, 1:2])
            K.stt('dve', yn, yn[:], y, y[:], ss[:, 1:2], nw, nw[:, g * 512:(g + 1) * 512],
                  ALU.mult, ALU.mult, extra_reads=[ss])
            q = pt[i]
            for fc in range(4):
                K.tr(q, q[:, fc * 128:(fc + 1) * 128], yn, yn[:, fc * 128:(fc + 1) * 128], cstb, IDB)
            K.copy('act', ysT, ysT[:, :, tt * 128:(tt + 1) * 128], q,
                   q[:].rearrange("p (c t) -> p c t", t=128))
        K.store('sp', YBR, ybr_d[g * 512:(g + 1) * 512, :].rearrange("(c p) t -> p c t", p=128),
                ysT, ysT[:])


def rope(K, e, dst, dst_ap, src, src_ap, tab, cos_ap, sin_ap, tmp, tmp_ap, nh, hd):
    q = hd // 4
    K.tt(e, dst, dst_ap, src, src_ap, tab, bc(cos_ap[:, None, :], [128, nh, hd]), ALU.mult)
    for a in range(2):
        for s in range(2):
            o0 = a * 2 * q + s * q
            i0 = a * 2 * q + (1 - s) * q
            K.tt(e, tmp, tmp_ap[:, :, o0:o0 + q], src, src_ap[:, :, i0:i0 + q],
                 tab, bc(sin_ap[:, None, o0:o0 + q], [128, nh, q]), ALU.mult)
    K.tt(e, dst, dst_ap, dst, dst_ap, tmp, tmp_ap, ALU.add)


def attention(K, G, qT, kT, v_tm, nmaps, map_q, map_k, map_v, scale, finish):
    cstb, ONESB = G['cstb'], G['ONESB']
    pS = [K.psum('aS%d' % i, [128, 512], F32) for i in range(3)]
    pO = [K.psum('aO%d' % i, [128, 512], F32) for i in range(2)]
    pZ = [K.psum('aZ%d' % i, [128, 512], F32) for i in range(2)]
    PT = [K.tile('PT%d' % i, [128, 512], BF16) for i in range(3)]
    qblocks = [(0, 256, 0, 2)] + [(256 + 512 * i, 256 + 512 * (i + 1), 0, NT) for i in range(4)]
    n = 0
    u = 0
    for m in range(nmaps):
        for qb, (q0, q1, k0, k1) in enumerate(qblocks):
            nq = q1 - q0
            o, z = pO[u % 2], pZ[u % 2]
            u += 1
            for kt in range(k0, k1):
                s = pS[n % 3]
                pt = PT[n % 3]
                n += 1
                K.mm(s, s[:, 0:nq], kT, map_k(m)(kt), qT, map_q(m)(q0, q1), True, True)
                K.act(pt, pt[:, 0:nq], s, s[:, 0:nq], AF.Exp, scale=scale)
                K.mm(o, o[:, 0:nq], v_tm, map_v(m)(kt), pt, pt[:, 0:nq], kt == k0, kt == k1 - 1)
                K.mm(z, z[:, 0:nq], cstb, ONESB, pt, pt[:, 0:nq], kt == k0, kt == k1 - 1)
            finish(m, qb, (q0, q1), o, z)


def phase_gqa(K, G, hT, wsl):
    L = G['L']
    WD, YBR, ybr_d = G['WD'], G['YBR'], G['ybr_d']
    cst, cstb, IDB = G['cst'], G['cstb'], G['IDB']
    with K.scope():
        ropeG = K.tile('ropeG', [128, 16, 2, 128], F32, dma=True)
        K.load('sp', ropeG, ropeG[:], WD, G['ropeG_in'])
        qT = K.tile('qT', [128, 8, T], BF16)
        kT = K.tile('kT', [128, 2, T], BF16)
        v_tm = K.tile('v_tm', [128, NT, 256], BF16)
        gn = K.tile('gn', [128, 2, 128], F32, dma=True)
        K.load('sp', gn, gn[:, 0, :], WD, G['gq_norm'][L:L + 1, :].broadcast_to([128, 128]))
        K.load('sp', gn, gn[:, 1, :], WD, G['gk_norm'][L:L + 1, :].broadcast_to([128, 128]), part=True)
        with K.scope():
            W = K.tile('Wg', [128, KC, 1536], BF16, dma=True)
            K.load('pool', W, W[:, :, 0:512], WD, wsl(O_GQ, O_GQ + 512))
            K.load('pool', W, W[:, :, 512:1024], WD, wsl(O_GQ + 512, O_GQ + 1024), part=True)
            K.load('pool', W, W[:, :, 1024:1536], WD, wsl(O_GK, O_GK + 512), part=True)
            pq = [K.psum('pq%d' % i, [128, 512], F32) for i in range(3)]
            ptr = [K.psum('ptr%d' % i, [128, 512], BF16) for i in range(2)]
            qf = [K.tile('qf%d' % i, [128, 10, 128], F32) for i in range(2)]
            sq = [K.tile('sq%d' % i, [128, 10, 128], F32) for i in range(2)]
            qr = [K.tile('qr%d' % i, [128, 10, 128], F32) for i in range(2)]
            qb_ = [K.tile('qb%d' % i, [128, 10, 128], BF16) for i in range(2)]
            ssq = [K.tile('ssq%d' % i, [128, 10], F32) for i in range(2)]
            for tt in range(NT):
                i = tt % 2
                f, s2, r, qb16, ss = qf[i], sq[i], qr[i], qb_[i], ssq[i]
                for j in range(3):
                    proj_tm(K, pq[j], hT, tt, W, j * 512, 512)
                K.copy('act', f, f[:, 0:4, :], pq[0], pq[0][:].rearrange("p (h d) -> p h d", d=128))
                K.copy('act', f, f[:, 4:8, :], pq[1], pq[1][:].rearrange("p (h d) -> p h d", d=128))
                K.copy('act', f, f[:, 8:10, :], pq[2], pq[2][:, 0:256].rearrange("p (h d) -> p h d", d=128))
                K.copy('act', v_tm, v_tm[:, tt, :], pq[2], pq[2][:, 256:512])
                K.tt('pool', s2, s2[:], f, f[:], f, f[:], ALU.mult)
                K.op('dve', lambda e: e.reduce_sum(out=ss[:], in_=s2[:], axis=AX.X), reads=[s2], writes=[ss])
                K.ts('dve', ss, ss[:], ss, ss[:], 1.0 / 128, EPS, ALU.mult, ALU.add)
                K.rsqrt(ss, ss[:])
                K.tt('dve', f, f[:], f, f[:], ss, bc(ss[:, :, None], [128, 10, 128]), ALU.mult)
                K.tt('dve', f, f[:, 0:8, :], f, f[:, 0:8, :], gn, bc(gn[:, 0:1, :], [128, 8, 128]), ALU.mult)
                K.tt('dve', f, f[:, 8:10, :], f, f[:, 8:10, :], gn, bc(gn[:, 1:2, :], [128, 2, 128]), ALU.mult)
                if tt >= 2:
                    rope(K, 'pool', r, r[:], f, f[:], ropeG, ropeG[:, tt - 2, 0, :], ropeG[:, tt - 2, 1, :],
                         s2, s2[:], 10, 128)
                    K.copy('dve', qb16, qb16[:], r, r[:])
                else:
                    K.copy('dve', qb16, qb16[:], f, f[:])
                for half in range(3):
                    hh = [(0, 4), (4, 8), (8, 10)][half]
                    p = ptr[(tt * 3 + half) % 2]
                    for h in range(hh[0], hh[1]):
                        K.tr(p, p[:, (h - hh[0]) * 128:(h - hh[0] + 1) * 128], qb16, qb16[:, h, :], cstb, IDB)
                    nh = hh[1] - hh[0]
                    src = p[:, 0:nh * 128].rearrange("p (h t) -> p h t", t=128)
                    if half < 2:
                        K.copy('act', qT, qT[:, hh[0]:hh[1], tt * 128:(tt + 1) * 128], p, src)
                    else:
                        K.copy('act', kT, kT[:, :, tt * 128:(tt + 1) * 128], p, src)
        with K.scope():
            og = [K.tile('og%d' % i, [128, 512], BF16, dma=True) for i in range(3)]
            rz = [K.tile('rz%d' % i, [128, 512], F32) for i in range(2)]
            cnt = [0]

            def finish(m, qb, qq, o, z):
                q0, q1 = qq
                nq = q1 - q0
                i = cnt[0]
                cnt[0] += 1
                r = rz[i % 2]
                ob = og[i % 3]
                K.op('dve', lambda e: e.reciprocal(out=r[:, 0:nq], in_=z[:, 0:nq]), reads=[z], writes=[r])
                K.tt('dve', ob, ob[:, 0:nq], o, o[:, 0:nq], r, r[:, 0:nq], ALU.mult)
                K.store('sp', YBR, ybr_d[(16 + m) * 128:(17 + m) * 128, q0:q1], ob, ob[:, 0:nq])

            attention(K, G, qT, kT, v_tm, 8,
                      lambda m: (lambda q0, q1: qT[:, m, q0:q1]),
                      lambda m: (lambda kt: kT[:, m // 4, kt * 128:(kt + 1) * 128]),
                      lambda m: (lambda kt: v_tm[:, kt, (m // 4) * 128:(m // 4 + 1) * 128]),
                      128 ** -0.5, finish)


def phase_diff(K, G, hT, wsl, hh):
    L, lam_init = G['L'], G['lam_init']
    WD, YBR, ybr_d = G['WD'], G['YBR'], G['ybr_d']
    cst, cstb, IDB, ONESF = G['cst'], G['cstb'], G['IDB'], G['ONESF']
    with K.scope():
        ropeD = K.tile('ropeD', [128, 16, 2, 64], F32, dma=True)
        K.load('sp', ropeD, ropeD[:], WD, G['ropeD_in'])
        qT = K.tile('dqT', [128, 4, T], BF16)
        kT = K.tile('dkT', [128, 4, T], BF16)
        v_tm = K.tile('dv_tm', [128, NT, 512], BF16)
        lm = K.tile('lm', [128, 4, 64], F32, dma=True)
        K.load('sp', lm, lm[:].rearrange("p a b -> p (a b)"), WD, G['dlam'][L:L + 1, :].broadcast_to([128, 256]))
        lt = K.tile('lt', [128, 2, 64], F32)
        ls = K.tile('ls', [128, 4], F32)
        K.tt('dve', lt, lt[:, 0, :], lm, lm[:, 0, :], lm, lm[:, 1, :], ALU.mult)
        K.tt('dve', lt, lt[:, 1, :], lm, lm[:, 2, :], lm, lm[:, 3, :], ALU.mult)
        K.op('dve', lambda e: e.reduce_sum(out=ls[:, 0:2], in_=lt[:], axis=AX.X), reads=[lt], writes=[ls])
        K.act(ls, ls[:, 0:2], ls, ls[:, 0:2], AF.Exp)
        K.tt('dve', ls, ls[:, 2:3], ls, ls[:, 0:1], ls, ls[:, 1:2], ALU.subtract)
        K.ts('dve', ls, ls[:, 3:4], ls, ls[:, 2:3], lam_init, -1.0, ALU.add, ALU.mult)
        dnw = K.tile('dnw', [128, 1], F32, dma=True)
        K.load('sp', dnw, dnw[:], WD, G['dnw_fm'][L])
        K.ts('dve', dnw, dnw[:], dnw, dnw[:], 1.0 - lam_init, None, ALU.mult)
        with K.scope():
            Wb = [K.tile('Wd%d' % i, [128, KC, 512], BF16, dma=True) for i in range(3)]
            for j, c0 in enumerate((O_DQ, O_DK, O_DV)):
                K.load('pool', Wb[j], Wb[j][:], WD, wsl(c0 + hh * 512, c0 + (hh + 1) * 512))
            pq = [K.psum('dpq%d' % i, [128, 512], F32) for i in range(3)]
            ptr = [K.psum('dptr%d' % i, [128, 512], BF16) for i in range(2)]
            qf = [K.tile('dqf%d' % i, [128, 8, 64], F32) for i in range(2)]
            tm = [K.tile('dtm%d' % i, [128, 8, 64], F32) for i in range(2)]
            qr = [K.tile('dqr%d' % i, [128, 8, 64], F32) for i in range(2)]
            q16 = [K.tile('dq16%d' % i, [128, 512], BF16) for i in range(2)]
            n = 0
            for tt in range(NT):
                for j in range(3):
                    p = pq[n % 3]
                    i = n % 2
                    n += 1
                    proj_tm(K, p, hT, tt, Wb[j], 0, 512)
                    if j == 2:
                        K.copy('act', v_tm, v_tm[:, tt, :], p, p[:])
                        continue
                    f, t_, r, b16 = qf[i], tm[i], qr[i], q16[i]
                    if tt >= 2:
                        K.copy('act', f, f[:].rearrange("p h d -> p (h d)"), p, p[:])
                        rope(K, 'pool' if j % 2 else 'dve', r, r[:], f, f[:], ropeD, ropeD[:, tt - 2, 0, :],
                             ropeD[:, tt - 2, 1, :], t_, t_[:], 8, 64)
                        K.copy('act', b16, b16[:], r, r[:].rearrange("p h d -> p (h d)"))
                    else:
                        K.copy('act', b16, b16[:], p, p[:])
                    q = ptr[i]
                    for c in range(4):
                        K.tr(q, q[:, c * 128:(c + 1) * 128], b16, b16[:, c * 128:(c + 1) * 128], cstb, IDB)
                    dst = qT if j == 0 else kT
                    K.copy('dve', dst, dst[:, :, tt * 128:(tt + 1) * 128], q,
                           q[:].rearrange("p (c t) -> p c t", t=128))
        with K.scope():
            o1 = [K.tile('o1_%d' % i, [128, 512], F32) for i in range(2)]
            o2 = [K.tile('o2_%d' % i, [128, 512], F32) for i in range(2)]
            osq = [K.tile('osq%d' % i, [128, 512], F32) for i in range(2)]
            rs = [K.tile('rs%d' % i, [128, 512], F32) for i in range(2)]
            od = [K.tile('od%d' % i, [128, 512], BF16, dma=True) for i in range(3)]
            rz = [K.tile('drz%d' % i, [128, 512], F32) for i in range(2)]
            pN = K.psum('pN', [128, 512], F32)
            cnt = [0]

            def finish(m, qb, qq, o, z):
                q0, q1 = qq
                nq = q1 - q0
                h, j = m // 2, m % 2
                i = cnt[0] % 2
                r = rz[m % 2]
                K.op('dve', lambda e: e.reciprocal(out=r[:, 0:nq], in_=z[:, 0:nq]), reads=[z], writes=[r])
                if j == 0:
                    K.tt('dve', o1[i], o1[i][:, 0:nq], o, o[:, 0:nq], r, r[:, 0:nq], ALU.mult)
                    return
                K.tt('dve', o2[i], o2[i][:, 0:nq], o, o[:, 0:nq], r, r[:, 0:nq], ALU.mult)
                K.stt('dve', o1[i], o1[i][:, 0:nq], o2[i], o2[i][:, 0:nq], ls[:, 3:4], o1[i], o1[i][:, 0:nq],
                      ALU.mult, ALU.add, extra_reads=[ls])
                K.tt('pool', osq[i], osq[i][:, 0:nq], o1[i], o1[i][:, 0:nq], o1[i], o1[i][:, 0:nq], ALU.mult)
                K.mm(pN, pN[:, 0:nq], cst, ONESF, osq[i], osq[i][:, 0:nq], True, True)
                K.ts('dve', rs[i], rs[i][:, 0:nq], pN, pN[:, 0:nq], 1.0 / 128, EPS, ALU.mult, ALU.add)
                K.rsqrt(rs[i], rs[i][:, 0:nq])
                k = cnt[0] % 3
                cnt[0] += 1
                K.stt('dve', od[k], od[k][:, 0:nq], o1[i], o1[i][:, 0:nq], dnw[:, 0:1], rs[i], rs[i][:, 0:nq],
                      ALU.mult, ALU.mult, extra_reads=[dnw])
                hg = hh * 4 + h
                K.store('sp', YBR, ybr_d[(24 + hg) * 128:(25 + hg) * 128, q0:q1], od[k], od[k][:, 0:nq])

            attention_diff(K, G, qT, kT, v_tm, finish)


def attention_diff(K, G, qT, kT, v_tm, finish):
    cstb, ONESB = G['cstb'], G['ONESB']
    pS = [K.psum('bS%d' % i, [128, 512], F32) for i in range(3)]
    pO = [K.psum('bO%d' % i, [128, 512], F32) for i in range(2)]
    pZ = [K.psum('bZ%d' % i, [128, 512], F32) for i in range(2)]
    PT = [K.tile('bPT%d' % i, [128, 512], BF16) for i in range(3)]
    qblocks = [(0, 256, 0, 2)] + [(256 + 512 * i, 256 + 512 * (i + 1), 0, NT) for i in range(4)]
    n = 0
    u = 0
    scale = 64 ** -0.5
    for h in range(4):
        for qb, (q0, q1, k0, k1) in enumerate(qblocks):
            nq = q1 - q0
            for j in range(2):
                o, z = pO[u % 2], pZ[u % 2]
                u += 1
                ps_ = slice(j * 64, (j + 1) * 64)
                for kt in range(k0, k1):
                    s = pS[n % 3]
                    pt = PT[n % 3]
                    n += 1
                    K.mm(s, s[:, 0:nq], kT, kT[ps_, h, kt * 128:(kt + 1) * 128], qT, qT[ps_, h, q0:q1], True, True)
                    K.act(pt, pt[:, 0:nq], s, s[:, 0:nq], AF.Exp, scale=scale)
                    K.mm(o, o[:, 0:nq], v_tm, v_tm[:, kt, h * 128:(h + 1) * 128], pt, pt[:, 0:nq],
                         kt == k0, kt == k1 - 1)
                    K.mm(z, z[:, 0:nq], cstb, ONESB, pt, pt[:, 0:nq], kt == k0, kt == k1 - 1)
                finish(h * 2 + j, qb, (q0, q1), o, z)


def layernorm_tile(K, e, xn, xn_ap, src, src_ap, rows, gi, bi, st, mv):
    for c in range(2):
        K.op('dve', lambda en: en.bn_stats(out=st[:, c, :], in_=src_ap[:, c * 512:(c + 1) * 512]),
             reads=[src], writes=[st])
    K.op('dve', lambda en: en.bn_aggr(out=mv[:, 0:2], in_=st[:]), reads=[st], writes=[mv])
    K.ts('dve', mv, mv[:, 2:3], mv, mv[:, 1:2], EPS, None, ALU.add)
    K.rsqrt(mv, mv[:, 2:3])
    K.ts('dve', xn, xn_ap, src, src_ap, mv[:, 0:1], mv[:, 2:3], ALU.subtract, ALU.mult, extra_reads=[mv])
    K.tt(e, xn, xn_ap, xn, xn_ap, rows, rows[:, gi, :], ALU.mult)
    K.tt(e, xn, xn_ap, xn, xn_ap, rows, rows[:, bi, :], ALU.add)


def phase_merge(K, G, hT, wsl, th):
    L, b = G['L'], G['b']
    WD, YBR, ybr_d, XB, xB_d = G['WD'], G['YBR'], G['ybr_d'], G['XB'], G['xB_d']
    cst, cstb, IDB = G['cst'], G['cstb'], G['IDB']
    xsrc = G['xsrc']
    HT = NT // 2
    tiles = list(range(th * HT, (th + 1) * HT))
    with K.scope():
        macc = K.tile('macc', [128, HT, D], F32)
        bg = K.tile('bg', [128, 3072], F32, dma=True)
        K.load('sp', bg, bg[:], WD, G['b_gate'][L:L + 1, :].broadcast_to([128, 3072]))
        branches = [(G['w_sso'], 16, 0), (G['w_gqo'], 8, 16), (G['w_dfo'], 8, 24)]
        for br, (wout, nck, c0) in enumerate(branches):
            with K.scope():
                Wb = K.tile('Wbr', [128, nck, D], BF16, dma=True)
                for c4 in range(0, nck, 4):
                    K.load('pool', Wb, Wb[:, c4:c4 + 4, :], WD,
                           wout[L].rearrange("(c p) f -> p c f", p=128)[:, c4:c4 + 4, :], part=(c4 > 0))
                Wg = K.tile('Wgt', [128, KC, D], BF16, dma=True)
                K.load('pool', Wg, Wg[:, :, 0:512], WD, wsl(O_GATE + br * D, O_GATE + br * D + 512))
                K.load('pool', Wg, Wg[:, :, 512:D], WD, wsl(O_GATE + br * D + 512, O_GATE + (br + 1) * D), part=True)
                yt = [K.tile('ybt%d' % i, [128, nck, 128], BF16, dma=True) for i in range(3)]
                pg = [K.psum('pg%d' % i, [128, 512], F32) for i in range(2)]
                pp = [K.psum('pp%d' % i, [128, 512], F32) for i in range(2)]
                gt = [K.tile('gt%d' % i, [128, 512], F32) for i in range(2)]
                n = 0
                for ti, tt in enumerate(tiles):
                    y = yt[ti % 3]
                    K.load('sp', y, y[:], YBR,
                           ybr_d[c0 * 128:(c0 + nck) * 128, tt * 128:(tt + 1) * 128].rearrange("(c p) t -> p c t", p=128))
                    for hf in range(2):
                        i = n % 2
                        n += 1
                        g_, p_, gg = pg[i], pp[i], gt[i]
                        cs = slice(hf * 512, (hf + 1) * 512)
                        proj_tm(K, g_, hT, tt, Wg, hf * 512, 512)
                        K.tt('dve', gg, gg[:], g_, g_[:], bg, bg[:, br * D + hf * 512:br * D + (hf + 1) * 512], ALU.add)
                        K.act(gg, gg[:], gg, gg[:], AF.Sigmoid)
                        for c in range(nck):
                            K.mm(p_, p_[:], y, y[:, c, :], Wb, Wb[:, c, cs], c == 0, c == nck - 1)
                        if br == 0:
                            K.tt('dve', macc, macc[:, ti, cs], p_, p_[:], gg, gg[:], ALU.mult)
                        else:
                            K.tt('dve', gg, gg[:], p_, p_[:], gg, gg[:], ALU.mult)
                            K.tt('pool', macc, macc[:, ti, cs], macc, macc[:, ti, cs], gg, gg[:], ALU.add)
        with K.scope():
            rows = load_rows(K, G, 2, G['ln1_g'], G['ln1_b'])
            Wo = K.tile('Wo', [128, KC, D], BF16, dma=True)
            for c4 in range(0, KC, 4):
                K.load('pool', Wo, Wo[:, c4:c4 + 4, :], WD,
                       G['w_o'][L].rearrange("(c p) f -> p c f", p=128)[:, c4:c4 + 4, :], part=(c4 > 0))
            mb = [K.tile('mb%d' % i, [128, D], BF16) for i in range(2)]
            mT = [K.tile('mT%d' % i, [128, KC, 128], BF16) for i in range(2)]
            xt = [K.tile('xt%d' % i, [128, D], F32, dma=True) for i in range(2)]
            xn = [K.tile('xn%d' % i, [128, D], F32, dma=True) for i in range(2)]
            st = [K.tile('st%d' % i, [128, 2, 6], F32) for i in range(2)]
            mv = [K.tile('mv%d' % i, [128, 4], F32) for i in range(2)]
            ptr = [K.psum('mptr%d' % i, [128, 1024], BF16) for i in range(2)]
            py = [K.psum('mpy%d' % i, [128, 512], F32) for i in range(4)]
            for ti, tt in enumerate(tiles):
                i = ti % 2
                isc = 1 if tt < 2 else 0
                x_ = xt[i]
                db, ap = xsrc(tt, 1)
                K.load('sp', x_, x_[:], db, ap)
                K.copy('act', mb[i], mb[i][:], macc, macc[:, ti, :])
                q = ptr[i]
                for c in range(KC):
                    K.tr(q, q[:, c * 128:(c + 1) * 128], mb[i], mb[i][:, c * 128:(c + 1) * 128], cstb, IDB)
                K.copy('act', mT[i], mT[i][:].rearrange("p c t -> p (c t)"), q, q[:])
                xo = xn[i]
                for hf in range(2):
                    p_ = py[(ti * 2 + hf) % 4]
                    cs = slice(hf * 512, (hf + 1) * 512)
                    for c in range(KC):
                        K.mm(p_, p_[:], mT[i], mT[i][:, c, :], Wo, Wo[:, c, cs], c == 0, c == KC - 1)
                    K.tt('dve', xo, xo[:, cs], p_, p_[:], rows, rows[:, isc, cs], ALU.mult)
                K.stt('dve', xo, xo[:], x_, x_[:], ALPHA, xo, xo[:], ALU.mult, ALU.add)
                layernorm_tile(K, 'pool', xo, xo[:], xo, xo[:], rows, 2, 3, st[i], mv[i])
                K.store('sp', XB, xB_d[tt * 128:(tt + 1) * 128, :], xo, xo[:])


def phase_ffn(K, G, hT):
    L, b = G['L'], G['b']
    WD, XB, xB_d, XA, xA_d, OUT, out = G['WD'], G['XB'], G['xB_d'], G['XA'], G['xA_d'], G['OUT'], G['out']
    NF = FFH // 128
    HTOK = T // 2
    with K.scope():
        rows = load_rows(K, G, 5, G['ln2_g'], G['ln2_b'])
        Wo = K.tile('Wfo', [128, NF, D], BF16, dma=True)
        wvo = G['ffn_wo'][L].rearrange("(c p) f -> p c f", p=128)
        for c0 in range(0, NF, 4):
            c1 = min(NF, c0 + 4)
            K.load('pool', Wo, Wo[:, c0:c1, :], WD, wvo[:, c0:c1, :], part=(c0 > 0))
        uT = K.tile('uT', [128, NF, HTOK], BF16)
        for th in range(2):
            base = th * HTOK
            blocks = [(0, 256), (256, 768), (768, 1152)] if th == 0 else [(0, 512), (512, 1024), (1024, 1152)]
            with K.scope():
                Wa = [K.tile('Wa%d' % i, [128, KC, 256], BF16, dma=True) for i in range(3)]
                pa = [K.psum('pa%d' % i, [128, 512], F32) for i in range(3)]
                pb = [K.psum('pb%d' % i, [128, 512], F32) for i in range(3)]
                sa = [K.tile('sa%d' % i, [128, 512], F32) for i in range(2)]
                wv = G['ffn_wi'][L].rearrange("(kc p) f -> p kc f", p=128)
                n = 0
                for fc in range(NF):
                    W = Wa[fc % 3]
                    K.load('pool', W, W[:, :, 0:128], WD, wv[:, :, fc * 128:(fc + 1) * 128])
                    K.load('pool', W, W[:, :, 128:256], WD, wv[:, :, FFH + fc * 128:FFH + (fc + 1) * 128], part=True)
                    for (a0, a1) in blocks:
                        na = a1 - a0
                        A, B_, s_ = pa[n % 3], pb[n % 3], sa[n % 2]
                        n += 1
                        for kc in range(KC):
                            K.mm(A, A[:, 0:na], W, W[:, kc, 0:128], hT, hT[:, kc, base + a0:base + a1],
                                 kc == 0, kc == KC - 1)
                        for kc in range(KC):
                            K.mm(B_, B_[:, 0:na], W, W[:, kc, 128:256], hT, hT[:, kc, base + a0:base + a1],
                                 kc == 0, kc == KC - 1)
                        K.act(s_, s_[:, 0:na], A, A[:, 0:na], AF.Silu)
                        K.tt('dve', uT, uT[:, fc, a0:a1], s_, s_[:, 0:na], B_, B_[:, 0:na], ALU.mult)
            with K.scope():
                xt = [K.tile('fxt%d' % i, [128, D], F32, dma=True) for i in range(2)]
                xn = [K.tile('fxn%d' % i, [128, D], F32, dma=True) for i in range(2)]
                st = [K.tile('fst%d' % i, [128, 2, 6], F32) for i in range(2)]
                mv = [K.tile('fmv%d' % i, [128, 4], F32) for i in range(2)]
                py = [K.psum('fpy%d' % i, [128, 512], F32) for i in range(4)]
                for ti in range(NT // 2):
                    tt = th * (NT // 2) + ti
                    i = ti % 2
                    isc = 1 if tt < 2 else 0
                    x_ = xt[i]
                    K.load('sp', x_, x_[:], XB, xB_d[tt * 128:(tt + 1) * 128, :])
                    xo = xn[i]
                    for hf in range(2):
                        p_ = py[(ti * 2 + hf) % 4]
                        cs = slice(hf * 512, (hf + 1) * 512)
                        for c in range(NF):
                            K.mm(p_, p_[:], uT, uT[:, c, ti * 128:(ti + 1) * 128], Wo, Wo[:, c, cs],
                                 c == 0, c == NF - 1)
                        K.tt('dve', xo, xo[:, cs], p_, p_[:], rows, rows[:, isc, cs], ALU.mult)
                    K.stt('dve', xo, xo[:], x_, x_[:], ALPHA, xo, xo[:], ALU.mult, ALU.add)
                    layernorm_tile(K, 'pool', xo, xo[:], xo, xo[:], rows, 2, 3, st[i], mv[i])
                    if G['L'] == G['depth_run'] - 1:
                        if tt >= 2:
                            K.store('sp', OUT, out[b, (tt - 2) * 128:(tt - 1) * 128, :], xo, xo[:])
                    else:
                        K.store('sp', XA, xA_d[tt * 128:(tt + 1) * 128, :], xo, xo[:])


def host_consts():
    p = np.arange(128)[:, None]
    f = np.arange(128)[None, :]
    c = np.zeros((128, 6, 128), np.float32)
    c[:, 0] = (p == f)
    c[:, 1] = (p <= f)
    c[:, 2] = (p > f)
    c[:, 3] = (p >= f)
    c[:, 4] = (p < f)
    c[:, 5] = 1.0
    return c


def host_rope(hd):
    t = np.arange(LAT)
    pos_row = (t // 64).astype(np.float32)
    pos_col = (t % 64).astype(np.float32)
    d_axis = hd // 2
    inv = (10000.0 ** (-np.arange(0, d_axis, 2, dtype=np.float32) / d_axis)).astype(np.float32)
    ar = pos_row[:, None] * inv
    ac = pos_col[:, None] * inv
    ang = np.concatenate([ar, ar, ac, ac], axis=-1).astype(np.float32)
    cos = np.cos(ang).astype(np.float32)
    sin = np.sin(ang).astype(np.float32)
    q = hd // 4
    sign = np.concatenate([-np.ones(q), np.ones(q), -np.ones(q), np.ones(q)]).astype(np.float32)
    tab = np.stack([cos, sin * sign], axis=1)
    return np.ascontiguousarray(tab.reshape(16, 128, 2, hd).transpose(1, 0, 2, 3))


_CACHE = {}


def run(inputs, NB, depth_run, ncores=8):
    key = (NB, depth_run)
    if key not in _CACHE:
        _CACHE[key] = build_program(NB, depth_run)
    nc = _CACHE[key]
    f = lambda k: np.ascontiguousarray(np.asarray(inputs[k], dtype=np.float32))
    shared = {
        "ada_w": f('ada_w'), "ada_b": f('ada_b'),
        "ada_b_fm": np.ascontiguousarray(f('ada_b').reshape(DEPTH, 48, 128).transpose(0, 2, 1)),
        "w_in": f('w_in'), "b_gate": f('b_gate'),
        "conv_w": np.ascontiguousarray(f('ssm_conv_w').reshape(DEPTH, 5, 24, 128).transpose(0, 3, 2, 1)),
        "conv_b": np.ascontiguousarray(f('ssm_conv_b').reshape(DEPTH, 24, 128).transpose(0, 2, 1)),
        "ssm_dt_bias": f('ssm_dt_bias').reshape(DEPTH, 64), "ssm_a_log": f('ssm_a_log').reshape(DEPTH, 64),
        "ssm_d": f('ssm_d'), "ssm_norm_w": f('ssm_norm_w'), "w_ssm_out": f('w_ssm_out'),
        "gqa_q_norm": f('gqa_q_norm'), "gqa_k_norm": f('gqa_k_norm'), "w_gqa_out": f('w_gqa_out'),
        "diff_lambda": f('diff_lambda').reshape(DEPTH, 256),
        "diff_norm_w": f('diff_norm_w').reshape(DEPTH, 128, 1),
        "w_diff_out": f('w_diff_out'), "w_o": f('w_o'), "ln1_g": f('ln1_g'), "ln1_b": f('ln1_b'),
        "ffn_w_in": f('ffn_w_in'), "ffn_w_out": f('ffn_w_out'), "ln2_g": f('ln2_g'), "ln2_b": f('ln2_b'),
        "consts": host_consts(), "ropeG": host_rope(128), "ropeD": host_rope(64),
    }
    x, c, ctx, c_ctx = f('x'), f('c'), f('ctx'), f('c_ctx')
    in_maps = []
    for i in range(ncores):
        sl = slice(i * NB, (i + 1) * NB)
        c5 = np.zeros((5, D), np.float32)
        c5[:NB] = c[sl]
        c5[4] = c_ctx
        cT = np.ascontiguousarray(c5.reshape(5, KC, 128).transpose(2, 1, 0))
        m = dict(shared)
        m.update({"x": x[sl], "ctx": ctx[sl], "cT": cT})
        in_maps.append(m)
    res = run_bass_kernel_spmd(nc, in_maps, core_ids=list(range(ncores)))
    return np.concatenate([r["out"] for r in res.results], axis=0)


def kernel(**inputs):
    return run(inputs, 4, DEPTH).astype(np.float32)
```

```python
import math
from contextlib import ExitStack, contextmanager

import numpy as np
import concourse.bass as bass
import concourse.mybir as mybir
from concourse.bass_utils import run_bass_kernel_spmd

F32 = mybir.dt.float32
BF16 = mybir.dt.bfloat16
AF = mybir.ActivationFunctionType
ALU = mybir.AluOpType
AX = mybir.AxisListType

D = 1024
KC = 8
T = 2304
NT = 18
LAT = 2048
CTX = 256
DEPTH = 4
ALPHA = (2 * DEPTH) ** 0.25
EPS = 1e-6
FFH = 2816
O_Z, O_XBC, O_DT, O_GQ, O_GK, O_GV, O_DQ, O_DK, O_DV, O_GATE = (
    0, 2048, 5120, 5184, 6208, 6464, 6720, 7744, 8768, 9792)
TB = [(0, 256), (256, 768), (768, 1280), (1280, 1792), (1792, 2304)]
NSEM = 40
MARKS = []


class Buf:
    def __init__(self, t, sem=None):
        self.t = t
        self.w = {}
        self.r = {}
        self.sem = sem

    def __getitem__(self, idx):
        return self.t[idx]


class DBuf:
    def __init__(self, ap):
        self.ap = ap
        self.w = {}
        self.r = {}


class Kx:
    def __init__(self, nc, es):
        self.nc = nc
        self.eng = {'pe': nc.tensor, 'act': nc.scalar, 'dve': nc.vector,
                    'pool': nc.gpsimd, 'sp': nc.sync}
        self.sem = {k: es.enter_context(nc.semaphore('sem_' + k)) for k in self.eng}
        self.cnt = {k: 0 for k in self.eng}
        self.seen = {k: {} for k in self.eng}
        self.sempool = [[es.enter_context(nc.semaphore('dsem%d' % i)), 'd%d' % i, 0]
                        for i in range(NSEM)]
        self.pending = {}
        self.dbufs = []
        self.stacks = [es]
        self.scope_sems = [[]]
        self.uid = 0

    def tile(self, name, shape, dtype, dma=False):
        self.uid += 1
        t = self.stacks[-1].enter_context(
            self.nc.sbuf_tensor('%s_%d' % (name, self.uid), list(shape), dtype))
        sem = None
        if dma:
            sem = self.sempool.pop()
            self.scope_sems[-1].append(sem)
        return Buf(t, sem)

    def psum(self, name, shape, dtype):
        self.uid += 1
        t = self.stacks[-1].enter_context(
            self.nc.psum_tensor('%s_%d' % (name, self.uid), list(shape), dtype))
        return Buf(t)

    def dbuf(self, ap):
        d = DBuf(ap)
        self.dbufs.append(d)
        return d

    @contextmanager
    def scope(self, name=None):
        es = ExitStack()
        self.stacks.append(es)
        self.scope_sems.append([])
        c0 = self.cnt['pe']
        try:
            yield
        finally:
            if name is not None:
                MARKS.append((name, c0, self.cnt['pe']))
            self.barrier()
            for s in self.scope_sems.pop():
                self.sempool.append(s)
            self.stacks.pop()
            es.close()

    def _need(self, e, deps):
        need = {}
        for (sem, key, val) in deps:
            if key == e and e == 'pe':
                continue
            if self.seen[e].get(key, 0) >= val:
                continue
            if key not in need or need[key][2] < val:
                need[key] = (sem, key, val)
        return list(need.values())

    def _wait(self, e, deps, attach=False):
        need = self._need(e, deps)
        last = None
        if attach and need:
            last = need.pop()
        for (sem, key, val) in need:
            self.eng[e].wait_ge(sem, val)
            self.seen[e][key] = val
        if last is not None:
            self.seen[e][last[1]] = last[2]
        return last

    def op(self, e, fn, reads=(), writes=()):
        deps = []
        for b in reads:
            deps += list(b.w.values())
        for b in writes:
            deps += [t for t in b.w.values() if t[1] != e]
            deps += [t for t in b.r.values() if t[1] != e]
        last = self._wait(e, deps, attach=False)
        ins = fn(self.eng[e])
        if last is not None:
            ins._wait_ge(last[0], last[2])
        self.cnt[e] += 1
        ins.then_inc(self.sem[e], 1)
        tok = (self.sem[e], e, self.cnt[e])
        for b in reads:
            b.r[e] = tok
        for b in writes:
            b.w = {e: tok}
            b.r = {}
        return ins

    def load(self, q, sb, out_ap, dr, in_ap, part=False):
        deps = list(dr.w.values()) + list(sb.r.values())
        deps += [t for t in sb.w.values() if not (part and t[1] == sb.sem[1])]
        self._wait(q, deps)
        ins = self.eng[q].dma_start(out=out_ap, in_=in_ap)
        sb.sem[2] += 16
        ins.then_inc(sb.sem[0], 16)
        tok = (sb.sem[0], sb.sem[1], sb.sem[2])
        self.pending[tok[1]] = tok
        dr.r[tok[1]] = tok
        if part:
            sb.w[tok[1]] = tok
        else:
            sb.w = {tok[1]: tok}
        sb.r = {}

    def store(self, q, dr, out_ap, sb, in_ap):
        deps = list(sb.w.values()) + list(dr.r.values())
        self._wait(q, deps)
        ins = self.eng[q].dma_start(out=out_ap, in_=in_ap)
        sb.sem[2] += 16
        ins.then_inc(sb.sem[0], 16)
        tok = (sb.sem[0], sb.sem[1], sb.sem[2])
        self.pending[tok[1]] = tok
        sb.r[tok[1]] = tok
        dr.w[tok[1]] = tok

    def barrier(self):
        sp = 'sp'
        deps = [(self.sem[e], e, self.cnt[e]) for e in ('pe', 'act', 'dve', 'pool')
                if self.cnt[e] > 0]
        deps += list(self.pending.values())
        self._wait(sp, deps)
        self.eng[sp].sem_inc(self.sem[sp], 1)
        self.cnt[sp] += 1
        tok = (self.sem[sp], sp, self.cnt[sp])
        for e in ('pe', 'act', 'dve', 'pool'):
            self._wait(e, [tok])
        self.pending = {}
        for d in self.dbufs:
            d.w = {}
            d.r = {}

    def mm(self, ps, out_ap, lhsT_b, lhsT_ap, rhs_b, rhs_ap, start, stop):
        return self.op('pe', lambda e: e.matmul(out_ap, lhsT_ap, rhs_ap, start=start, stop=stop),
                       reads=[lhsT_b, rhs_b], writes=[ps])

    def tr(self, ps, out_ap, in_b, in_ap, ident_b, ident_ap):
        return self.op('pe', lambda e: e.transpose(out_ap, in_ap, ident_ap),
                       reads=[in_b, ident_b], writes=[ps])

    def act(self, out_b, out_ap, in_b, in_ap, func, bias=None, scale=None, extra_reads=(),
            accum=None, accum_b=None):
        kw = {}
        if bias is not None:
            kw['bias'] = bias
        if scale is not None:
            kw['scale'] = scale
        if accum is not None:
            kw['accum_out'] = accum
        wr = [out_b] + ([accum_b] if accum_b is not None else [])
        return self.op('act', lambda e: e.activation(out=out_ap, in_=in_ap, func=func, **kw),
                       reads=[in_b] + list(extra_reads), writes=wr)

    def tt(self, e, out_b, out_ap, a_b, a_ap, b_b, b_ap, op):
        return self.op(e, lambda en: en.tensor_tensor(out=out_ap, in0=a_ap, in1=b_ap, op=op),
                       reads=[a_b, b_b], writes=[out_b])

    def ts(self, e, out_b, out_ap, a_b, a_ap, s1, s2, op0, op1=None, extra_reads=()):
        if op1 is None:
            f = lambda en: en.tensor_scalar(out=out_ap, in0=a_ap, scalar1=s1, scalar2=None, op0=op0)
        else:
            f = lambda en: en.tensor_scalar(out=out_ap, in0=a_ap, scalar1=s1, scalar2=s2,
                                            op0=op0, op1=op1)
        return self.op(e, f, reads=[a_b] + list(extra_reads), writes=[out_b])

    def stt(self, e, out_b, out_ap, a_b, a_ap, scalar, b_b, b_ap, op0, op1, extra_reads=()):
        return self.op(e, lambda en: en.scalar_tensor_tensor(out=out_ap, in0=a_ap, scalar=scalar,
                                                             in1=b_ap, op0=op0, op1=op1),
                       reads=[a_b, b_b] + list(extra_reads), writes=[out_b])

    def copy(self, e, out_b, out_ap, in_b, in_ap):
        if e == 'act':
            return self.op('act', lambda en: en.copy(out=out_ap, in_=in_ap),
                           reads=[in_b], writes=[out_b])
        return self.op(e, lambda en: en.tensor_copy(out=out_ap, in_=in_ap),
                       reads=[in_b], writes=[out_b])

    def rsqrt(self, b, ap):
        self.act(b, ap, b, ap, AF.Ln)
        self.act(b, ap, b, ap, AF.Exp, scale=-0.5)

    def memset(self, e, b, ap, val):
        return self.op(e, lambda en: en.memset(ap, val), reads=[], writes=[b])


def bc(ap, shape):
    return ap.broadcast_to(list(shape))


def build_program(NB, depth_run, debug=False):
    nc = bass.Bass("TRN2", target_bir_lowering=False)

    def din(name, shape, dt=F32):
        return nc.dram_tensor(name, list(shape), dt, kind="ExternalInput").ap()

    x_in = din("x", [NB, LAT, D])
    ctx_in = din("ctx", [NB, CTX, D])
    cT_in = din("cT", [128, KC, 5])
    ada_w = din("ada_w", [DEPTH, D, 6 * D])
    ada_b_tm = din("ada_b", [DEPTH, 6 * D])
    ada_b_fm = din("ada_b_fm", [DEPTH, 128, 48])
    w_in = din("w_in", [DEPTH, D, 12864])
    b_gate = din("b_gate", [DEPTH, 3072])
    conv_w = din("conv_w", [DEPTH, 128, 24, 5])
    conv_b = din("conv_b", [DEPTH, 128, 24])
    dt_bias = din("ssm_dt_bias", [DEPTH, 64])
    a_log = din("ssm_a_log", [DEPTH, 64])
    ssm_d = din("ssm_d", [DEPTH, 32])
    ssm_nw = din("ssm_norm_w", [DEPTH, 2048])
    w_sso = din("w_ssm_out", [DEPTH, 2048, D])
    gq_norm = din("gqa_q_norm", [DEPTH, 128])
    gk_norm = din("gqa_k_norm", [DEPTH, 128])
    w_gqo = din("w_gqa_out", [DEPTH, D, D])
    dlam = din("diff_lambda", [DEPTH, 256])
    dnw_fm = din("diff_norm_w", [DEPTH, 128, 1])
    w_dfo = din("w_diff_out", [DEPTH, D, D])
    w_o = din("w_o", [DEPTH, D, D])
    ln1_g = din("ln1_g", [DEPTH, D])
    ln1_b = din("ln1_b", [DEPTH, D])
    ffn_wi = din("ffn_w_in", [DEPTH, D, 2 * FFH])
    ffn_wo = din("ffn_w_out", [DEPTH, FFH, D])
    ln2_g = din("ln2_g", [DEPTH, D])
    ln2_b = din("ln2_b", [DEPTH, D])
    consts_in = din("consts", [128, 6, 128])
    ropeG_in = din("ropeG", [128, 16, 2, 128])
    ropeD_in = din("ropeD", [128, 16, 2, 64])
    out = nc.dram_tensor("out", [NB, LAT, D], F32, kind="ExternalOutput").ap()

    def scratch(name, shape, dt):
        return nc.dram_tensor(name, list(shape), dt, kind="Internal").ap()

    xA_d = scratch("xA", [T, D], F32)
    xB_d = scratch("xB", [T, D], F32)
    ybr_d = scratch("ybr", [32 * 128, T], BF16)
    mod_d = scratch("mod_tm", [DEPTH, 5, 6 * D], F32)

    with ExitStack() as es:
        K = Kx(nc, es)
        X_IN = K.dbuf(x_in)
        CTX_IN = K.dbuf(ctx_in)
        OUT = K.dbuf(out)
        XA = K.dbuf(xA_d)
        XB = K.dbuf(xB_d)
        YBR = K.dbuf(ybr_d)
        MODD = K.dbuf(mod_d)
        WD = K.dbuf(None)

        cst = K.tile('cst', [128, 6, 128], F32, dma=True)
        K.load('sp', cst, cst[:], WD, consts_in)
        IDF, MLE, MGT, MGE, MLT, ONESF = (cst[:, i, :] for i in range(6))
        cstb = K.tile('cstb', [128, 2, 128], BF16)
        K.copy('dve', cstb, cstb[:, 0, :], cst, cst[:, 0, :])
        K.copy('dve', cstb, cstb[:, 1, :], cst, cst[:, 5, :])
        IDB = cstb[:, 0, :]
        ONESB = cstb[:, 1, :]
        modT = K.tile('modT', [128, DEPTH, 48, 5], F32)

        with K.scope('adaLN'):
            cT = K.tile('cT', [128, KC, 5], F32, dma=True)
            K.load('sp', cT, cT[:], WD, cT_in)
            scT = K.tile('scT', [128, KC, 5], BF16)
            K.act(scT, scT[:], cT, cT[:], AF.Silu)
            abf = K.tile('abf', [128, DEPTH, 48], F32, dma=True)
            K.load('sp', abf, abf[:], WD, ada_b_fm.rearrange("l p c -> p l c"))
            wts = [K.tile('adaw%d' % i, [128, KC, 512], BF16, dma=True) for i in range(2)]
            abt = [K.tile('abt%d' % i, [5, 512], F32, dma=True) for i in range(2)]
            mo = [K.tile('mo%d' % i, [5, 512], F32, dma=True) for i in range(2)]
            ps_t = [K.psum('ps0t%d' % i, [128, 512], F32) for i in range(2)]
            ps_f = [K.psum('ps0f%d' % i, [128, 4, 5], F32) for i in range(2)]
            n = 0
            for L in range(depth_run):
                for j in range(12):
                    W = wts[n % 2]
                    K.load('pool', W, W[:], WD,
                           ada_w[L].rearrange("(kc p) f -> p kc f", p=128)[:, :, j * 512:(j + 1) * 512])
                    ab = abt[n % 2]
                    K.load('sp', ab, ab[:], WD,
                           ada_b_tm[L:L + 1, j * 512:(j + 1) * 512].broadcast_to([5, 512]))
                    pt = ps_t[n % 2]
                    for kc in range(KC):
                        K.mm(pt, pt[0:5, :], scT, scT[:, kc, :], W, W[:, kc, :], kc == 0, kc == KC - 1)
                    m = mo[n % 2]
                    K.tt('dve', m, m[:], pt, pt[0:5, :], ab, ab[:], ALU.add)
                    K.store('sp', MODD, mod_d[L, :, j * 512:(j + 1) * 512], m, m[:])
                    pf = ps_f[n % 2]
                    for f in range(4):
                        for kc in range(KC):
                            K.mm(pf, pf[:, f, :], W, W[:, kc, f * 128:(f + 1) * 128], scT, scT[:, kc, :],
                                 kc == 0, kc == KC - 1)
                    K.tt('dve', modT, modT[:, L, j * 4:(j + 1) * 4, :], pf, pf[:],
                         abf, bc(abf[:, L, j * 4:(j + 1) * 4, None], [128, 4, 5]), ALU.add)
                    n += 1

        for b in range(NB):
            for L in range(depth_run):
                last = (L == DEPTH - 1)
                lam_init = 0.8 - 0.6 * math.exp(-0.3 * L)
                if L == 0:
                    def xsrc(t0, nt, b=b):
                        if t0 < 2:
                            return CTX_IN, ctx_in[b, t0 * 128:(t0 + nt) * 128, :]
                        return X_IN, x_in[b, (t0 - 2) * 128:(t0 - 2 + nt) * 128, :]
                else:
                    def xsrc(t0, nt):
                        return XA, xA_d[t0 * 128:(t0 + nt) * 128, :]
                with K.scope():
                    layer(K, nc, locals())
    return nc


def layer(K, nc, G):
    b, L, last, lam_init, xsrc = G['b'], G['L'], G['last'], G['lam_init'], G['xsrc']
    WD, MODD, YBR, XA, XB, OUT = G['WD'], G['MODD'], G['YBR'], G['XA'], G['XB'], G['OUT']
    cst, cstb, modT = G['cst'], G['cstb'], G['modT']
    IDF, MLE, MGT, MGE, MLT, ONESF, IDB, ONESB = (G[k] for k in
                                                  ('IDF', 'MLE', 'MGT', 'MGE', 'MLT', 'ONESF', 'IDB', 'ONESB'))
    w_in = G['w_in']
    mod_d, ybr_d, xA_d, xB_d, out = G['mod_d'], G['ybr_d'], G['xA_d'], G['xB_d'], G['out']

    def wsl(c0, c1):
        return w_in[L].rearrange("(kc p) f -> p kc f", p=128)[:, :, c0:c1]

    sc1 = K.tile('sc1', [128, 2, 8], F32)
    sh1 = K.tile('sh1', [128, 2, 8], F32)
    sc2 = K.tile('sc2', [128, 2, 8], F32)
    sh2 = K.tile('sh2', [128, 2, 8], F32)
    for j, m in enumerate((b, 4)):
        K.copy('dve', sh1, sh1[:, j, :], modT, modT[:, L, 0:8, m])
        K.ts('dve', sc1, sc1[:, j, :], modT, modT[:, L, 8:16, m], 1.0, None, ALU.add)
        K.copy('dve', sh2, sh2[:, j, :], modT, modT[:, L, 24:32, m])
        K.ts('dve', sc2, sc2[:, j, :], modT, modT[:, L, 32:40, m], 1.0, None, ALU.add)
    hT = K.tile('hT', [128, KC, T], BF16)

    phase_A(K, G, hT, sc1, sh1, xsrc)
    phase_ssm(K, G, hT, wsl)
    phase_gqa(K, G, hT, wsl)
    for hh in range(2):
        phase_diff(K, G, hT, wsl, hh)
    for th in range(2):
        phase_merge(K, G, hT, wsl, th)
    phase_A(K, G, hT, sc2, sh2, lambda t0, nt: (XB, xB_d[t0 * 128:(t0 + nt) * 128, :]))
    phase_ffn(K, G, hT)


def load_rows(K, G, kmod, lg, lb):
    L, b = G['L'], G['b']
    rows = K.tile('rows', [128, 4, D], F32, dma=True)
    for j, m in enumerate((b, 4)):
        K.load('sp', rows, rows[:, j, :], G['MODD'],
               G['mod_d'][L, m:m + 1, kmod * D:(kmod + 1) * D].broadcast_to([128, D]), part=(j > 0))
    for j, src in enumerate((lg, lb)):
        K.load('sp', rows, rows[:, 2 + j, :], G['WD'], src[L:L + 1, :].broadcast_to([128, D]), part=True)
    return rows


def phase_A(K, G, hT, sc1, sh1, xsrc):
    IDF = G['IDF']
    cst = G['cst']
    with K.scope('A'):
        ps = [K.psum('psA%d' % i, [128, 512], F32) for i in range(4)]
        xg = [K.tile('xg%d' % i, [128, 4, D], F32, dma=True) for i in range(2)]
        groups = [(0, 2, 1), (2, 4, 0), (6, 4, 0), (10, 4, 0), (14, 4, 0)]
        n = 0
        for gi, (t0, nt, isc) in enumerate(groups):
            xb = xg[gi % 2]
            db, ap = xsrc(t0, nt)
            K.load('sp', xb, xb[:, 0:nt, :], db, ap.rearrange("(n p) d -> p n d", p=128))
            for kc in range(KC):
                p = ps[n % 4]
                for j in range(nt):
                    K.tr(p, p[:, j * 128:(j + 1) * 128], xb, xb[:, j, kc * 128:(kc + 1) * 128], cst, IDF)
                o = hT[:, kc, t0 * 128:(t0 + nt) * 128]
                if n % 2 == 0:
                    K.ts('dve', hT, o, p, p[:, 0:nt * 128], sc1[:, isc, kc:kc + 1], sh1[:, isc, kc:kc + 1],
                         ALU.mult, ALU.add, extra_reads=[sc1, sh1])
                else:
                    K.act(hT, o, p, p[:, 0:nt * 128], AF.Identity, bias=sh1[:, isc, kc:kc + 1],
                          scale=sc1[:, isc, kc:kc + 1], extra_reads=[sc1, sh1])
                n += 1


def proj_tm(K, ps, hT, tt, W, c0, ncols):
    for kc in range(KC):
        K.mm(ps, ps[:, 0:ncols], hT, hT[:, kc, tt * 128:(tt + 1) * 128], W, W[:, kc, c0:c0 + ncols],
             kc == 0, kc == KC - 1)


def phase_ssm(K, G, hT, wsl):
    L = G['L']
    WD, YBR, ybr_d = G['WD'], G['YBR'], G['ybr_d']
    cst, cstb = G['cst'], G['cstb']
    IDB, ONESF = G['IDB'], G['ONESF']
    MLE, MGT, MGE, MLT = G['MLE'], G['MGT'], G['MGE'], G['MLT']
    with K.scope():
        dt = K.tile('dt', [128, NT, 64], F32)
        adt = K.tile('adt', [128, NT, 64], F32)
        prm = K.tile('prm', [128, 3, 64], F32, dma=True)
        K.load('sp', prm, prm[:, 0, :], WD, G['dt_bias'][L:L + 1, :].broadcast_to([128, 64]))
        K.load('sp', prm, prm[:, 1, :], WD, G['a_log'][L:L + 1, :].broadcast_to([128, 64]), part=True)
        K.load('sp', prm, prm[:, 2, 0:32], WD, G['ssm_d'][L:L + 1, :].broadcast_to([128, 32]), part=True)
        nw = K.tile('nw', [128, 2048], F32, dma=True)
        K.load('sp', nw, nw[:], WD, G['ssm_nw'][L:L + 1, :].broadcast_to([128, 2048]))
        cw = K.tile('cw', [128, 24, 5], F32, dma=True)
        K.load('sp', cw, cw[:], WD, G['conv_w'][L])
        cb = K.tile('cb', [128, 24], F32, dma=True)
        K.load('sp', cb, cb[:], WD, G['conv_b'][L])
        aneg = K.tile('aneg', [128, 64], F32)
        K.act(aneg, aneg[:], prm, prm[:, 1, :], AF.Exp)
        K.ts('dve', aneg, aneg[:], aneg, aneg[:], -1.0, None, ALU.mult)
        with K.scope('ssm_dt'):
            Wdt = K.tile('Wdt', [128, KC, 64], BF16, dma=True)
            K.load('pool', Wdt, Wdt[:], WD, wsl(O_DT, O_DT + 64))
            psd = [K.psum('psd%d' % i, [128, 64], F32) for i in range(2)]
            tmp = [K.tile('dtt%d' % i, [128, 64], F32) for i in range(2)]
            for tt in range(NT):
                p = psd[tt % 2]
                proj_tm(K, p, hT, tt, Wdt, 0, 64)
                t1 = tmp[tt % 2]
                K.tt('dve', t1, t1[:], p, p[:], prm, prm[:, 0, :], ALU.add)
                K.act(t1, t1[:], t1, t1[:], AF.Exp)
                K.act(dt, dt[:, tt, :], t1, t1[:], AF.Ln, bias=1.0)
            K.tt('dve', adt, adt[:], dt, dt[:], aneg, bc(aneg[:, None, :], [128, NT, 64]), ALU.mult)

        for g in range(4):
            with K.scope():
                ssm_group(K, G, hT, wsl, g, dt, adt, prm, nw, cw, cb)


def ssm_group(K, G, hT, wsl, g, dt, adt, prm, nw, cw, cb):
    WD, YBR, ybr_d = G['WD'], G['YBR'], G['ybr_d']
    cst, cstb = G['cst'], G['cstb']
    IDB, ONESF = G['IDB'], G['ONESF']
    MLE, MGT, MGE, MLT = G['MLE'], G['MGT'], G['MGE'], G['MLT']
    PADW = 2316
    xs_tm = K.tile('xs_tm', [128, NT, 512], BF16)
    B_tm = K.tile('B_tm', [128, NT, 128], BF16)
    BT = K.tile('BT', [128, T], BF16)
    CT = K.tile('CT', [128, T], BF16)
    with K.scope('ssm_B1'):
        W6 = K.tile('W6', [128, KC, 768], BF16, dma=True)
        K.load('pool', W6, W6[:, :, 0:512], WD, wsl(O_XBC + g * 512, O_XBC + (g + 1) * 512))
        K.load('pool', W6, W6[:, :, 512:640], WD,
               wsl(O_XBC + 2048 + g * 128, O_XBC + 2048 + (g + 1) * 128), part=True)
        K.load('pool', W6, W6[:, :, 640:768], WD,
               wsl(O_XBC + 2560 + g * 128, O_XBC + 2560 + (g + 1) * 128), part=True)
        cchunk = [g * 4 + 0, g * 4 + 1, g * 4 + 2, g * 4 + 3, 16 + g, 20 + g]
        xcT = K.tile('xcT', [128, 4, T], BF16)
        xpad = [K.tile('xpad%d' % i, [128, PADW], F32) for i in range(2)]
        cv = [K.tile('cv0', [128, PADW], F32)]
        for i in range(2):
            K.memset('pool', xpad[i], xpad[i][:], 0.0)
        psp = [K.psum('psp%d' % i, [128, 512], F32) for i in range(3)]
        pst = [K.psum('pst%d' % i, [128, 512], BF16) for i in range(2)]
        n = 0
        for fc in range(6):
            xp = xpad[fc % 2]
            c = cv[0]
            cc = cchunk[fc]
            for (a0, a1) in TB:
                p = psp[n % 3]
                n += 1
                for kc in range(KC):
                    K.mm(p, p[:, 0:a1 - a0], W6, W6[:, kc, fc * 128:(fc + 1) * 128], hT, hT[:, kc, a0:a1],
                         kc == 0, kc == KC - 1)
                off = 2 if a0 < 256 else 6
                K.copy('act', xp, xp[:, a0 + off:a1 + off], p, p[:, 0:a1 - a0])
            W = 2308
            K.ts('dve', c, c[:, 2:2 + W], xp, xp[:, 2:2 + W], cw[:, cc, 2:3], cb[:, cc:cc + 1],
                 ALU.mult, ALU.add, extra_reads=[cw, cb])
            for j in (0, 1, 3, 4):
                K.stt('dve', c, c[:, 2:2 + W], xp, xp[:, j:j + W], cw[:, cc, j:j + 1], c, c[:, 2:2 + W],
                      ALU.mult, ALU.add, extra_reads=[cw])
            if fc < 4:
                ob, o0, o1 = xcT, xcT[:, fc, 0:256], xcT[:, fc, 256:T]
            elif fc == 4:
                ob, o0, o1 = BT, BT[:, 0:256], BT[:, 256:T]
            else:
                ob, o0, o1 = CT, CT[:, 0:256], CT[:, 256:T]
            K.act(ob, o0, c, c[:, 2:258], AF.Silu)
            K.act(ob, o1, c, c[:, 262:2310], AF.Silu)
        for tt in range(NT):
            p = pst[tt % 2]
            for fc in range(4):
                K.tr(p, p[:, fc * 128:(fc + 1) * 128], xcT, xcT[:, fc, tt * 128:(tt + 1) * 128], cstb, IDB)
            K.copy('dve' if tt % 2 else 'act', xs_tm, xs_tm[:, tt, :], p, p[:])
        for t4 in range(0, NT, 4):
            nt = min(4, NT - t4)
            p = pst[(t4 // 4) % 2]
            for j in range(nt):
                K.tr(p, p[:, j * 128:(j + 1) * 128], BT, BT[:, (t4 + j) * 128:(t4 + j + 1) * 128], cstb, IDB)
            K.copy('dve', B_tm, B_tm[:, t4:t4 + nt, :], p,
                   p[:, 0:nt * 128].rearrange("p (n c) -> p n c", c=128))

    yacc = K.tile('yacc', [128, NT, 512], F32)
    with K.scope('ssm_sweep'):
        S = K.tile('S', [128, 512], F32)
        Sb = K.tile('Sb', [128, 512], BF16)
        Xb = [K.tile('X%d' % i, [128, 8, 128], F32) for i in range(2)]
        Eb = [K.tile('E%d' % i, [128, 8, 128], F32) for i in range(2)]
        MTb = [K.tile('MT%d' % i, [128, 8, 128], BF16) for i in range(2)]
        Gmb = [K.tile('Gm%d' % i, [128, 128], F32) for i in range(2)]
        xdtb = [K.tile('xdt%d' % i, [128, 8, 64], BF16) for i in range(2)]
        xwb = [K.tile('xw%d' % i, [128, 8, 64], BF16) for i in range(2)]
        smb = [K.tile('sm%d' % i, [128, 16], F32) for i in range(2)]
        tmpb = [K.tile('yt%d' % i, [128, 512], F32) for i in range(2)]
        pG = K.psum('pG', [128, 128], F32)
        pD = [K.psum('pD%d' % i, [128, 512], F32) for i in range(2)]
        pY = K.psum('pY', [128, 512], F32)
        pYo = K.psum('pYo', [128, 512], F32)
        pS = K.psum('pS', [128, 512], F32)
        pc = K.psum('pc', [128, 16], F32)
        n = 0
        for d in range(2):
            Ma, Mb, Mg = (MLE, MGT, MLE) if d == 0 else (MGE, MLT, MGE)
            endcol = 127 if d == 0 else 0
            order = list(range(NT)) if d == 0 else [1, 0] + list(range(NT - 1, 1, -1))
            hs = slice(d * 32 + g * 8, d * 32 + g * 8 + 8)
            K.memset('dve', S, S[:], 0.0)
            K.memset('dve', Sb, Sb[:], 0.0)
            def stage1(c, i):
                tok = slice(c * 128, (c + 1) * 128)
                X, E, MT, Gm, xdt, xw, sm = Xb[i], Eb[i], MTb[i], Gmb[i], xdtb[i], xwb[i], smb[i]
                K.mm(pG, pG[:], BT, BT[:, tok], CT, CT[:, tok], True, True)
                K.tt('pool', X, X[:], adt, bc(adt[:, c, hs, None], [128, 8, 128]),
                     cst, bc(Ma[:, None, :], [128, 8, 128]), ALU.mult)
                K.tt('dve', Gm, Gm[:], pG, pG[:], cst, Mg, ALU.mult)
                K.mm(pc, pc[:, 0:8], cst, Ma, adt, adt[:, c, hs], True, True)
                K.mm(pc, pc[:, 8:16], cst, ONESF, adt, adt[:, c, hs], True, True)
                K.act(sm, sm[:], pc, pc[:], AF.Exp)
                K.tt('pool', xdt, xdt[:], xs_tm, xs_tm[:, c, :].rearrange("p (h q) -> p h q", q=64),
                     dt, bc(dt[:, c, hs, None], [128, 8, 64]), ALU.mult)

            def stage1b(c, i):
                X, E, MT, Gm, xdt, xw, sm = Xb[i], Eb[i], MTb[i], Gmb[i], xdtb[i], xwb[i], smb[i]
                for hh in range(2):
                    K.mm(pD[hh], pD[hh][:], cst, Mb, X,
                         X[:, hh * 4:(hh + 1) * 4, :].rearrange("p h l -> p (h l)"), True, True)
                    K.act(E, E[:, hh * 4:(hh + 1) * 4, :].rearrange("p h l -> p (h l)"), pD[hh], pD[hh][:], AF.Exp)
                K.tt('dve', MT, MT[:], E, E[:], Gm, bc(Gm[:, None, :], [128, 8, 128]), ALU.mult)
                K.tt('pool', xw, xw[:], xdt, xdt[:], E, bc(E[:, :, endcol:endcol + 1], [128, 8, 64]), ALU.mult)

            def stage2(c, i):
                tok = slice(c * 128, (c + 1) * 128)
                MT, xdt, xw, sm, ytmp = MTb[i], xdtb[i], xwb[i], smb[i], tmpb[i]
                K.mm(pYo, pYo[:], CT, CT[:, tok], Sb, Sb[:], True, True)
                K.mm(pS, pS[:], B_tm, B_tm[:, c, :], xw, xw[:].rearrange("p h q -> p (h q)"), True, True)
                for h in range(8):
                    K.mm(pY, pY[:, h * 64:(h + 1) * 64], MT, MT[:, h, :], xdt, xdt[:, h, :], True, True)
                K.tt('dve', S, S[:].rearrange("p (h q) -> p h q", q=64),
                     S, S[:].rearrange("p (h q) -> p h q", q=64),
                     sm, bc(sm[:, 8:16, None], [128, 8, 64]), ALU.mult)
                K.tt('dve', S, S[:], S, S[:], pS, pS[:], ALU.add)
                K.copy('act', Sb, Sb[:], S, S[:])
                K.tt('dve', ytmp, ytmp[:].rearrange("p (h q) -> p h q", q=64),
                     pYo, pYo[:].rearrange("p (h q) -> p h q", q=64),
                     sm, bc(sm[:, 0:8, None], [128, 8, 64]), ALU.mult)
                if d == 0:
                    K.tt('dve', yacc, yacc[:, c, :], ytmp, ytmp[:], pY, pY[:], ALU.add)
                else:
                    K.tt('dve', ytmp, ytmp[:], ytmp, ytmp[:], pY, pY[:], ALU.add)
                    K.tt('pool', yacc, yacc[:, c, :], yacc, yacc[:, c, :], ytmp, ytmp[:], ALU.add)

            for t in range(len(order) + 1):
                if t < len(order):
                    stage1(order[t], t % 2)
                if t >= 1:
                    stage2(order[t - 1], (t - 1) % 2)
                if t < len(order):
                    stage1b(order[t], t % 2)

    with K.scope('ssm_post'):
        Wz = K.tile('Wz', [128, KC, 512], BF16, dma=True)
        K.load('pool', Wz, Wz[:], WD, wsl(O_Z + g * 512, O_Z + (g + 1) * 512))
        ysT = K.tile('ysT', [128, 4, T], BF16, dma=True)
        pz = [K.psum('pz%d' % i, [128, 512], F32) for i in range(2)]
        pt = [K.psum('pt%d' % i, [128, 512], BF16) for i in range(2)]
        zs = [K.tile('zs%d' % i, [128, 512], F32) for i in range(2)]
        yb = [K.tile('yb%d' % i, [128, 512], F32) for i in range(2)]
        ynb = [K.tile('yn%d' % i, [128, 512], BF16) for i in range(2)]
        jk = [K.tile('jk%d' % i, [128, 512], F32) for i in range(2)]
        ssb = [K.tile('ss%d' % i, [128, 2], F32) for i in range(2)]
        for tt in range(NT):
            i = tt % 2
            p, z, y, yn, ss = pz[i], zs[i], yb[i], ynb[i], ssb[i]
            proj_tm(K, p, hT, tt, Wz, 0, 512)
            K.act(z, z[:], p, p[:], AF.Silu)
            K.tt('pool', y, y[:].rearrange("p (h q) -> p h q", q=64),
                 xs_tm, xs_tm[:, tt, :].rearrange("p (h q) -> p h q", q=64),
                 prm, bc(prm[:, 2, g * 8:(g + 1) * 8, None], [128, 8, 64]), ALU.mult)
            K.tt('dve', y, y[:], y, y[:], yacc, yacc[:, tt, :], ALU.add)
            K.tt('dve', y, y[:], y, y[:], z, z[:], ALU.mult)
            K.act(jk[i], jk[i][:], y, y[:], AF.Square, accum=ss[:, 0:1], accum_b=ss)
            K.ts('dve', ss, ss[:, 1:2], ss, ss[:, 0:1], 1.0 / 512, EPS, ALU.mult, ALU.add)
            K.rsqrt(ss, ss[:, 1:2])
            K.stt('dve', yn, yn[:], y, y[:], ss[:, 1:2], nw, nw[:, g * 512:(g + 1) * 512],
                  ALU.mult, ALU.mult, extra_reads=[ss])
            q = pt[i]
            for fc in range(4):
                K.tr(q, q[:, fc * 128:(fc + 1) * 128], yn, yn[:, fc * 128:(fc + 1) * 128], cstb, IDB)
            K.copy('act', ysT, ysT[:, :, tt * 128:(tt + 1) * 128], q,
                   q[:].rearrange("p (c t) -> p c t", t=128))
        K.store('sp', YBR, ybr_d[g * 512:(g + 1) * 512, :].rearrange("(c p) t -> p c t", p=128),
                ysT, ysT[:])


def rope(K, e, dst, dst_ap, src, src_ap, tab, cos_ap, sin_ap, tmp, tmp_ap, nh, hd):
    q = hd // 4
    K.tt(e, dst, dst_ap, src, src_ap, tab, bc(cos_ap[:, None, :], [128, nh, hd]), ALU.mult)
    for a in range(2):
        for s in range(2):
            o0 = a * 2 * q + s * q
            i0 = a * 2 * q + (1 - s) * q
            K.tt(e, tmp, tmp_ap[:, :, o0:o0 + q], src, src_ap[:, :, i0:i0 + q],
                 tab, bc(sin_ap[:, None, o0:o0 + q], [128, nh, q]), ALU.mult)
    K.tt(e, dst, dst_ap, dst, dst_ap, tmp, tmp_ap, ALU.add)


def attention_core(K, G, groups, scale, kT, qT, v_tm, pfx):
    cstb, ONESB = G['cstb'], G['ONESB']
    pS = [K.psum(pfx + 'S%d' % i, [128, 512], F32) for i in range(3)]
    pO = [K.psum(pfx + 'O%d' % i, [128, 512], F32) for i in range(2)]
    pZ = [K.psum(pfx + 'Z%d' % i, [128, 512], F32) for i in range(2)]
    PT = [K.tile(pfx + 'PT%d' % i, [128, 512], BF16) for i in range(3)]
    items = []
    for gi, g in enumerate(groups):
        for kt in range(g['k0'], g['k1']):
            items.append((gi, g, kt))
    LA = 2
    deferred = []
    for idx in range(len(items) + LA):
        while deferred and deferred[0][0] <= idx:
            deferred.pop(0)[1]()
        if idx < len(items):
            gi, g, kt = items[idx]
            nq = g['nq']
            s_, pt = pS[idx % 3], PT[idx % 3]
            K.mm(s_, s_[:, 0:nq], kT, g['k'](kt), qT, g['q'], True, True)
            K.act(pt, pt[:, 0:nq], s_, s_[:, 0:nq], AF.Exp, scale=scale)
        j = idx - LA
        if j >= 0:
            gi, g, kt = items[j]
            nq = g['nq']
            pt = PT[j % 3]
            o, z = pO[gi % 2], pZ[gi % 2]
            K.mm(o, o[:, 0:nq], v_tm, g['v'](kt), pt, pt[:, 0:nq], kt == g['k0'], kt == g['k1'] - 1)
            K.mm(z, z[:, 0:nq], cstb, ONESB, pt, pt[:, 0:nq], kt == g['k0'], kt == g['k1'] - 1)
            if kt == g['k1'] - 1:
                cont = g['fin'](o, z)
                if cont is not None:
                    deferred.append((idx + 6, cont))
    for d in deferred:
        d[1]()


QBLOCKS = [(0, 256, 0, 2)] + [(256 + 512 * i, 256 + 512 * (i + 1), 0, NT) for i in range(4)]


def phase_gqa(K, G, hT, wsl):
    L = G['L']
    WD, YBR, ybr_d = G['WD'], G['YBR'], G['ybr_d']
    cst, cstb, IDB = G['cst'], G['cstb'], G['IDB']
    with K.scope():
        ropeG = K.tile('ropeG', [128, 16, 2, 128], F32, dma=True)
        K.load('sp', ropeG, ropeG[:], WD, G['ropeG_in'])
        qT = K.tile('qT', [128, 8, T], BF16)
        kT = K.tile('kT', [128, 2, T], BF16)
        v_tm = K.tile('v_tm', [128, NT, 256], BF16)
        gn = K.tile('gn', [128, 2, 128], F32, dma=True)
        K.load('sp', gn, gn[:, 0, :], WD, G['gq_norm'][L:L + 1, :].broadcast_to([128, 128]))
        K.load('sp', gn, gn[:, 1, :], WD, G['gk_norm'][L:L + 1, :].broadcast_to([128, 128]), part=True)
        with K.scope('gqa_proj'):
            W = K.tile('Wg', [128, KC, 1536], BF16, dma=True)
            K.load('pool', W, W[:, :, 0:512], WD, wsl(O_GQ, O_GQ + 512))
            K.load('pool', W, W[:, :, 512:1024], WD, wsl(O_GQ + 512, O_GQ + 1024), part=True)
            K.load('pool', W, W[:, :, 1024:1536], WD, wsl(O_GK, O_GK + 512), part=True)
            pq = [K.psum('pq%d' % i, [128, 512], F32) for i in range(3)]
            ptr = [K.psum('ptr%d' % i, [128, 512], BF16) for i in range(2)]
            qf = [K.tile('qf%d' % i, [128, 10, 128], F32) for i in range(2)]
            sq = [K.tile('sq%d' % i, [128, 10, 128], F32) for i in range(2)]
            qr = [K.tile('qr%d' % i, [128, 10, 128], F32) for i in range(2)]
            qb_ = [K.tile('qb%d' % i, [128, 10, 128], BF16) for i in range(2)]
            ssq = [K.tile('ssq%d' % i, [128, 10], F32) for i in range(2)]
            for tt in range(NT):
                i = tt % 2
                f, s2, r, qb16, ss = qf[i], sq[i], qr[i], qb_[i], ssq[i]
                for j in range(3):
                    proj_tm(K, pq[j], hT, tt, W, j * 512, 512)
                K.copy('act', f, f[:, 0:4, :], pq[0], pq[0][:].rearrange("p (h d) -> p h d", d=128))
                K.copy('act', f, f[:, 4:8, :], pq[1], pq[1][:].rearrange("p (h d) -> p h d", d=128))
                K.copy('act', f, f[:, 8:10, :], pq[2], pq[2][:, 0:256].rearrange("p (h d) -> p h d", d=128))
                K.copy('act', v_tm, v_tm[:, tt, :], pq[2], pq[2][:, 256:512])
                K.tt('pool', s2, s2[:], f, f[:], f, f[:], ALU.mult)
                K.op('dve', lambda e: e.reduce_sum(out=ss[:], in_=s2[:], axis=AX.X), reads=[s2], writes=[ss])
                K.ts('dve', ss, ss[:], ss, ss[:], 1.0 / 128, EPS, ALU.mult, ALU.add)
                K.rsqrt(ss, ss[:])
                K.tt('dve', f, f[:], f, f[:], ss, bc(ss[:, :, None], [128, 10, 128]), ALU.mult)
                K.tt('dve', f, f[:, 0:8, :], f, f[:, 0:8, :], gn, bc(gn[:, 0:1, :], [128, 8, 128]), ALU.mult)
                K.tt('dve', f, f[:, 8:10, :], f, f[:, 8:10, :], gn, bc(gn[:, 1:2, :], [128, 2, 128]), ALU.mult)
                if tt >= 2:
                    rope(K, 'pool', r, r[:], f, f[:], ropeG, ropeG[:, tt - 2, 0, :], ropeG[:, tt - 2, 1, :],
                         s2, s2[:], 10, 128)
                    K.copy('dve', qb16, qb16[:], r, r[:])
                else:
                    K.copy('dve', qb16, qb16[:], f, f[:])
                for half in range(3):
                    hh = [(0, 4), (4, 8), (8, 10)][half]
                    p = ptr[(tt * 3 + half) % 2]
                    for h in range(hh[0], hh[1]):
                        K.tr(p, p[:, (h - hh[0]) * 128:(h - hh[0] + 1) * 128], qb16, qb16[:, h, :], cstb, IDB)
                    nh = hh[1] - hh[0]
                    src = p[:, 0:nh * 128].rearrange("p (h t) -> p h t", t=128)
                    if half < 2:
                        K.copy('act', qT, qT[:, hh[0]:hh[1], tt * 128:(tt + 1) * 128], p, src)
                    else:
                        K.copy('act', kT, kT[:, :, tt * 128:(tt + 1) * 128], p, src)
        with K.scope('gqa_attn'):
            og = [K.tile('og%d' % i, [128, 512], BF16, dma=True) for i in range(3)]
            rz = [K.tile('rz%d' % i, [128, 512], F32) for i in range(2)]
            cnt = [0]

            def mkfin(m, q0, q1):
                def fin(o, z):
                    nq = q1 - q0
                    i = cnt[0]
                    cnt[0] += 1
                    r = rz[i % 2]
                    ob = og[i % 3]
                    K.op('dve', lambda e: e.reciprocal(out=r[:, 0:nq], in_=z[:, 0:nq]), reads=[z], writes=[r])
                    K.tt('dve', ob, ob[:, 0:nq], o, o[:, 0:nq], r, r[:, 0:nq], ALU.mult)
                    K.store('sp', YBR, ybr_d[(16 + m) * 128:(17 + m) * 128, q0:q1], ob, ob[:, 0:nq])
                return fin

            groups = []
            for m in range(8):
                for (q0, q1, k0, k1) in QBLOCKS:
                    groups.append(dict(
                        k=(lambda kt, m=m: kT[:, m // 4, kt * 128:(kt + 1) * 128]),
                        q=qT[:, m, q0:q1],
                        v=(lambda kt, m=m: v_tm[:, kt, (m // 4) * 128:(m // 4 + 1) * 128]),
                        nq=q1 - q0, k0=k0, k1=k1, fin=mkfin(m, q0, q1)))
            attention_core(K, G, groups, 128 ** -0.5, kT, qT, v_tm, 'a')


def phase_diff(K, G, hT, wsl, hh):
    L, lam_init = G['L'], G['lam_init']
    WD, YBR, ybr_d = G['WD'], G['YBR'], G['ybr_d']
    cst, cstb, IDB, ONESF = G['cst'], G['cstb'], G['IDB'], G['ONESF']
    with K.scope():
        ropeD = K.tile('ropeD', [128, 16, 2, 64], F32, dma=True)
        K.load('sp', ropeD, ropeD[:], WD, G['ropeD_in'])
        qT = K.tile('dqT', [128, 4, T], BF16)
        kT = K.tile('dkT', [128, 4, T], BF16)
        v_tm = K.tile('dv_tm', [128, NT, 512], BF16)
        lm = K.tile('lm', [128, 4, 64], F32, dma=True)
        K.load('sp', lm, lm[:].rearrange("p a b -> p (a b)"), WD, G['dlam'][L:L + 1, :].broadcast_to([128, 256]))
        lt = K.tile('lt', [128, 2, 64], F32)
        ls = K.tile('ls', [128, 4], F32)
        K.tt('dve', lt, lt[:, 0, :], lm, lm[:, 0, :], lm, lm[:, 1, :], ALU.mult)
        K.tt('dve', lt, lt[:, 1, :], lm, lm[:, 2, :], lm, lm[:, 3, :], ALU.mult)
        K.op('dve', lambda e: e.reduce_sum(out=ls[:, 0:2], in_=lt[:], axis=AX.X), reads=[lt], writes=[ls])
        K.act(ls, ls[:, 0:2], ls, ls[:, 0:2], AF.Exp)
        K.tt('dve', ls, ls[:, 2:3], ls, ls[:, 0:1], ls, ls[:, 1:2], ALU.subtract)
        K.ts('dve', ls, ls[:, 3:4], ls, ls[:, 2:3], lam_init, -1.0, ALU.add, ALU.mult)
        dnw = K.tile('dnw', [128, 1], F32, dma=True)
        K.load('sp', dnw, dnw[:], WD, G['dnw_fm'][L])
        K.ts('dve', dnw, dnw[:], dnw, dnw[:], 1.0 - lam_init, None, ALU.mult)
        with K.scope('diff_proj'):
            Wb = [K.tile('Wd%d' % i, [128, KC, 512], BF16, dma=True) for i in range(3)]
            for j, c0 in enumerate((O_DQ, O_DK, O_DV)):
                K.load('pool', Wb[j], Wb[j][:], WD, wsl(c0 + hh * 512, c0 + (hh + 1) * 512))
            pq = [K.psum('dpq%d' % i, [128, 512], F32) for i in range(3)]
            ptr = [K.psum('dptr%d' % i, [128, 512], BF16) for i in range(2)]
            qf = [K.tile('dqf%d' % i, [128, 8, 64], F32) for i in range(2)]
            tm = [K.tile('dtm%d' % i, [128, 8, 64], F32) for i in range(2)]
            qr = [K.tile('dqr%d' % i, [128, 8, 64], F32) for i in range(2)]
            q16 = [K.tile('dq16%d' % i, [128, 512], BF16) for i in range(2)]
            n = 0
            for tt in range(NT):
                for j in range(3):
                    p = pq[n % 3]
                    i = n % 2
                    n += 1
                    proj_tm(K, p, hT, tt, Wb[j], 0, 512)
                    if j == 2:
                        K.copy('act', v_tm, v_tm[:, tt, :], p, p[:])
                        continue
                    f, t_, r, b16 = qf[i], tm[i], qr[i], q16[i]
                    if tt >= 2:
                        K.copy('act', f, f[:].rearrange("p h d -> p (h d)"), p, p[:])
                        rope(K, 'pool' if j % 2 else 'dve', r, r[:], f, f[:], ropeD, ropeD[:, tt - 2, 0, :],
                             ropeD[:, tt - 2, 1, :], t_, t_[:], 8, 64)
                        K.copy('act', b16, b16[:], r, r[:].rearrange("p h d -> p (h d)"))
                    else:
                        K.copy('act', b16, b16[:], p, p[:])
                    q = ptr[i]
                    for c in range(4):
                        K.tr(q, q[:, c * 128:(c + 1) * 128], b16, b16[:, c * 128:(c + 1) * 128], cstb, IDB)
                    dst = qT if j == 0 else kT
                    K.copy('dve', dst, dst[:, :, tt * 128:(tt + 1) * 128], q,
                           q[:].rearrange("p (c t) -> p c t", t=128))
        with K.scope('diff_attn'):
            o1 = [K.tile('o1_%d' % i, [128, 512], F32) for i in range(2)]
            o2 = [K.tile('o2_%d' % i, [128, 512], F32) for i in range(2)]
            osq = [K.tile('osq%d' % i, [128, 512], F32) for i in range(2)]
            rs = [K.tile('rs%d' % i, [128, 512], F32) for i in range(2)]
            od = [K.tile('od%d' % i, [128, 512], BF16, dma=True) for i in range(3)]
            rz = [K.tile('drz%d' % i, [128, 512], F32) for i in range(2)]
            pN = K.psum('pN', [128, 512], F32)
            cnt = [0]

            def mkfin(h, j, q0, q1):
                def fin(o, z):
                    nq = q1 - q0
                    i = cnt[0] % 2
                    r = rz[j]
                    K.op('dve', lambda e: e.reciprocal(out=r[:, 0:nq], in_=z[:, 0:nq]), reads=[z], writes=[r])
                    if j == 0:
                        K.tt('dve', o1[i], o1[i][:, 0:nq], o, o[:, 0:nq], r, r[:, 0:nq], ALU.mult)
                        return
                    K.tt('dve', o2[i], o2[i][:, 0:nq], o, o[:, 0:nq], r, r[:, 0:nq], ALU.mult)
                    K.stt('dve', o1[i], o1[i][:, 0:nq], o2[i], o2[i][:, 0:nq], ls[:, 3:4], o1[i], o1[i][:, 0:nq],
                          ALU.mult, ALU.add, extra_reads=[ls])
                    K.tt('pool', osq[i], osq[i][:, 0:nq], o1[i], o1[i][:, 0:nq], o1[i], o1[i][:, 0:nq], ALU.mult)
                    k = cnt[0] % 3
                    cnt[0] += 1

                    def cont():
                        K.mm(pN, pN[:, 0:nq], cst, ONESF, osq[i], osq[i][:, 0:nq], True, True)
                        K.ts('dve', rs[i], rs[i][:, 0:nq], pN, pN[:, 0:nq], 1.0 / 128, EPS, ALU.mult, ALU.add)
                        K.rsqrt(rs[i], rs[i][:, 0:nq])
                        K.stt('dve', od[k], od[k][:, 0:nq], o1[i], o1[i][:, 0:nq], dnw[:, 0:1], rs[i], rs[i][:, 0:nq],
                              ALU.mult, ALU.mult, extra_reads=[dnw])
                        hg = hh * 4 + h
                        K.store('sp', YBR, ybr_d[(24 + hg) * 128:(25 + hg) * 128, q0:q1], od[k], od[k][:, 0:nq])
                    return cont
                return fin

            groups = []
            for h in range(4):
                for (q0, q1, k0, k1) in QBLOCKS:
                    for j in range(2):
                        ps_ = slice(j * 64, (j + 1) * 64)
                        groups.append(dict(
                            k=(lambda kt, h=h, ps_=ps_: kT[ps_, h, kt * 128:(kt + 1) * 128]),
                            q=qT[ps_, h, q0:q1],
                            v=(lambda kt, h=h: v_tm[:, kt, h * 128:(h + 1) * 128]),
                            nq=q1 - q0, k0=k0, k1=k1, fin=mkfin(h, j, q0, q1)))
            attention_core(K, G, groups, 64 ** -0.5, kT, qT, v_tm, 'b')


def layernorm_tile(K, e, xn, xn_ap, src, src_ap, rows, gi, bi, st, mv):
    for c in range(2):
        K.op('dve', lambda en: en.bn_stats(out=st[:, c, :], in_=src_ap[:, c * 512:(c + 1) * 512]),
             reads=[src], writes=[st])
    K.op('dve', lambda en: en.bn_aggr(out=mv[:, 0:2], in_=st[:]), reads=[st], writes=[mv])
    K.ts('dve', mv, mv[:, 2:3], mv, mv[:, 1:2], EPS, None, ALU.add)
    K.rsqrt(mv, mv[:, 2:3])
    K.ts('dve', xn, xn_ap, src, src_ap, mv[:, 0:1], mv[:, 2:3], ALU.subtract, ALU.mult, extra_reads=[mv])
    K.tt(e, xn, xn_ap, xn, xn_ap, rows, rows[:, gi, :], ALU.mult)
    K.tt(e, xn, xn_ap, xn, xn_ap, rows, rows[:, bi, :], ALU.add)


def phase_merge(K, G, hT, wsl, th):
    L, b = G['L'], G['b']
    WD, YBR, ybr_d, XB, xB_d = G['WD'], G['YBR'], G['ybr_d'], G['XB'], G['xB_d']
    cst, cstb, IDB = G['cst'], G['cstb'], G['IDB']
    xsrc = G['xsrc']
    HT = NT // 2
    tiles = list(range(th * HT, (th + 1) * HT))
    with K.scope():
        macc = K.tile('macc', [128, HT, D], F32)
        bg = K.tile('bg', [128, 3072], F32, dma=True)
        K.load('sp', bg, bg[:], WD, G['b_gate'][L:L + 1, :].broadcast_to([128, 3072]))
        branches = [(G['w_sso'], 16, 0), (G['w_gqo'], 8, 16), (G['w_dfo'], 8, 24)]
        for br, (wout, nck, c0) in enumerate(branches):
            with K.scope('merge_br'):
                Wb = K.tile('Wbr', [128, nck, D], BF16, dma=True)
                for c4 in range(0, nck, 4):
                    K.load('pool', Wb, Wb[:, c4:c4 + 4, :], WD,
                           wout[L].rearrange("(c p) f -> p c f", p=128)[:, c4:c4 + 4, :], part=(c4 > 0))
                Wg = K.tile('Wgt', [128, KC, D], BF16, dma=True)
                K.load('pool', Wg, Wg[:, :, 0:512], WD, wsl(O_GATE + br * D, O_GATE + br * D + 512))
                K.load('pool', Wg, Wg[:, :, 512:D], WD, wsl(O_GATE + br * D + 512, O_GATE + (br + 1) * D), part=True)
                yt = [K.tile('ybt%d' % i, [128, nck, 128], BF16, dma=True) for i in range(3)]
                pg = [K.psum('pg%d' % i, [128, 512], F32) for i in range(2)]
                pp = [K.psum('pp%d' % i, [128, 512], F32) for i in range(2)]
                gt = [K.tile('gt%d' % i, [128, 512], F32) for i in range(2)]
                n = 0
                for ti, tt in enumerate(tiles):
                    y = yt[ti % 3]
                    K.load('sp', y, y[:], YBR,
                           ybr_d[c0 * 128:(c0 + nck) * 128, tt * 128:(tt + 1) * 128].rearrange("(c p) t -> p c t", p=128))
                    for hf in range(2):
                        i = n % 2
                        n += 1
                        g_, p_, gg = pg[i], pp[i], gt[i]
                        cs = slice(hf * 512, (hf + 1) * 512)
                        proj_tm(K, g_, hT, tt, Wg, hf * 512, 512)
                        K.tt('dve', gg, gg[:], g_, g_[:], bg, bg[:, br * D + hf * 512:br * D + (hf + 1) * 512], ALU.add)
                        K.act(gg, gg[:], gg, gg[:], AF.Sigmoid)
                        for c in range(nck):
                            K.mm(p_, p_[:], y, y[:, c, :], Wb, Wb[:, c, cs], c == 0, c == nck - 1)
                        if br == 0:
                            K.tt('dve', macc, macc[:, ti, cs], p_, p_[:], gg, gg[:], ALU.mult)
                        else:
                            K.tt('dve', gg, gg[:], p_, p_[:], gg, gg[:], ALU.mult)
                            K.tt('pool', macc, macc[:, ti, cs], macc, macc[:, ti, cs], gg, gg[:], ALU.add)
        with K.scope('merge_wo'):
            rows = load_rows(K, G, 2, G['ln1_g'], G['ln1_b'])
            Wo = K.tile('Wo', [128, KC, D], BF16, dma=True)
            for c4 in range(0, KC, 4):
                K.load('pool', Wo, Wo[:, c4:c4 + 4, :], WD,
                       G['w_o'][L].rearrange("(c p) f -> p c f", p=128)[:, c4:c4 + 4, :], part=(c4 > 0))
            mb = [K.tile('mb%d' % i, [128, D], BF16) for i in range(2)]
            mT = [K.tile('mT%d' % i, [128, KC, 128], BF16) for i in range(2)]
            xt = [K.tile('xt%d' % i, [128, D], F32, dma=True) for i in range(2)]
            xn = [K.tile('xn%d' % i, [128, D], F32, dma=True) for i in range(2)]
            st = [K.tile('st%d' % i, [128, 2, 6], F32) for i in range(2)]
            mv = [K.tile('mv%d' % i, [128, 4], F32) for i in range(2)]
            ptr = [K.psum('mptr%d' % i, [128, 1024], BF16) for i in range(2)]
            py = [K.psum('mpy%d' % i, [128, 512], F32) for i in range(4)]
            for ti, tt in enumerate(tiles):
                i = ti % 2
                isc = 1 if tt < 2 else 0
                x_ = xt[i]
                db, ap = xsrc(tt, 1)
                K.load('sp', x_, x_[:], db, ap)
                K.copy('act', mb[i], mb[i][:], macc, macc[:, ti, :])
                q = ptr[i]
                for c in range(KC):
                    K.tr(q, q[:, c * 128:(c + 1) * 128], mb[i], mb[i][:, c * 128:(c + 1) * 128], cstb, IDB)
                K.copy('act', mT[i], mT[i][:].rearrange("p c t -> p (c t)"), q, q[:])
                xo = xn[i]
                for hf in range(2):
                    p_ = py[(ti * 2 + hf) % 4]
                    cs = slice(hf * 512, (hf + 1) * 512)
                    for c in range(KC):
                        K.mm(p_, p_[:], mT[i], mT[i][:, c, :], Wo, Wo[:, c, cs], c == 0, c == KC - 1)
                    K.tt('dve', xo, xo[:, cs], p_, p_[:], rows, rows[:, isc, cs], ALU.mult)
                K.stt('dve', xo, xo[:], x_, x_[:], ALPHA, xo, xo[:], ALU.mult, ALU.add)
                layernorm_tile(K, 'pool', xo, xo[:], xo, xo[:], rows, 2, 3, st[i], mv[i])
                K.store('sp', XB, xB_d[tt * 128:(tt + 1) * 128, :], xo, xo[:])


def phase_ffn(K, G, hT):
    L, b = G['L'], G['b']
    WD, XB, xB_d, XA, xA_d, OUT, out = G['WD'], G['XB'], G['xB_d'], G['XA'], G['xA_d'], G['OUT'], G['out']
    NF = FFH // 128
    HTOK = T // 2
    with K.scope():
        rows = load_rows(K, G, 5, G['ln2_g'], G['ln2_b'])
        Wo = K.tile('Wfo', [128, NF, D], BF16, dma=True)
        wvo = G['ffn_wo'][L].rearrange("(c p) f -> p c f", p=128)
        for c0 in range(0, NF, 4):
            c1 = min(NF, c0 + 4)
            K.load('pool', Wo, Wo[:, c0:c1, :], WD, wvo[:, c0:c1, :], part=(c0 > 0))
        uT = K.tile('uT', [128, NF, HTOK], BF16)
        for th in range(2):
            base = th * HTOK
            blocks = [(0, 256), (256, 768), (768, 1152)] if th == 0 else [(0, 512), (512, 1024), (1024, 1152)]
            with K.scope('ffn_in'):
                Wa = [K.tile('Wa%d' % i, [128, KC, 256], BF16, dma=True) for i in range(3)]
                pa = [K.psum('pa%d' % i, [128, 512], F32) for i in range(3)]
                pb = [K.psum('pb%d' % i, [128, 512], F32) for i in range(3)]
                sa = [K.tile('sa%d' % i, [128, 512], F32) for i in range(2)]
                wv = G['ffn_wi'][L].rearrange("(kc p) f -> p kc f", p=128)
                n = 0
                for fc in range(NF):
                    W = Wa[fc % 3]
                    K.load('pool', W, W[:, :, 0:128], WD, wv[:, :, fc * 128:(fc + 1) * 128])
                    K.load('pool', W, W[:, :, 128:256], WD, wv[:, :, FFH + fc * 128:FFH + (fc + 1) * 128], part=True)
                    for (a0, a1) in blocks:
                        na = a1 - a0
                        A, B_, s_ = pa[n % 3], pb[n % 3], sa[n % 2]
                        n += 1
                        for kc in range(KC):
                            K.mm(A, A[:, 0:na], W, W[:, kc, 0:128], hT, hT[:, kc, base + a0:base + a1],
                                 kc == 0, kc == KC - 1)
                        for kc in range(KC):
                            K.mm(B_, B_[:, 0:na], W, W[:, kc, 128:256], hT, hT[:, kc, base + a0:base + a1],
                                 kc == 0, kc == KC - 1)
                        K.act(s_, s_[:, 0:na], A, A[:, 0:na], AF.Silu)
                        K.tt('dve', uT, uT[:, fc, a0:a1], s_, s_[:, 0:na], B_, B_[:, 0:na], ALU.mult)
            with K.scope('ffn_out'):
                xt = [K.tile('fxt%d' % i, [128, D], F32, dma=True) for i in range(2)]
                xn = [K.tile('fxn%d' % i, [128, D], F32, dma=True) for i in range(2)]
                st = [K.tile('fst%d' % i, [128, 2, 6], F32) for i in range(2)]
                mv = [K.tile('fmv%d' % i, [128, 4], F32) for i in range(2)]
                py = [K.psum('fpy%d' % i, [128, 512], F32) for i in range(4)]
                for ti in range(NT // 2):
                    tt = th * (NT // 2) + ti
                    i = ti % 2
                    isc = 1 if tt < 2 else 0
                    x_ = xt[i]
                    K.load('sp', x_, x_[:], XB, xB_d[tt * 128:(tt + 1) * 128, :])
                    xo = xn[i]
                    for hf in range(2):
                        p_ = py[(ti * 2 + hf) % 4]
                        cs = slice(hf * 512, (hf + 1) * 512)
                        for c in range(NF):
                            K.mm(p_, p_[:], uT, uT[:, c, ti * 128:(ti + 1) * 128], Wo, Wo[:, c, cs],
                                 c == 0, c == NF - 1)
                        K.tt('dve', xo, xo[:, cs], p_, p_[:], rows, rows[:, isc, cs], ALU.mult)
                    K.stt('dve', xo, xo[:], x_, x_[:], ALPHA, xo, xo[:], ALU.mult, ALU.add)
                    layernorm_tile(K, 'pool', xo, xo[:], xo, xo[:], rows, 2, 3, st[i], mv[i])
                    if G['L'] == G['depth_run'] - 1:
                        if tt >= 2:
                            K.store('sp', OUT, out[b, (tt - 2) * 128:(tt - 1) * 128, :], xo, xo[:])
                    else:
                        K.store('sp', XA, xA_d[tt * 128:(tt + 1) * 128, :], xo, xo[:])


def host_consts():
    p = np.arange(128)[:, None]
    f = np.arange(128)[None, :]
    c = np.zeros((128, 6, 128), np.float32)
    c[:, 0] = (p == f)
    c[:, 1] = (p <= f)
    c[:, 2] = (p > f)
    c[:, 3] = (p >= f)
    c[:, 4] = (p < f)
    c[:, 5] = 1.0
    return c


def host_rope(hd):
    t = np.arange(LAT)
    pos_row = (t // 64).astype(np.float32)
    pos_col = (t % 64).astype(np.float32)
    d_axis = hd // 2
    inv = (10000.0 ** (-np.arange(0, d_axis, 2, dtype=np.float32) / d_axis)).astype(np.float32)
    ar = pos_row[:, None] * inv
    ac = pos_col[:, None] * inv
    ang = np.concatenate([ar, ar, ac, ac], axis=-1).astype(np.float32)
    cos = np.cos(ang).astype(np.float32)
    sin = np.sin(ang).astype(np.float32)
    q = hd // 4
    sign = np.concatenate([-np.ones(q), np.ones(q), -np.ones(q), np.ones(q)]).astype(np.float32)
    tab = np.stack([cos, sin * sign], axis=1)
    return np.ascontiguousarray(tab.reshape(16, 128, 2, hd).transpose(1, 0, 2, 3))


_CACHE = {}


def run(inputs, NB, depth_run, ncores=8, trace=False):
    key = (NB, depth_run)
    if key not in _CACHE:
        _CACHE[key] = build_program(NB, depth_run)
    nc = _CACHE[key]
    f = lambda k: np.ascontiguousarray(np.asarray(inputs[k], dtype=np.float32))
    shared = {
        "ada_w": f('ada_w'), "ada_b": f('ada_b'),
        "ada_b_fm": np.ascontiguousarray(f('ada_b').reshape(DEPTH, 48, 128).transpose(0, 2, 1)),
        "w_in": f('w_in'), "b_gate": f('b_gate'),
        "conv_w": np.ascontiguousarray(f('ssm_conv_w').reshape(DEPTH, 5, 24, 128).transpose(0, 3, 2, 1)),
        "conv_b": np.ascontiguousarray(f('ssm_conv_b').reshape(DEPTH, 24, 128).transpose(0, 2, 1)),
        "ssm_dt_bias": f('ssm_dt_bias').reshape(DEPTH, 64), "ssm_a_log": f('ssm_a_log').reshape(DEPTH, 64),
        "ssm_d": f('ssm_d'), "ssm_norm_w": f('ssm_norm_w'), "w_ssm_out": f('w_ssm_out'),
        "gqa_q_norm": f('gqa_q_norm'), "gqa_k_norm": f('gqa_k_norm'), "w_gqa_out": f('w_gqa_out'),
        "diff_lambda": f('diff_lambda').reshape(DEPTH, 256),
        "diff_norm_w": f('diff_norm_w').reshape(DEPTH, 128, 1),
        "w_diff_out": f('w_diff_out'), "w_o": f('w_o'), "ln1_g": f('ln1_g'), "ln1_b": f('ln1_b'),
        "ffn_w_in": f('ffn_w_in'), "ffn_w_out": f('ffn_w_out'), "ln2_g": f('ln2_g'), "ln2_b": f('ln2_b'),
        "consts": host_consts(), "ropeG": host_rope(128), "ropeD": host_rope(64),
    }
    x, c, ctx, c_ctx = f('x'), f('c'), f('ctx'), f('c_ctx')
    in_maps = []
    for i in range(ncores):
        sl = slice(i * NB, (i + 1) * NB)
        c5 = np.zeros((5, D), np.float32)
        c5[:NB] = c[sl]
        c5[4] = c_ctx
        cT = np.ascontiguousarray(c5.reshape(5, KC, 128).transpose(2, 1, 0))
        m = dict(shared)
        m.update({"x": x[sl], "ctx": ctx[sl], "cT": cT})
        in_maps.append(m)
    if trace:
        res = run_bass_kernel_spmd(nc, in_maps, core_ids=list(range(ncores)), trace=True)
        print("exec_time_ns", res.exec_time_ns)
    else:
        res = run_bass_kernel_spmd(nc, in_maps, core_ids=list(range(ncores)))
    return np.concatenate([r["out"] for r in res.results], axis=0)


def kernel(**inputs):
    return run(inputs, 4, DEPTH).astype(np.float32)
```

```python
import math
from contextlib import ExitStack, contextmanager

import numpy as np
import concourse.bass as bass
import concourse.mybir as mybir
from concourse.bass_utils import run_bass_kernel_spmd

F32 = mybir.dt.float32
BF16 = mybir.dt.bfloat16
AF = mybir.ActivationFunctionType
ALU = mybir.AluOpType
AX = mybir.AxisListType

D = 1024
KC = 8
T = 2304
NT = 18
LAT = 2048
CTX = 256
DEPTH = 4
ALPHA = (2 * DEPTH) ** 0.25
EPS = 1e-6
FFH = 2816
O_Z, O_XBC, O_DT, O_GQ, O_GK, O_GV, O_DQ, O_DK, O_DV, O_GATE = (
    0, 2048, 5120, 5184, 6208, 6464, 6720, 7744, 8768, 9792)
TB = [(0, 256), (256, 768), (768, 1280), (1280, 1792), (1792, 2304)]
NSEM = 40
ATTACH = True
MARKS = []


class Buf:
    def __init__(self, t, sem=None):
        self.t = t
        self.w = {}
        self.r = {}
        self.sem = sem

    def __getitem__(self, idx):
        return self.t[idx]


class DBuf:
    def __init__(self, ap):
        self.ap = ap
        self.w = {}
        self.r = {}


class Kx:
    def __init__(self, nc, es):
        self.nc = nc
        self.eng = {'pe': nc.tensor, 'act': nc.scalar, 'dve': nc.vector,
                    'pool': nc.gpsimd, 'sp': nc.sync}
        self.sem = {k: es.enter_context(nc.semaphore('sem_' + k)) for k in self.eng}
        self.cnt = {k: 0 for k in self.eng}
        self.seen = {k: {} for k in self.eng}
        self.sempool = [[es.enter_context(nc.semaphore('dsem%d' % i)), 'd%d' % i, 0]
                        for i in range(NSEM)]
        self.pending = {}
        self.dbufs = []
        self.stacks = [es]
        self.scope_sems = [[]]
        self.uid = 0

    def tile(self, name, shape, dtype, dma=False):
        self.uid += 1
        t = self.stacks[-1].enter_context(
            self.nc.sbuf_tensor('%s_%d' % (name, self.uid), list(shape), dtype))
        sem = None
        if dma:
            sem = self.sempool.pop()
            self.scope_sems[-1].append(sem)
        return Buf(t, sem)

    def psum(self, name, shape, dtype):
        self.uid += 1
        t = self.stacks[-1].enter_context(
            self.nc.psum_tensor('%s_%d' % (name, self.uid), list(shape), dtype))
        return Buf(t)

    def dbuf(self, ap):
        d = DBuf(ap)
        self.dbufs.append(d)
        return d

    @contextmanager
    def scope(self, name=None):
        es = ExitStack()
        self.stacks.append(es)
        self.scope_sems.append([])
        c0 = self.cnt['pe']
        try:
            yield
        finally:
            if name is not None:
                MARKS.append((name, c0, self.cnt['pe']))
            self.barrier()
            for s in self.scope_sems.pop():
                self.sempool.append(s)
            self.stacks.pop()
            es.close()

    def _need(self, e, deps):
        need = {}
        for (sem, key, val) in deps:
            if key == e and e == 'pe':
                continue
            if self.seen[e].get(key, 0) >= val:
                continue
            if key not in need or need[key][2] < val:
                need[key] = (sem, key, val)
        return list(need.values())

    def _wait(self, e, deps, attach=False):
        need = self._need(e, deps)
        last = None
        if attach and need:
            last = need.pop()
        for (sem, key, val) in need:
            self.eng[e].wait_ge(sem, val)
            self.seen[e][key] = val
        if last is not None:
            self.seen[e][last[1]] = last[2]
        return last

    def op(self, e, fn, reads=(), writes=(), attach=False):
        deps = []
        for b in reads:
            deps += list(b.w.values())
        for b in writes:
            deps += [t for t in b.w.values() if t[1] != e]
            deps += [t for t in b.r.values() if t[1] != e]
        last = self._wait(e, deps, attach=(attach and ATTACH))
        ins = fn(self.eng[e])
        if last is not None:
            ins._wait_ge(last[0], last[2])
        self.cnt[e] += 1
        ins.then_inc(self.sem[e], 1)
        tok = (self.sem[e], e, self.cnt[e])
        for b in reads:
            b.r[e] = tok
        for b in writes:
            b.w = {e: tok}
            b.r = {}
        return ins

    def load(self, q, sb, out_ap, dr, in_ap, part=False):
        deps = list(dr.w.values()) + list(sb.r.values())
        deps += [t for t in sb.w.values() if not (part and t[1] == sb.sem[1])]
        self._wait(q, deps)
        ins = self.eng[q].dma_start(out=out_ap, in_=in_ap)
        sb.sem[2] += 16
        ins.then_inc(sb.sem[0], 16)
        tok = (sb.sem[0], sb.sem[1], sb.sem[2])
        self.pending[tok[1]] = tok
        dr.r[tok[1]] = tok
        if part:
            sb.w[tok[1]] = tok
        else:
            sb.w = {tok[1]: tok}
        sb.r = {}

    def store(self, q, dr, out_ap, sb, in_ap):
        deps = list(sb.w.values()) + list(dr.r.values())
        self._wait(q, deps)
        ins = self.eng[q].dma_start(out=out_ap, in_=in_ap)
        sb.sem[2] += 16
        ins.then_inc(sb.sem[0], 16)
        tok = (sb.sem[0], sb.sem[1], sb.sem[2])
        self.pending[tok[1]] = tok
        sb.r[tok[1]] = tok
        dr.w[tok[1]] = tok

    def barrier(self):
        sp = 'sp'
        deps = [(self.sem[e], e, self.cnt[e]) for e in ('pe', 'act', 'dve', 'pool')
                if self.cnt[e] > 0]
        deps += list(self.pending.values())
        self._wait(sp, deps)
        self.eng[sp].sem_inc(self.sem[sp], 1)
        self.cnt[sp] += 1
        tok = (self.sem[sp], sp, self.cnt[sp])
        for e in ('pe', 'act', 'dve', 'pool'):
            self._wait(e, [tok])
        self.pending = {}
        for d in self.dbufs:
            d.w = {}
            d.r = {}

    def mm(self, ps, out_ap, lhsT_b, lhsT_ap, rhs_b, rhs_ap, start, stop, attach=False):
        return self.op('pe', lambda e: e.matmul(out_ap, lhsT_ap, rhs_ap, start=start, stop=stop),
                       reads=[lhsT_b, rhs_b], writes=[ps], attach=attach)

    def tr(self, ps, out_ap, in_b, in_ap, ident_b, ident_ap):
        return self.op('pe', lambda e: e.transpose(out_ap, in_ap, ident_ap),
                       reads=[in_b, ident_b], writes=[ps])

    def act(self, out_b, out_ap, in_b, in_ap, func, bias=None, scale=None, extra_reads=(),
            accum=None, accum_b=None, attach=False):
        kw = {}
        if bias is not None:
            kw['bias'] = bias
        if scale is not None:
            kw['scale'] = scale
        if accum is not None:
            kw['accum_out'] = accum
        wr = [out_b] + ([accum_b] if accum_b is not None else [])
        return self.op('act', lambda e: e.activation(out=out_ap, in_=in_ap, func=func, **kw),
                       reads=[in_b] + list(extra_reads), writes=wr, attach=attach)

    def tt(self, e, out_b, out_ap, a_b, a_ap, b_b, b_ap, op):
        return self.op(e, lambda en: en.tensor_tensor(out=out_ap, in0=a_ap, in1=b_ap, op=op),
                       reads=[a_b, b_b], writes=[out_b])

    def ts(self, e, out_b, out_ap, a_b, a_ap, s1, s2, op0, op1=None, extra_reads=()):
        if op1 is None:
            f = lambda en: en.tensor_scalar(out=out_ap, in0=a_ap, scalar1=s1, scalar2=None, op0=op0)
        else:
            f = lambda en: en.tensor_scalar(out=out_ap, in0=a_ap, scalar1=s1, scalar2=s2,
                                            op0=op0, op1=op1)
        return self.op(e, f, reads=[a_b] + list(extra_reads), writes=[out_b])

    def stt(self, e, out_b, out_ap, a_b, a_ap, scalar, b_b, b_ap, op0, op1, extra_reads=()):
        return self.op(e, lambda en: en.scalar_tensor_tensor(out=out_ap, in0=a_ap, scalar=scalar,
                                                             in1=b_ap, op0=op0, op1=op1),
                       reads=[a_b, b_b] + list(extra_reads), writes=[out_b])

    def copy(self, e, out_b, out_ap, in_b, in_ap):
        if e == 'act':
            return self.op('act', lambda en: en.copy(out=out_ap, in_=in_ap),
                           reads=[in_b], writes=[out_b])
        return self.op(e, lambda en: en.tensor_copy(out=out_ap, in_=in_ap),
                       reads=[in_b], writes=[out_b])

    def rsqrt(self, b, ap):
        self.act(b, ap, b, ap, AF.Ln)
        self.act(b, ap, b, ap, AF.Exp, scale=-0.5)

    def memset(self, e, b, ap, val):
        return self.op(e, lambda en: en.memset(ap, val), reads=[], writes=[b])


def bc(ap, shape):
    return ap.broadcast_to(list(shape))


def build_program(NB, depth_run, debug=False):
    nc = bass.Bass("TRN2", target_bir_lowering=False)

    def din(name, shape, dt=F32):
        return nc.dram_tensor(name, list(shape), dt, kind="ExternalInput").ap()

    x_in = din("x", [NB, LAT, D])
    ctx_in = din("ctx", [NB, CTX, D])
    cT_in = din("cT", [128, KC, 5])
    ada_w = din("ada_w", [DEPTH, D, 6 * D])
    ada_b_tm = din("ada_b", [DEPTH, 6 * D])
    ada_b_fm = din("ada_b_fm", [DEPTH, 128, 48])
    w_in = din("w_in", [DEPTH, D, 12864])
    b_gate = din("b_gate", [DEPTH, 3072])
    conv_w = din("conv_w", [DEPTH, 128, 24, 5])
    conv_b = din("conv_b", [DEPTH, 128, 24])
    dt_bias = din("ssm_dt_bias", [DEPTH, 64])
    a_log = din("ssm_a_log", [DEPTH, 64])
    ssm_d = din("ssm_d", [DEPTH, 32])
    ssm_nw = din("ssm_norm_w", [DEPTH, 2048])
    w_sso = din("w_ssm_out", [DEPTH, 2048, D])
    gq_norm = din("gqa_q_norm", [DEPTH, 128])
    gk_norm = din("gqa_k_norm", [DEPTH, 128])
    w_gqo = din("w_gqa_out", [DEPTH, D, D])
    dlam = din("diff_lambda", [DEPTH, 256])
    dnw_fm = din("diff_norm_w", [DEPTH, 128, 1])
    w_dfo = din("w_diff_out", [DEPTH, D, D])
    w_o = din("w_o", [DEPTH, D, D])
    ln1_g = din("ln1_g", [DEPTH, D])
    ln1_b = din("ln1_b", [DEPTH, D])
    ffn_wi = din("ffn_w_in", [DEPTH, D, 2 * FFH])
    ffn_wo = din("ffn_w_out", [DEPTH, FFH, D])
    ln2_g = din("ln2_g", [DEPTH, D])
    ln2_b = din("ln2_b", [DEPTH, D])
    consts_in = din("consts", [128, 6, 128])
    ropeG_in = din("ropeG", [128, 16, 2, 128])
    ropeD_in = din("ropeD", [128, 16, 2, 64])
    out = nc.dram_tensor("out", [NB, LAT, D], F32, kind="ExternalOutput").ap()

    def scratch(name, shape, dt):
        return nc.dram_tensor(name, list(shape), dt, kind="Internal").ap()

    xA_d = scratch("xA", [T, D], F32)
    xB_d = scratch("xB", [T, D], F32)
    ybr_d = scratch("ybr", [32 * 128, T], BF16)
    mod_d = scratch("mod_tm", [DEPTH, 5, 6 * D], F32)

    with ExitStack() as es:
        K = Kx(nc, es)
        X_IN = K.dbuf(x_in)
        CTX_IN = K.dbuf(ctx_in)
        OUT = K.dbuf(out)
        XA = K.dbuf(xA_d)
        XB = K.dbuf(xB_d)
        YBR = K.dbuf(ybr_d)
        MODD = K.dbuf(mod_d)
        WD = K.dbuf(None)

        cst = K.tile('cst', [128, 6, 128], F32, dma=True)
        K.load('sp', cst, cst[:], WD, consts_in)
        IDF, MLE, MGT, MGE, MLT, ONESF = (cst[:, i, :] for i in range(6))
        cstb = K.tile('cstb', [128, 2, 128], BF16)
        K.copy('dve', cstb, cstb[:, 0, :], cst, cst[:, 0, :])
        K.copy('dve', cstb, cstb[:, 1, :], cst, cst[:, 5, :])
        IDB = cstb[:, 0, :]
        ONESB = cstb[:, 1, :]
        modT = K.tile('modT', [128, DEPTH, 48, 5], F32)

        with K.scope('adaLN'):
            cT = K.tile('cT', [128, KC, 5], F32, dma=True)
            K.load('sp', cT, cT[:], WD, cT_in)
            scT = K.tile('scT', [128, KC, 5], BF16)
            K.act(scT, scT[:], cT, cT[:], AF.Silu)
            abf = K.tile('abf', [128, DEPTH, 48], F32, dma=True)
            K.load('sp', abf, abf[:], WD, ada_b_fm.rearrange("l p c -> p l c"))
            wts = [K.tile('adaw%d' % i, [128, KC, 512], BF16, dma=True) for i in range(2)]
            abt = [K.tile('abt%d' % i, [5, 512], F32, dma=True) for i in range(2)]
            mo = [K.tile('mo%d' % i, [5, 512], F32, dma=True) for i in range(2)]
            ps_t = [K.psum('ps0t%d' % i, [128, 512], F32) for i in range(2)]
            ps_f = [K.psum('ps0f%d' % i, [128, 4, 5], F32) for i in range(2)]
            n = 0
            for L in range(depth_run):
                for j in range(12):
                    W = wts[n % 2]
                    K.load('pool', W, W[:], WD,
                           ada_w[L].rearrange("(kc p) f -> p kc f", p=128)[:, :, j * 512:(j + 1) * 512])
                    ab = abt[n % 2]
                    K.load('sp', ab, ab[:], WD,
                           ada_b_tm[L:L + 1, j * 512:(j + 1) * 512].broadcast_to([5, 512]))
                    pt = ps_t[n % 2]
                    for kc in range(KC):
                        K.mm(pt, pt[0:5, :], scT, scT[:, kc, :], W, W[:, kc, :], kc == 0, kc == KC - 1)
                    m = mo[n % 2]
                    K.tt('dve', m, m[:], pt, pt[0:5, :], ab, ab[:], ALU.add)
                    K.store('sp', MODD, mod_d[L, :, j * 512:(j + 1) * 512], m, m[:])
                    pf = ps_f[n % 2]
                    for f in range(4):
                        for kc in range(KC):
                            K.mm(pf, pf[:, f, :], W, W[:, kc, f * 128:(f + 1) * 128], scT, scT[:, kc, :],
                                 kc == 0, kc == KC - 1)
                    K.tt('dve', modT, modT[:, L, j * 4:(j + 1) * 4, :], pf, pf[:],
                         abf, bc(abf[:, L, j * 4:(j + 1) * 4, None], [128, 4, 5]), ALU.add)
                    n += 1

        for b in range(NB):
            for L in range(depth_run):
                last = (L == DEPTH - 1)
                lam_init = 0.8 - 0.6 * math.exp(-0.3 * L)
                if L == 0:
                    def xsrc(t0, nt, b=b):
                        if t0 < 2:
                            return CTX_IN, ctx_in[b, t0 * 128:(t0 + nt) * 128, :]
                        return X_IN, x_in[b, (t0 - 2) * 128:(t0 - 2 + nt) * 128, :]
                else:
                    def xsrc(t0, nt):
                        return XA, xA_d[t0 * 128:(t0 + nt) * 128, :]
                with K.scope():
                    layer(K, nc, locals())
    return nc


def layer(K, nc, G):
    b, L, last, lam_init, xsrc = G['b'], G['L'], G['last'], G['lam_init'], G['xsrc']
    WD, MODD, YBR, XA, XB, OUT = G['WD'], G['MODD'], G['YBR'], G['XA'], G['XB'], G['OUT']
    cst, cstb, modT = G['cst'], G['cstb'], G['modT']
    IDF, MLE, MGT, MGE, MLT, ONESF, IDB, ONESB = (G[k] for k in
                                                  ('IDF', 'MLE', 'MGT', 'MGE', 'MLT', 'ONESF', 'IDB', 'ONESB'))
    w_in = G['w_in']
    mod_d, ybr_d, xA_d, xB_d, out = G['mod_d'], G['ybr_d'], G['xA_d'], G['xB_d'], G['out']

    def wsl(c0, c1):
        return w_in[L].rearrange("(kc p) f -> p kc f", p=128)[:, :, c0:c1]

    sc1 = K.tile('sc1', [128, 2, 8], F32)
    sh1 = K.tile('sh1', [128, 2, 8], F32)
    sc2 = K.tile('sc2', [128, 2, 8], F32)
    sh2 = K.tile('sh2', [128, 2, 8], F32)
    for j, m in enumerate((b, 4)):
        K.copy('dve', sh1, sh1[:, j, :], modT, modT[:, L, 0:8, m])
        K.ts('dve', sc1, sc1[:, j, :], modT, modT[:, L, 8:16, m], 1.0, None, ALU.add)
        K.copy('dve', sh2, sh2[:, j, :], modT, modT[:, L, 24:32, m])
        K.ts('dve', sc2, sc2[:, j, :], modT, modT[:, L, 32:40, m], 1.0, None, ALU.add)
    hT = K.tile('hT', [128, KC, T], BF16)

    phase_A(K, G, hT, sc1, sh1, xsrc)
    phase_ssm(K, G, hT, wsl)
    phase_gqa(K, G, hT, wsl)
    for hh in range(2):
        phase_diff(K, G, hT, wsl, hh)
    for th in range(2):
        phase_merge(K, G, hT, wsl, th)
    phase_A(K, G, hT, sc2, sh2, lambda t0, nt: (XB, xB_d[t0 * 128:(t0 + nt) * 128, :]))
    phase_ffn(K, G, hT)


def load_rows(K, G, kmod, lg, lb):
    L, b = G['L'], G['b']
    rows = K.tile('rows', [128, 4, D], F32, dma=True)
    for j, m in enumerate((b, 4)):
        K.load('sp', rows, rows[:, j, :], G['MODD'],
               G['mod_d'][L, m:m + 1, kmod * D:(kmod + 1) * D].broadcast_to([128, D]), part=(j > 0))
    for j, src in enumerate((lg, lb)):
        K.load('sp', rows, rows[:, 2 + j, :], G['WD'], src[L:L + 1, :].broadcast_to([128, D]), part=True)
    return rows


def phase_A(K, G, hT, sc1, sh1, xsrc):
    IDF = G['IDF']
    cst = G['cst']
    with K.scope('A'):
        ps = [K.psum('psA%d' % i, [128, 512], F32) for i in range(4)]
        xg = [K.tile('xg%d' % i, [128, 4, D], F32, dma=True) for i in range(2)]
        groups = [(0, 2, 1), (2, 4, 0), (6, 4, 0), (10, 4, 0), (14, 4, 0)]
        n = 0
        for gi, (t0, nt, isc) in enumerate(groups):
            xb = xg[gi % 2]
            db, ap = xsrc(t0, nt)
            K.load('sp', xb, xb[:, 0:nt, :], db, ap.rearrange("(n p) d -> p n d", p=128))
            for kc in range(KC):
                p = ps[n % 4]
                for j in range(nt):
                    K.tr(p, p[:, j * 128:(j + 1) * 128], xb, xb[:, j, kc * 128:(kc + 1) * 128], cst, IDF)
                o = hT[:, kc, t0 * 128:(t0 + nt) * 128]
                if n % 2 == 0:
                    K.ts('dve', hT, o, p, p[:, 0:nt * 128], sc1[:, isc, kc:kc + 1], sh1[:, isc, kc:kc + 1],
                         ALU.mult, ALU.add, extra_reads=[sc1, sh1])
                else:
                    K.act(hT, o, p, p[:, 0:nt * 128], AF.Identity, bias=sh1[:, isc, kc:kc + 1],
                          scale=sc1[:, isc, kc:kc + 1], extra_reads=[sc1, sh1])
                n += 1


def proj_tm(K, ps, hT, tt, W, c0, ncols):
    for kc in range(KC):
        K.mm(ps, ps[:, 0:ncols], hT, hT[:, kc, tt * 128:(tt + 1) * 128], W, W[:, kc, c0:c0 + ncols],
             kc == 0, kc == KC - 1)


def phase_ssm(K, G, hT, wsl):
    L = G['L']
    WD, YBR, ybr_d = G['WD'], G['YBR'], G['ybr_d']
    cst, cstb = G['cst'], G['cstb']
    IDB, ONESF = G['IDB'], G['ONESF']
    MLE, MGT, MGE, MLT = G['MLE'], G['MGT'], G['MGE'], G['MLT']
    with K.scope():
        dt = K.tile('dt', [128, NT, 64], F32)
        adt = K.tile('adt', [128, NT, 64], F32)
        prm = K.tile('prm', [128, 3, 64], F32, dma=True)
        K.load('sp', prm, prm[:, 0, :], WD, G['dt_bias'][L:L + 1, :].broadcast_to([128, 64]))
        K.load('sp', prm, prm[:, 1, :], WD, G['a_log'][L:L + 1, :].broadcast_to([128, 64]), part=True)
        K.load('sp', prm, prm[:, 2, 0:32], WD, G['ssm_d'][L:L + 1, :].broadcast_to([128, 32]), part=True)
        nw = K.tile('nw', [128, 2048], F32, dma=True)
        K.load('sp', nw, nw[:], WD, G['ssm_nw'][L:L + 1, :].broadcast_to([128, 2048]))
        cw = K.tile('cw', [128, 24, 5], F32, dma=True)
        K.load('sp', cw, cw[:], WD, G['conv_w'][L])
        cb = K.tile('cb', [128, 24], F32, dma=True)
        K.load('sp', cb, cb[:], WD, G['conv_b'][L])
        aneg = K.tile('aneg', [128, 64], F32)
        K.act(aneg, aneg[:], prm, prm[:, 1, :], AF.Exp)
        K.ts('dve', aneg, aneg[:], aneg, aneg[:], -1.0, None, ALU.mult)
        with K.scope('ssm_dt'):
            Wdt = K.tile('Wdt', [128, KC, 64], BF16, dma=True)
            K.load('pool', Wdt, Wdt[:], WD, wsl(O_DT, O_DT + 64))
            psd = [K.psum('psd%d' % i, [128, 64], F32) for i in range(2)]
            tmp = [K.tile('dtt%d' % i, [128, 64], F32) for i in range(2)]
            for tt in range(NT):
                p = psd[tt % 2]
                proj_tm(K, p, hT, tt, Wdt, 0, 64)
                t1 = tmp[tt % 2]
                K.tt('dve', t1, t1[:], p, p[:], prm, prm[:, 0, :], ALU.add)
                K.act(t1, t1[:], t1, t1[:], AF.Exp)
                K.act(dt, dt[:, tt, :], t1, t1[:], AF.Ln, bias=1.0)
            K.tt('dve', adt, adt[:], dt, dt[:], aneg, bc(aneg[:, None, :], [128, NT, 64]), ALU.mult)

        for g in range(4):
            with K.scope():
                ssm_group(K, G, hT, wsl, g, dt, adt, prm, nw, cw, cb)


def ssm_group(K, G, hT, wsl, g, dt, adt, prm, nw, cw, cb):
    WD, YBR, ybr_d = G['WD'], G['YBR'], G['ybr_d']
    cst, cstb = G['cst'], G['cstb']
    IDB, ONESF = G['IDB'], G['ONESF']
    MLE, MGT, MGE, MLT = G['MLE'], G['MGT'], G['MGE'], G['MLT']
    PADW = 2316
    xs_tm = K.tile('xs_tm', [128, NT, 512], BF16)
    B_tm = K.tile('B_tm', [128, NT, 128], BF16)
    BT = K.tile('BT', [128, T], BF16)
    CT = K.tile('CT', [128, T], BF16)
    with K.scope('ssm_B1'):
        W6 = K.tile('W6', [128, KC, 768], BF16, dma=True)
        K.load('pool', W6, W6[:, :, 0:512], WD, wsl(O_XBC + g * 512, O_XBC + (g + 1) * 512))
        K.load('pool', W6, W6[:, :, 512:640], WD,
               wsl(O_XBC + 2048 + g * 128, O_XBC + 2048 + (g + 1) * 128), part=True)
        K.load('pool', W6, W6[:, :, 640:768], WD,
               wsl(O_XBC + 2560 + g * 128, O_XBC + 2560 + (g + 1) * 128), part=True)
        cchunk = [g * 4 + 0, g * 4 + 1, g * 4 + 2, g * 4 + 3, 16 + g, 20 + g]
        xcT = K.tile('xcT', [128, 4, T], BF16)
        xpad = [K.tile('xpad%d' % i, [128, PADW], F32) for i in range(2)]
        cv = [K.tile('cv0', [128, PADW], F32)]
        for i in range(2):
            K.memset('pool', xpad[i], xpad[i][:], 0.0)
        psp = [K.psum('psp%d' % i, [128, 512], F32) for i in range(3)]
        pst = [K.psum('pst%d' % i, [128, 512], BF16) for i in range(2)]
        n = 0
        for fc in range(6):
            xp = xpad[fc % 2]
            c = cv[0]
            cc = cchunk[fc]
            for (a0, a1) in TB:
                p = psp[n % 3]
                n += 1
                for kc in range(KC):
                    K.mm(p, p[:, 0:a1 - a0], W6, W6[:, kc, fc * 128:(fc + 1) * 128], hT, hT[:, kc, a0:a1],
                         kc == 0, kc == KC - 1)
                off = 2 if a0 < 256 else 6
                K.copy('act', xp, xp[:, a0 + off:a1 + off], p, p[:, 0:a1 - a0])
            W = 2308
            K.ts('dve', c, c[:, 2:2 + W], xp, xp[:, 2:2 + W], cw[:, cc, 2:3], cb[:, cc:cc + 1],
                 ALU.mult, ALU.add, extra_reads=[cw, cb])
            for j in (0, 1, 3, 4):
                K.stt('dve', c, c[:, 2:2 + W], xp, xp[:, j:j + W], cw[:, cc, j:j + 1], c, c[:, 2:2 + W],
                      ALU.mult, ALU.add, extra_reads=[cw])
            if fc < 4:
                ob, o0, o1 = xcT, xcT[:, fc, 0:256], xcT[:, fc, 256:T]
            elif fc == 4:
                ob, o0, o1 = BT, BT[:, 0:256], BT[:, 256:T]
            else:
                ob, o0, o1 = CT, CT[:, 0:256], CT[:, 256:T]
            K.act(ob, o0, c, c[:, 2:258], AF.Silu)
            K.act(ob, o1, c, c[:, 262:2310], AF.Silu)
        for tt in range(NT):
            p = pst[tt % 2]
            for fc in range(4):
                K.tr(p, p[:, fc * 128:(fc + 1) * 128], xcT, xcT[:, fc, tt * 128:(tt + 1) * 128], cstb, IDB)
            K.copy('dve' if tt % 2 else 'act', xs_tm, xs_tm[:, tt, :], p, p[:])
        for t4 in range(0, NT, 4):
            nt = min(4, NT - t4)
            p = pst[(t4 // 4) % 2]
            for j in range(nt):
                K.tr(p, p[:, j * 128:(j + 1) * 128], BT, BT[:, (t4 + j) * 128:(t4 + j + 1) * 128], cstb, IDB)
            K.copy('dve', B_tm, B_tm[:, t4:t4 + nt, :], p,
                   p[:, 0:nt * 128].rearrange("p (n c) -> p n c", c=128))

    yacc = K.tile('yacc', [128, NT, 512], F32)
    with K.scope('ssm_sweep'):
        S = K.tile('S', [128, 512], F32)
        Sb = K.tile('Sb', [128, 512], BF16)
        Xb = [K.tile('X%d' % i, [128, 8, 128], F32) for i in range(2)]
        Eb = [K.tile('E%d' % i, [128, 8, 128], F32) for i in range(2)]
        MTb = [K.tile('MT%d' % i, [128, 8, 128], BF16) for i in range(2)]
        Gmb = [K.tile('Gm%d' % i, [128, 128], F32) for i in range(2)]
        xdtb = [K.tile('xdt%d' % i, [128, 8, 64], BF16) for i in range(2)]
        xwb = [K.tile('xw%d' % i, [128, 8, 64], BF16) for i in range(2)]
        smb = [K.tile('sm%d' % i, [128, 16], F32) for i in range(2)]
        tmpb = [K.tile('yt%d' % i, [128, 512], F32) for i in range(2)]
        pG = K.psum('pG', [128, 128], F32)
        pD = [K.psum('pD%d' % i, [128, 512], F32) for i in range(2)]
        pY = K.psum('pY', [128, 512], F32)
        pYo = K.psum('pYo', [128, 512], F32)
        pS = K.psum('pS', [128, 512], F32)
        pc = K.psum('pc', [128, 16], F32)
        n = 0
        for d in range(2):
            Ma, Mb, Mg = (MLE, MGT, MLE) if d == 0 else (MGE, MLT, MGE)
            endcol = 127 if d == 0 else 0
            order = list(range(NT)) if d == 0 else [1, 0] + list(range(NT - 1, 1, -1))
            hs = slice(d * 32 + g * 8, d * 32 + g * 8 + 8)
            K.memset('dve', S, S[:], 0.0)
            K.memset('dve', Sb, Sb[:], 0.0)
            def stage1(c, i):
                tok = slice(c * 128, (c + 1) * 128)
                X, E, MT, Gm, xdt, xw, sm = Xb[i], Eb[i], MTb[i], Gmb[i], xdtb[i], xwb[i], smb[i]
                K.mm(pG, pG[:], BT, BT[:, tok], CT, CT[:, tok], True, True)
                K.tt('dve', X, X[:], adt, bc(adt[:, c, hs, None], [128, 8, 128]),
                     cst, bc(Ma[:, None, :], [128, 8, 128]), ALU.mult)
                K.tt('dve', Gm, Gm[:], pG, pG[:], cst, Mg, ALU.mult)
                K.mm(pc, pc[:, 0:8], cst, Ma, adt, adt[:, c, hs], True, True)
                K.mm(pc, pc[:, 8:16], cst, ONESF, adt, adt[:, c, hs], True, True)
                K.act(sm, sm[:], pc, pc[:], AF.Exp)
                K.tt('pool', xdt, xdt[:], xs_tm, xs_tm[:, c, :].rearrange("p (h q) -> p h q", q=64),
                     dt, bc(dt[:, c, hs, None], [128, 8, 64]), ALU.mult)

            def stage1b(c, i):
                X, E, MT, Gm, xdt, xw, sm = Xb[i], Eb[i], MTb[i], Gmb[i], xdtb[i], xwb[i], smb[i]
                for hh in range(2):
                    K.mm(pD[hh], pD[hh][:], cst, Mb, X,
                         X[:, hh * 4:(hh + 1) * 4, :].rearrange("p h l -> p (h l)"), True, True)
                    K.act(E, E[:, hh * 4:(hh + 1) * 4, :].rearrange("p h l -> p (h l)"), pD[hh], pD[hh][:], AF.Exp)
                K.tt('dve', MT, MT[:], E, E[:], Gm, bc(Gm[:, None, :], [128, 8, 128]), ALU.mult)
                K.tt('pool', xw, xw[:], xdt, xdt[:], E, bc(E[:, :, endcol:endcol + 1], [128, 8, 64]), ALU.mult)

            def stage2(c, i):
                tok = slice(c * 128, (c + 1) * 128)
                MT, xdt, xw, sm, ytmp = MTb[i], xdtb[i], xwb[i], smb[i], tmpb[i]
                K.mm(pYo, pYo[:], CT, CT[:, tok], Sb, Sb[:], True, True)
                K.mm(pS, pS[:], B_tm, B_tm[:, c, :], xw, xw[:].rearrange("p h q -> p (h q)"), True, True)
                for h in range(8):
                    K.mm(pY, pY[:, h * 64:(h + 1) * 64], MT, MT[:, h, :], xdt, xdt[:, h, :], True, True)
                K.tt('dve', S, S[:].rearrange("p (h q) -> p h q", q=64),
                     S, S[:].rearrange("p (h q) -> p h q", q=64),
                     sm, bc(sm[:, 8:16, None], [128, 8, 64]), ALU.mult)
                K.tt('dve', S, S[:], S, S[:], pS, pS[:], ALU.add)
                K.copy('act', Sb, Sb[:], S, S[:])
                K.tt('dve', ytmp, ytmp[:].rearrange("p (h q) -> p h q", q=64),
                     pYo, pYo[:].rearrange("p (h q) -> p h q", q=64),
                     sm, bc(sm[:, 0:8, None], [128, 8, 64]), ALU.mult)
                if d == 0:
                    K.tt('dve', yacc, yacc[:, c, :], ytmp, ytmp[:], pY, pY[:], ALU.add)
                else:
                    K.tt('dve', ytmp, ytmp[:], ytmp, ytmp[:], pY, pY[:], ALU.add)
                    K.tt('pool', yacc, yacc[:, c, :], yacc, yacc[:, c, :], ytmp, ytmp[:], ALU.add)

            for t in range(len(order) + 1):
                if t < len(order):
                    stage1(order[t], t % 2)
                if t >= 1:
                    stage2(order[t - 1], (t - 1) % 2)
                if t < len(order):
                    stage1b(order[t], t % 2)

    with K.scope('ssm_post'):
        Wz = K.tile('Wz', [128, KC, 512], BF16, dma=True)
        K.load('pool', Wz, Wz[:], WD, wsl(O_Z + g * 512, O_Z + (g + 1) * 512))
        ysT = K.tile('ysT', [128, 4, T], BF16, dma=True)
        pz = [K.psum('pz%d' % i, [128, 512], F32) for i in range(2)]
        pt = [K.psum('pt%d' % i, [128, 512], BF16) for i in range(2)]
        zs = [K.tile('zs%d' % i, [128, 512], F32) for i in range(2)]
        yb = [K.tile('yb%d' % i, [128, 512], F32) for i in range(2)]
        ynb = [K.tile('yn%d' % i, [128, 512], BF16) for i in range(2)]
        jk = [K.tile('jk%d' % i, [128, 512], F32) for i in range(2)]
        ssb = [K.tile('ss%d' % i, [128, 2], F32) for i in range(2)]
        for tt in range(NT):
            i = tt % 2
            p, z, y = pz[i], zs[i], yb[i]
            proj_tm(K, p, hT, tt, Wz, 0, 512)
            K.act(z, z[:], p, p[:], AF.Silu)
            K.tt('pool', y, y[:].rearrange("p (h q) -> p h q", q=64),
                 xs_tm, xs_tm[:, tt, :].rearrange("p (h q) -> p h q", q=64),
                 prm, bc(prm[:, 2, g * 8:(g + 1) * 8, None], [128, 8, 64]), ALU.mult)
            K.tt('dve', y, y[:], y, y[:], yacc, yacc[:, tt, :], ALU.add)
            K.tt('dve', yacc, yacc[:, tt, :], y, y[:], z, z[:], ALU.mult)
        for tt in range(NT):
            i = tt % 2
            yn, ss = ynb[i], ssb[i]
            K.act(jk[i], jk[i][:], yacc, yacc[:, tt, :], AF.Square, accum=ss[:, 0:1], accum_b=ss)
            K.ts('dve', ss, ss[:, 1:2], ss, ss[:, 0:1], 1.0 / 512, EPS, ALU.mult, ALU.add)
            K.rsqrt(ss, ss[:, 1:2])
            K.stt('dve', yn, yn[:], yacc, yacc[:, tt, :], ss[:, 1:2], nw, nw[:, g * 512:(g + 1) * 512],
                  ALU.mult, ALU.mult, extra_reads=[ss])
            q = pt[i]
            for fc in range(4):
                K.tr(q, q[:, fc * 128:(fc + 1) * 128], yn, yn[:, fc * 128:(fc + 1) * 128], cstb, IDB)
            K.copy('dve' if tt % 2 else 'act', ysT, ysT[:, :, tt * 128:(tt + 1) * 128], q,
                   q[:].rearrange("p (c t) -> p c t", t=128))
        K.store('sp', YBR, ybr_d[g * 512:(g + 1) * 512, :].rearrange("(c p) t -> p c t", p=128),
                ysT, ysT[:])


def rope(K, e, dst, dst_ap, src, src_ap, tab, cos_ap, sin_ap, tmp, tmp_ap, nh, hd, e_first=None):
    q = hd // 4
    K.tt(e_first or e, dst, dst_ap, src, src_ap, tab, bc(cos_ap[:, None, :], [128, nh, hd]), ALU.mult)
    for a in range(2):
        for s in range(2):
            o0 = a * 2 * q + s * q
            i0 = a * 2 * q + (1 - s) * q
            K.tt(e, tmp, tmp_ap[:, :, o0:o0 + q], src, src_ap[:, :, i0:i0 + q],
                 tab, bc(sin_ap[:, None, o0:o0 + q], [128, nh, q]), ALU.mult)
    K.tt(e, dst, dst_ap, dst, dst_ap, tmp, tmp_ap, ALU.add)


def attention_core(K, G, groups, scale, kT, qT, v_tm, pfx):
    cstb, ONESB = G['cstb'], G['ONESB']
    pS = [K.psum(pfx + 'S%d' % i, [128, 2, 512], F32) for i in range(2)]
    pO = [K.psum(pfx + 'O%d' % i, [128, 512], F32) for i in range(2)]
    pZ = [K.psum(pfx + 'Z%d' % i, [128, 512], F32) for i in range(2)]
    PT = [K.tile(pfx + 'PT%d' % i, [128, 2, 512], BF16) for i in range(3)]
    items = []
    for gi, g in enumerate(groups):
        assert (g['k1'] - g['k0']) % 2 == 0
        for kt in range(g['k0'], g['k1'], 2):
            items.append((gi, g, kt))
    LA = 1
    deferred = []
    for idx in range(len(items) + LA):
        while deferred and deferred[0][0] <= idx:
            deferred.pop(0)[1]()
        if idx < len(items):
            gi, g, kt = items[idx]
            nq = g['nq']
            s_, pt = pS[idx % 2], PT[idx % 3]
            for a in range(2):
                K.mm(s_, s_[:, a, 0:nq], kT, g['k'](kt + a), qT, g['q'], True, True, attach=True)
            K.act(pt, pt[:, :, 0:nq], s_, s_[:, :, 0:nq], AF.Exp, scale=scale, attach=True)
        j = idx - LA
        if j >= 0:
            gi, g, kt = items[j]
            nq = g['nq']
            pt = PT[j % 3]
            o, z = pO[gi % 2], pZ[gi % 2]
            for a in range(2):
                first = (kt + a == g['k0'])
                last = (kt + a == g['k1'] - 1)
                K.mm(o, o[:, 0:nq], v_tm, g['v'](kt + a), pt, pt[:, a, 0:nq], first, last, attach=True)
                K.mm(z, z[:, 0:nq], cstb, ONESB, pt, pt[:, a, 0:nq], first, last, attach=True)
            if kt + 2 >= g['k1']:
                cont = g['fin'](o, z)
                if cont is not None:
                    deferred.append((idx + 3, cont))
    for d in deferred:
        d[1]()


QBLOCKS = [(0, 256, 0, 2)] + [(256 + 512 * i, 256 + 512 * (i + 1), 0, NT) for i in range(4)]


def phase_gqa(K, G, hT, wsl):
    L = G['L']
    WD, YBR, ybr_d = G['WD'], G['YBR'], G['ybr_d']
    cst, cstb, IDB = G['cst'], G['cstb'], G['IDB']
    with K.scope():
        ropeG = K.tile('ropeG', [128, 16, 2, 128], F32, dma=True)
        K.load('sp', ropeG, ropeG[:], WD, G['ropeG_in'])
        qT = K.tile('qT', [128, 8, T], BF16)
        kT = K.tile('kT', [128, 2, T], BF16)
        v_tm = K.tile('v_tm', [128, NT, 256], BF16)
        gn = K.tile('gn', [128, 2, 128], F32, dma=True)
        K.load('sp', gn, gn[:, 0, :], WD, G['gq_norm'][L:L + 1, :].broadcast_to([128, 128]))
        K.load('sp', gn, gn[:, 1, :], WD, G['gk_norm'][L:L + 1, :].broadcast_to([128, 128]), part=True)
        with K.scope('gqa_proj'):
            W = K.tile('Wg', [128, KC, 1536], BF16, dma=True)
            K.load('pool', W, W[:, :, 0:512], WD, wsl(O_GQ, O_GQ + 512))
            K.load('pool', W, W[:, :, 512:1024], WD, wsl(O_GQ + 512, O_GQ + 1024), part=True)
            K.load('pool', W, W[:, :, 1024:1536], WD, wsl(O_GK, O_GK + 512), part=True)
            pq = [K.psum('pq%d' % i, [128, 512], F32) for i in range(3)]
            ptr = [K.psum('ptr%d' % i, [128, 512], BF16) for i in range(2)]
            qf = [K.tile('qf%d' % i, [128, 10, 128], F32) for i in range(2)]
            sq = [K.tile('sq%d' % i, [128, 10, 128], F32) for i in range(2)]
            qr = [K.tile('qr%d' % i, [128, 10, 128], F32) for i in range(2)]
            qb_ = [K.tile('qb%d' % i, [128, 10, 128], BF16) for i in range(2)]
            ssq = [K.tile('ssq%d' % i, [128, 10], F32) for i in range(2)]
            def stA(tt):
                i = tt % 2
                f, s2, r, qb16, ss = qf[i], sq[i], qr[i], qb_[i], ssq[i]
                for j in range(3):
                    proj_tm(K, pq[j], hT, tt, W, j * 512, 512)
                K.copy('act', f, f[:, 0:4, :], pq[0], pq[0][:].rearrange("p (h d) -> p h d", d=128))
                K.copy('act', f, f[:, 4:8, :], pq[1], pq[1][:].rearrange("p (h d) -> p h d", d=128))
                K.copy('act', f, f[:, 8:10, :], pq[2], pq[2][:, 0:256].rearrange("p (h d) -> p h d", d=128))
                K.copy('act', v_tm, v_tm[:, tt, :], pq[2], pq[2][:, 256:512])
                K.act(s2, s2[:], f, f[:], AF.Square)
                K.op('dve', lambda e: e.reduce_sum(out=ss[:], in_=s2[:], axis=AX.X), reads=[s2], writes=[ss])
                K.ts('dve', ss, ss[:], ss, ss[:], 1.0 / 128, EPS, ALU.mult, ALU.add)
                K.rsqrt(ss, ss[:])
                K.tt('dve', f, f[:], f, f[:], ss, bc(ss[:, :, None], [128, 10, 128]), ALU.mult)
                K.tt('dve', f, f[:, 0:8, :], f, f[:, 0:8, :], gn, bc(gn[:, 0:1, :], [128, 8, 128]), ALU.mult)
                K.tt('dve', f, f[:, 8:10, :], f, f[:, 8:10, :], gn, bc(gn[:, 1:2, :], [128, 2, 128]), ALU.mult)
                if tt >= 2:
                    rope(K, 'pool', r, r[:], f, f[:], ropeG, ropeG[:, tt - 2, 0, :], ropeG[:, tt - 2, 1, :],
                         s2, s2[:], 10, 128, e_first='dve')
                    K.copy('dve', qb16, qb16[:], r, r[:])
                else:
                    K.copy('dve', qb16, qb16[:], f, f[:])

            def stB(tt):
                qb16 = qb_[tt % 2]
                for half in range(3):
                    hh = [(0, 4), (4, 8), (8, 10)][half]
                    p = ptr[(tt * 3 + half) % 2]
                    for h in range(hh[0], hh[1]):
                        K.tr(p, p[:, (h - hh[0]) * 128:(h - hh[0] + 1) * 128], qb16, qb16[:, h, :], cstb, IDB)
                    nh = hh[1] - hh[0]
                    src = p[:, 0:nh * 128].rearrange("p (h t) -> p h t", t=128)
                    if half < 2:
                        K.copy('act', qT, qT[:, hh[0]:hh[1], tt * 128:(tt + 1) * 128], p, src)
                    else:
                        K.copy('act', kT, kT[:, :, tt * 128:(tt + 1) * 128], p, src)

            for tt in range(NT + 1):
                if tt < NT:
                    stA(tt)
                if tt >= 1:
                    stB(tt - 1)
        with K.scope('gqa_attn'):
            og = [K.tile('og%d' % i, [128, 512], BF16, dma=True) for i in range(3)]
            rz = [K.tile('rz%d' % i, [128, 512], F32) for i in range(2)]
            cnt = [0]

            def mkfin(m, q0, q1):
                def fin(o, z):
                    nq = q1 - q0
                    i = cnt[0]
                    cnt[0] += 1
                    r = rz[i % 2]
                    ob = og[i % 3]
                    K.op('dve', lambda e: e.reciprocal(out=r[:, 0:nq], in_=z[:, 0:nq]), reads=[z], writes=[r])
                    K.tt('dve', ob, ob[:, 0:nq], o, o[:, 0:nq], r, r[:, 0:nq], ALU.mult)
                    K.store('sp', YBR, ybr_d[(16 + m) * 128:(17 + m) * 128, q0:q1], ob, ob[:, 0:nq])
                return fin

            groups = []
            for m in range(8):
                for (q0, q1, k0, k1) in QBLOCKS:
                    groups.append(dict(
                        k=(lambda kt, m=m: kT[:, m // 4, kt * 128:(kt + 1) * 128]),
                        q=qT[:, m, q0:q1],
                        v=(lambda kt, m=m: v_tm[:, kt, (m // 4) * 128:(m // 4 + 1) * 128]),
                        nq=q1 - q0, k0=k0, k1=k1, fin=mkfin(m, q0, q1)))
            attention_core(K, G, groups, 128 ** -0.5, kT, qT, v_tm, 'a')


def phase_diff(K, G, hT, wsl, hh):
    L, lam_init = G['L'], G['lam_init']
    WD, YBR, ybr_d = G['WD'], G['YBR'], G['ybr_d']
    cst, cstb, IDB, ONESF = G['cst'], G['cstb'], G['IDB'], G['ONESF']
    with K.scope():
        ropeD = K.tile('ropeD', [128, 16, 2, 64], F32, dma=True)
        K.load('sp', ropeD, ropeD[:], WD, G['ropeD_in'])
        qT = K.tile('dqT', [128, 4, T], BF16)
        kT = K.tile('dkT', [128, 4, T], BF16)
        v_tm = K.tile('dv_tm', [128, NT, 512], BF16)
        lm = K.tile('lm', [128, 4, 64], F32, dma=True)
        K.load('sp', lm, lm[:].rearrange("p a b -> p (a b)"), WD, G['dlam'][L:L + 1, :].broadcast_to([128, 256]))
        lt = K.tile('lt', [128, 2, 64], F32)
        ls = K.tile('ls', [128, 4], F32)
        K.tt('dve', lt, lt[:, 0, :], lm, lm[:, 0, :], lm, lm[:, 1, :], ALU.mult)
        K.tt('dve', lt, lt[:, 1, :], lm, lm[:, 2, :], lm, lm[:, 3, :], ALU.mult)
        K.op('dve', lambda e: e.reduce_sum(out=ls[:, 0:2], in_=lt[:], axis=AX.X), reads=[lt], writes=[ls])
        K.act(ls, ls[:, 0:2], ls, ls[:, 0:2], AF.Exp)
        K.tt('dve', ls, ls[:, 2:3], ls, ls[:, 0:1], ls, ls[:, 1:2], ALU.subtract)
        K.ts('dve', ls, ls[:, 3:4], ls, ls[:, 2:3], lam_init, -1.0, ALU.add, ALU.mult)
        dnw = K.tile('dnw', [128, 1], F32, dma=True)
        K.load('sp', dnw, dnw[:], WD, G['dnw_fm'][L])
        K.ts('dve', dnw, dnw[:], dnw, dnw[:], 1.0 - lam_init, None, ALU.mult)
        with K.scope('diff_proj'):
            Wb = [K.tile('Wd%d' % i, [128, KC, 512], BF16, dma=True) for i in range(3)]
            for j, c0 in enumerate((O_DQ, O_DK, O_DV)):
                K.load('pool', Wb[j], Wb[j][:], WD, wsl(c0 + hh * 512, c0 + (hh + 1) * 512))
            pq = [K.psum('dpq%d' % i, [128, 512], F32) for i in range(3)]
            ptr = [K.psum('dptr%d' % i, [128, 512], BF16) for i in range(2)]
            qf = [K.tile('dqf%d' % i, [128, 8, 64], F32) for i in range(4)]
            tm = [K.tile('dtm%d' % i, [128, 8, 64], F32) for i in range(4)]
            qr = [K.tile('dqr%d' % i, [128, 8, 64], F32) for i in range(4)]
            q16 = [K.tile('dq16%d' % i, [128, 512], BF16) for i in range(4)]
            nn = [0]

            def stA(tt):
                for j in range(3):
                    p = pq[nn[0] % 3]
                    nn[0] += 1
                    proj_tm(K, p, hT, tt, Wb[j], 0, 512)
                    if j == 2:
                        K.copy('act', v_tm, v_tm[:, tt, :], p, p[:])
                        continue
                    i = (tt % 2) * 2 + j
                    f, t_, r, b16 = qf[i], tm[i], qr[i], q16[i]
                    if tt >= 2:
                        K.copy('act', f, f[:].rearrange("p h d -> p (h d)"), p, p[:])
                        rope(K, 'pool' if j % 2 else 'dve', r, r[:], f, f[:], ropeD, ropeD[:, tt - 2, 0, :],
                             ropeD[:, tt - 2, 1, :], t_, t_[:], 8, 64)
                        K.copy('act', b16, b16[:], r, r[:].rearrange("p h d -> p (h d)"))
                    else:
                        K.copy('act', b16, b16[:], p, p[:])

            def stB(tt):
                for j in range(2):
                    i = (tt % 2) * 2 + j
                    b16 = q16[i]
                    q = ptr[j]
                    for c in range(4):
                        K.tr(q, q[:, c * 128:(c + 1) * 128], b16, b16[:, c * 128:(c + 1) * 128], cstb, IDB)
                    dst = qT if j == 0 else kT
                    K.copy('dve', dst, dst[:, :, tt * 128:(tt + 1) * 128], q,
                           q[:].rearrange("p (c t) -> p c t", t=128))

            for tt in range(NT + 1):
                if tt < NT:
                    stA(tt)
                if tt >= 1:
                    stB(tt - 1)
        with K.scope('diff_attn'):
            o1 = [K.tile('o1_%d' % i, [128, 512], F32) for i in range(2)]
            o2 = [K.tile('o2_%d' % i, [128, 512], F32) for i in range(2)]
            osq = [K.tile('osq%d' % i, [128, 512], F32) for i in range(2)]
            rs = [K.tile('rs%d' % i, [128, 512], F32) for i in range(2)]
            od = [K.tile('od%d' % i, [128, 512], BF16, dma=True) for i in range(3)]
            rz = [K.tile('drz%d' % i, [128, 512], F32) for i in range(2)]
            cnt = [0]

            def mkfin(h, j, q0, q1):
                def fin(o, z):
                    nq = q1 - q0
                    i = cnt[0] % 2
                    r = rz[j]
                    K.op('dve', lambda e: e.reciprocal(out=r[:, 0:nq], in_=z[:, 0:nq]), reads=[z], writes=[r])
                    if j == 0:
                        K.tt('dve', o1[i], o1[i][:, 0:nq], o, o[:, 0:nq], r, r[:, 0:nq], ALU.mult)
                        return
                    K.tt('dve', o2[i], o2[i][:, 0:nq], o, o[:, 0:nq], r, r[:, 0:nq], ALU.mult)
                    K.stt('dve', o1[i], o1[i][:, 0:nq], o2[i], o2[i][:, 0:nq], ls[:, 3:4], o1[i], o1[i][:, 0:nq],
                          ALU.mult, ALU.add, extra_reads=[ls])
                    K.tt('pool', osq[i], osq[i][:, 0:nq], o1[i], o1[i][:, 0:nq], o1[i], o1[i][:, 0:nq], ALU.mult)
                    k = cnt[0] % 3
                    cnt[0] += 1

                    def cont():
                        K.mm(z, z[:, 0:nq], cst, ONESF, osq[i], osq[i][:, 0:nq], True, True)
                        K.ts('dve', rs[i], rs[i][:, 0:nq], z, z[:, 0:nq], 1.0 / 128, EPS, ALU.mult, ALU.add)
                        K.rsqrt(rs[i], rs[i][:, 0:nq])
                        K.stt('dve', od[k], od[k][:, 0:nq], o1[i], o1[i][:, 0:nq], dnw[:, 0:1], rs[i], rs[i][:, 0:nq],
                              ALU.mult, ALU.mult, extra_reads=[dnw])
                        hg = hh * 4 + h
                        K.store('sp', YBR, ybr_d[(24 + hg) * 128:(25 + hg) * 128, q0:q1], od[k], od[k][:, 0:nq])
                    return cont
                return fin

            groups = []
            for h in range(4):
                for (q0, q1, k0, k1) in QBLOCKS:
                    for j in range(2):
                        ps_ = slice(j * 64, (j + 1) * 64)
                        groups.append(dict(
                            k=(lambda kt, h=h, ps_=ps_: kT[ps_, h, kt * 128:(kt + 1) * 128]),
                            q=qT[ps_, h, q0:q1],
                            v=(lambda kt, h=h: v_tm[:, kt, h * 128:(h + 1) * 128]),
                            nq=q1 - q0, k0=k0, k1=k1, fin=mkfin(h, j, q0, q1)))
            attention_core(K, G, groups, 64 ** -0.5, kT, qT, v_tm, 'b')


def layernorm_tile(K, e, xn, xn_ap, src, src_ap, rows, gi, bi, st, mv):
    for c in range(2):
        K.op('dve', lambda en: en.bn_stats(out=st[:, c, :], in_=src_ap[:, c * 512:(c + 1) * 512]),
             reads=[src], writes=[st])
    K.op('dve', lambda en: en.bn_aggr(out=mv[:, 0:2], in_=st[:]), reads=[st], writes=[mv])
    K.ts('dve', mv, mv[:, 2:3], mv, mv[:, 1:2], EPS, None, ALU.add)
    K.rsqrt(mv, mv[:, 2:3])
    K.ts('dve', xn, xn_ap, src, src_ap, mv[:, 0:1], mv[:, 2:3], ALU.subtract, ALU.mult, extra_reads=[mv])
    K.tt(e, xn, xn_ap, xn, xn_ap, rows, rows[:, gi, :], ALU.mult)
    K.tt(e, xn, xn_ap, xn, xn_ap, rows, rows[:, bi, :], ALU.add)


def phase_merge(K, G, hT, wsl, th):
    L, b = G['L'], G['b']
    WD, YBR, ybr_d, XB, xB_d = G['WD'], G['YBR'], G['ybr_d'], G['XB'], G['xB_d']
    cst, cstb, IDB = G['cst'], G['cstb'], G['IDB']
    xsrc = G['xsrc']
    HT = NT // 2
    tiles = list(range(th * HT, (th + 1) * HT))
    with K.scope():
        macc = K.tile('macc', [128, HT, D], F32)
        bg = K.tile('bg', [128, 3072], F32, dma=True)
        K.load('sp', bg, bg[:], WD, G['b_gate'][L:L + 1, :].broadcast_to([128, 3072]))
        branches = [(G['w_sso'], 16, 0), (G['w_gqo'], 8, 16), (G['w_dfo'], 8, 24)]
        for br, (wout, nck, c0) in enumerate(branches):
            with K.scope('merge_br'):
                Wb = K.tile('Wbr', [128, nck, D], BF16, dma=True)
                for c4 in range(0, nck, 4):
                    K.load('pool', Wb, Wb[:, c4:c4 + 4, :], WD,
                           wout[L].rearrange("(c p) f -> p c f", p=128)[:, c4:c4 + 4, :], part=(c4 > 0))
                Wg = K.tile('Wgt', [128, KC, D], BF16, dma=True)
                K.load('pool', Wg, Wg[:, :, 0:512], WD, wsl(O_GATE + br * D, O_GATE + br * D + 512))
                K.load('pool', Wg, Wg[:, :, 512:D], WD, wsl(O_GATE + br * D + 512, O_GATE + (br + 1) * D), part=True)
                yt = [K.tile('ybt%d' % i, [128, nck, 128], BF16, dma=True) for i in range(3)]
                pg = [K.psum('pg%d' % i, [128, 512], F32) for i in range(2)]
                pp = [K.psum('pp%d' % i, [128, 512], F32) for i in range(2)]
                gt = [K.tile('gt%d' % i, [128, 512], F32) for i in range(2)]
                n = 0
                for ti, tt in enumerate(tiles):
                    y = yt[ti % 3]
                    K.load('sp', y, y[:], YBR,
                           ybr_d[c0 * 128:(c0 + nck) * 128, tt * 128:(tt + 1) * 128].rearrange("(c p) t -> p c t", p=128))
                    for hf in range(2):
                        i = n % 2
                        n += 1
                        g_, p_, gg = pg[i], pp[i], gt[i]
                        cs = slice(hf * 512, (hf + 1) * 512)
                        proj_tm(K, g_, hT, tt, Wg, hf * 512, 512)
                        K.tt('dve', gg, gg[:], g_, g_[:], bg, bg[:, br * D + hf * 512:br * D + (hf + 1) * 512], ALU.add)
                        K.act(gg, gg[:], gg, gg[:], AF.Sigmoid)
                        for c in range(nck):
                            K.mm(p_, p_[:], y, y[:, c, :], Wb, Wb[:, c, cs], c == 0, c == nck - 1)
                        if br == 0:
                            K.tt('dve', macc, macc[:, ti, cs], p_, p_[:], gg, gg[:], ALU.mult)
                        else:
                            K.tt('dve', gg, gg[:], p_, p_[:], gg, gg[:], ALU.mult)
                            K.tt('pool', macc, macc[:, ti, cs], macc, macc[:, ti, cs], gg, gg[:], ALU.add)
        with K.scope('merge_wo'):
            rows = load_rows(K, G, 2, G['ln1_g'], G['ln1_b'])
            Wo = K.tile('Wo', [128, KC, D], BF16, dma=True)
            for c4 in range(0, KC, 4):
                K.load('pool', Wo, Wo[:, c4:c4 + 4, :], WD,
                       G['w_o'][L].rearrange("(c p) f -> p c f", p=128)[:, c4:c4 + 4, :], part=(c4 > 0))
            mb = [K.tile('mb%d' % i, [128, D], BF16) for i in range(2)]
            mT = [K.tile('mT%d' % i, [128, KC, 128], BF16) for i in range(2)]
            xt = [K.tile('xt%d' % i, [128, D], F32, dma=True) for i in range(2)]
            xn = [K.tile('xn%d' % i, [128, D], F32, dma=True) for i in range(2)]
            st = [K.tile('st%d' % i, [128, 2, 6], F32) for i in range(2)]
            mv = [K.tile('mv%d' % i, [128, 4], F32) for i in range(2)]
            ptr = [K.psum('mptr%d' % i, [128, 1024], BF16) for i in range(2)]
            py = [K.psum('mpy%d' % i, [128, 512], F32) for i in range(4)]
            for ti, tt in enumerate(tiles):
                i = ti % 2
                isc = 1 if tt < 2 else 0
                x_ = xt[i]
                db, ap = xsrc(tt, 1)
                K.load('sp', x_, x_[:], db, ap)
                K.copy('act', mb[i], mb[i][:], macc, macc[:, ti, :])
                q = ptr[i]
                for c in range(KC):
                    K.tr(q, q[:, c * 128:(c + 1) * 128], mb[i], mb[i][:, c * 128:(c + 1) * 128], cstb, IDB)
                K.copy('act', mT[i], mT[i][:].rearrange("p c t -> p (c t)"), q, q[:])
                xo = xn[i]
                for hf in range(2):
                    p_ = py[(ti * 2 + hf) % 4]
                    cs = slice(hf * 512, (hf + 1) * 512)
                    for c in range(KC):
                        K.mm(p_, p_[:], mT[i], mT[i][:, c, :], Wo, Wo[:, c, cs], c == 0, c == KC - 1)
                    K.tt('dve', xo, xo[:, cs], p_, p_[:], rows, rows[:, isc, cs], ALU.mult)
                K.stt('dve', xo, xo[:], x_, x_[:], ALPHA, xo, xo[:], ALU.mult, ALU.add)
                layernorm_tile(K, 'pool', xo, xo[:], xo, xo[:], rows, 2, 3, st[i], mv[i])
                K.store('sp', XB, xB_d[tt * 128:(tt + 1) * 128, :], xo, xo[:])


def phase_ffn(K, G, hT):
    L, b = G['L'], G['b']
    WD, XB, xB_d, XA, xA_d, OUT, out = G['WD'], G['XB'], G['xB_d'], G['XA'], G['xA_d'], G['OUT'], G['out']
    NF = FFH // 128
    HTOK = T // 2
    with K.scope():
        rows = load_rows(K, G, 5, G['ln2_g'], G['ln2_b'])
        Wo = K.tile('Wfo', [128, NF, D], BF16, dma=True)
        wvo = G['ffn_wo'][L].rearrange("(c p) f -> p c f", p=128)
        for c0 in range(0, NF, 4):
            c1 = min(NF, c0 + 4)
            K.load('pool', Wo, Wo[:, c0:c1, :], WD, wvo[:, c0:c1, :], part=(c0 > 0))
        uT = K.tile('uT', [128, NF, HTOK], BF16)
        for th in range(2):
            base = th * HTOK
            blocks = [(0, 256), (256, 768), (768, 1152)] if th == 0 else [(0, 512), (512, 1024), (1024, 1152)]
            with K.scope('ffn_in'):
                Wa = [K.tile('Wa%d' % i, [128, KC, 256], BF16, dma=True) for i in range(3)]
                pa = [K.psum('pa%d' % i, [128, 512], F32) for i in range(3)]
                pb = [K.psum('pb%d' % i, [128, 512], F32) for i in range(3)]
                sa = [K.tile('sa%d' % i, [128, 512], F32) for i in range(2)]
                wv = G['ffn_wi'][L].rearrange("(kc p) f -> p kc f", p=128)
                n = 0
                for fc in range(NF):
                    W = Wa[fc % 3]
                    K.load('pool', W, W[:, :, 0:128], WD, wv[:, :, fc * 128:(fc + 1) * 128])
                    K.load('pool', W, W[:, :, 128:256], WD, wv[:, :, FFH + fc * 128:FFH + (fc + 1) * 128], part=True)
                    for (a0, a1) in blocks:
                        na = a1 - a0
                        A, B_, s_ = pa[n % 3], pb[n % 3], sa[n % 2]
                        n += 1
                        for kc in range(KC):
                            K.mm(A, A[:, 0:na], W, W[:, kc, 0:128], hT, hT[:, kc, base + a0:base + a1],
                                 kc == 0, kc == KC - 1)
                        for kc in range(KC):
                            K.mm(B_, B_[:, 0:na], W, W[:, kc, 128:256], hT, hT[:, kc, base + a0:base + a1],
                                 kc == 0, kc == KC - 1)
                        K.act(s_, s_[:, 0:na], A, A[:, 0:na], AF.Silu)
                        K.tt('dve', uT, uT[:, fc, a0:a1], s_, s_[:, 0:na], B_, B_[:, 0:na], ALU.mult)
            with K.scope('ffn_out'):
                xt = [K.tile('fxt%d' % i, [128, D], F32, dma=True) for i in range(2)]
                xn = [K.tile('fxn%d' % i, [128, D], F32, dma=True) for i in range(2)]
                st = [K.tile('fst%d' % i, [128, 2, 6], F32) for i in range(2)]
                mv = [K.tile('fmv%d' % i, [128, 4], F32) for i in range(2)]
                py = [K.psum('fpy%d' % i, [128, 512], F32) for i in range(4)]
                for ti in range(NT // 2):
                    tt = th * (NT // 2) + ti
                    i = ti % 2
                    isc = 1 if tt < 2 else 0
                    x_ = xt[i]
                    K.load('sp', x_, x_[:], XB, xB_d[tt * 128:(tt + 1) * 128, :])
                    xo = xn[i]
                    for hf in range(2):
                        p_ = py[(ti * 2 + hf) % 4]
                        cs = slice(hf * 512, (hf + 1) * 512)
                        for c in range(NF):
                            K.mm(p_, p_[:], uT, uT[:, c, ti * 128:(ti + 1) * 128], Wo, Wo[:, c, cs],
                                 c == 0, c == NF - 1)
                        K.tt('dve', xo, xo[:, cs], p_, p_[:], rows, rows[:, isc, cs], ALU.mult)
                    K.stt('dve', xo, xo[:], x_, x_[:], ALPHA, xo, xo[:], ALU.mult, ALU.add)
                    layernorm_tile(K, 'pool', xo, xo[:], xo, xo[:], rows, 2, 3, st[i], mv[i])
                    if G['L'] == G['depth_run'] - 1:
                        if tt >= 2:
                            K.store('sp', OUT, out[b, (tt - 2) * 128:(tt - 1) * 128, :], xo, xo[:])
                    else:
                        K.store('sp', XA, xA_d[tt * 128:(tt + 1) * 128, :], xo, xo[:])


def host_consts():
    p = np.arange(128)[:, None]
    f = np.arange(128)[None, :]
    c = np.zeros((128, 6, 128), np.float32)
    c[:, 0] = (p == f)
    c[:, 1] = (p <= f)
    c[:, 2] = (p > f)
    c[:, 3] = (p >= f)
    c[:, 4] = (p < f)
    c[:, 5] = 1.0
    return c


def host_rope(hd):
    t = np.arange(LAT)
    pos_row = (t // 64).astype(np.float32)
    pos_col = (t % 64).astype(np.float32)
    d_axis = hd // 2
    inv = (10000.0 ** (-np.arange(0, d_axis, 2, dtype=np.float32) / d_axis)).astype(np.float32)
    ar = pos_row[:, None] * inv
    ac = pos_col[:, None] * inv
    ang = np.concatenate([ar, ar, ac, ac], axis=-1).astype(np.float32)
    cos = np.cos(ang).astype(np.float32)
    sin = np.sin(ang).astype(np.float32)
    q = hd // 4
    sign = np.concatenate([-np.ones(q), np.ones(q), -np.ones(q), np.ones(q)]).astype(np.float32)
    tab = np.stack([cos, sin * sign], axis=1)
    return np.ascontiguousarray(tab.reshape(16, 128, 2, hd).transpose(1, 0, 2, 3))


_CACHE = {}


def run(inputs, NB, depth_run, ncores=8, trace=False):
    key = (NB, depth_run)
    if key not in _CACHE:
        _CACHE[key] = build_program(NB, depth_run)
    nc = _CACHE[key]
    f = lambda k: np.ascontiguousarray(np.asarray(inputs[k], dtype=np.float32))
    shared = {
        "ada_w": f('ada_w'), "ada_b": f('ada_b'),
        "ada_b_fm": np.ascontiguousarray(f('ada_b').reshape(DEPTH, 48, 128).transpose(0, 2, 1)),
        "w_in": f('w_in'), "b_gate": f('b_gate'),
        "conv_w": np.ascontiguousarray(f('ssm_conv_w').reshape(DEPTH, 5, 24, 128).transpose(0, 3, 2, 1)),
        "conv_b": np.ascontiguousarray(f('ssm_conv_b').reshape(DEPTH, 24, 128).transpose(0, 2, 1)),
        "ssm_dt_bias": f('ssm_dt_bias').reshape(DEPTH, 64), "ssm_a_log": f('ssm_a_log').reshape(DEPTH, 64),
        "ssm_d": f('ssm_d'), "ssm_norm_w": f('ssm_norm_w'), "w_ssm_out": f('w_ssm_out'),
        "gqa_q_norm": f('gqa_q_norm'), "gqa_k_norm": f('gqa_k_norm'), "w_gqa_out": f('w_gqa_out'),
        "diff_lambda": f('diff_lambda').reshape(DEPTH, 256),
        "diff_norm_w": f('diff_norm_w').reshape(DEPTH, 128, 1),
        "w_diff_out": f('w_diff_out'), "w_o": f('w_o'), "ln1_g": f('ln1_g'), "ln1_b": f('ln1_b'),
        "ffn_w_in": f('ffn_w_in'), "ffn_w_out": f('ffn_w_out'), "ln2_g": f('ln2_g'), "ln2_b": f('ln2_b'),
        "consts": host_consts(), "ropeG": host_rope(128), "ropeD": host_rope(64),
    }
    x, c, ctx, c_ctx = f('x'), f('c'), f('ctx'), f('c_ctx')
    in_maps = []
    for i in range(ncores):
        sl = slice(i * NB, (i + 1) * NB)
        c5 = np.zeros((5, D), np.float32)
        c5[:NB] = c[sl]
        c5[4] = c_ctx
        cT = np.ascontiguousarray(c5.reshape(5, KC, 128).transpose(2, 1, 0))
        m = dict(shared)
        m.update({"x": x[sl], "ctx": ctx[sl], "cT": cT})
        in_maps.append(m)
    if trace:
        res = run_bass_kernel_spmd(nc, in_maps, core_ids=list(range(ncores)), trace=True)
        print("exec_time_ns", res.exec_time_ns)
    else:
        res = run_bass_kernel_spmd(nc, in_maps, core_ids=list(range(ncores)))
    return np.concatenate([r["out"] for r in res.results], axis=0)


def kernel(**inputs):
    return run(inputs, 4, DEPTH).astype(np.float32)
```

```python
import math
from contextlib import ExitStack, contextmanager

import numpy as np
import concourse.bass as bass
import concourse.mybir as mybir
from concourse.bass_utils import run_bass_kernel_spmd

F32 = mybir.dt.float32
BF16 = mybir.dt.bfloat16
AF = mybir.ActivationFunctionType
ALU = mybir.AluOpType
AX = mybir.AxisListType

D = 1024
KC = 8
T = 2304
NT = 18
LAT = 2048
CTX = 256
DEPTH = 4
ALPHA = (2 * DEPTH) ** 0.25
EPS = 1e-6
FFH = 2816
O_Z, O_XBC, O_DT, O_GQ, O_GK, O_GV, O_DQ, O_DK, O_DV, O_GATE = (
    0, 2048, 5120, 5184, 6208, 6464, 6720, 7744, 8768, 9792)
TB = [(0, 256), (256, 768), (768, 1280), (1280, 1792), (1792, 2304)]
NSEM = 40
ATTACH = True
MARKS = []


class Buf:
    def __init__(self, t, sem=None):
        self.t = t
        self.w = {}
        self.r = {}
        self.sem = sem

    def __getitem__(self, idx):
        return self.t[idx]


class DBuf:
    def __init__(self, ap):
        self.ap = ap
        self.w = {}
        self.r = {}


class Kx:
    def __init__(self, nc, es):
        self.nc = nc
        self.eng = {'pe': nc.tensor, 'act': nc.scalar, 'dve': nc.vector,
                    'pool': nc.gpsimd, 'sp': nc.sync}
        self.sem = {k: es.enter_context(nc.semaphore('sem_' + k)) for k in self.eng}
        self.cnt = {k: 0 for k in self.eng}
        self.seen = {k: {} for k in self.eng}
        self.sempool = [[es.enter_context(nc.semaphore('dsem%d' % i)), 'd%d' % i, 0]
                        for i in range(NSEM)]
        self.pending = {}
        self.dbufs = []
        self.stacks = [es]
        self.scope_sems = [[]]
        self.uid = 0

    def tile(self, name, shape, dtype, dma=False):
        self.uid += 1
        t = self.stacks[-1].enter_context(
            self.nc.sbuf_tensor('%s_%d' % (name, self.uid), list(shape), dtype))
        sem = None
        if dma:
            sem = self.sempool.pop()
            self.scope_sems[-1].append(sem)
        return Buf(t, sem)

    def psum(self, name, shape, dtype):
        self.uid += 1
        t = self.stacks[-1].enter_context(
            self.nc.psum_tensor('%s_%d' % (name, self.uid), list(shape), dtype))
        return Buf(t)

    def dbuf(self, ap):
        d = DBuf(ap)
        self.dbufs.append(d)
        return d

    @contextmanager
    def scope(self, name=None):
        es = ExitStack()
        self.stacks.append(es)
        self.scope_sems.append([])
        c0 = self.cnt['pe']
        try:
            yield
        finally:
            if name is not None:
                MARKS.append((name, c0, self.cnt['pe']))
            self.barrier()
            for s in self.scope_sems.pop():
                self.sempool.append(s)
            self.stacks.pop()
            es.close()

    def _need(self, e, deps):
        need = {}
        for (sem, key, val) in deps:
            if key == e and e == 'pe':
                continue
            if self.seen[e].get(key, 0) >= val:
                continue
            if key not in need or need[key][2] < val:
                need[key] = (sem, key, val)
        return list(need.values())

    def _wait(self, e, deps, attach=False):
        need = self._need(e, deps)
        last = None
        if attach and need:
            last = need.pop()
        for (sem, key, val) in need:
            self.eng[e].wait_ge(sem, val)
            self.seen[e][key] = val
        if last is not None:
            self.seen[e][last[1]] = last[2]
        return last

    def op(self, e, fn, reads=(), writes=(), attach=False):
        deps = []
        for b in reads:
            deps += list(b.w.values())
        for b in writes:
            deps += [t for t in b.w.values() if t[1] != e]
            deps += [t for t in b.r.values() if t[1] != e]
        last = self._wait(e, deps, attach=(attach and ATTACH))
        ins = fn(self.eng[e])
        if last is not None:
            ins._wait_ge(last[0], last[2])
        self.cnt[e] += 1
        ins.then_inc(self.sem[e], 1)
        tok = (self.sem[e], e, self.cnt[e])
        for b in reads:
            b.r[e] = tok
        for b in writes:
            b.w = {e: tok}
            b.r = {}
        return ins

    def load(self, q, sb, out_ap, dr, in_ap, part=False):
        deps = list(dr.w.values()) + list(sb.r.values())
        deps += [t for t in sb.w.values() if not (part and t[1] == sb.sem[1])]
        self._wait(q, deps)
        ins = self.eng[q].dma_start(out=out_ap, in_=in_ap)
        sb.sem[2] += 16
        ins.then_inc(sb.sem[0], 16)
        tok = (sb.sem[0], sb.sem[1], sb.sem[2])
        self.pending[tok[1]] = tok
        dr.r[tok[1]] = tok
        if part:
            sb.w[tok[1]] = tok
        else:
            sb.w = {tok[1]: tok}
        sb.r = {}

    def store(self, q, dr, out_ap, sb, in_ap):
        deps = list(sb.w.values()) + list(dr.r.values())
        self._wait(q, deps)
        ins = self.eng[q].dma_start(out=out_ap, in_=in_ap)
        sb.sem[2] += 16
        ins.then_inc(sb.sem[0], 16)
        tok = (sb.sem[0], sb.sem[1], sb.sem[2])
        self.pending[tok[1]] = tok
        sb.r[tok[1]] = tok
        dr.w[tok[1]] = tok

    def barrier(self):
        sp = 'sp'
        deps = [(self.sem[e], e, self.cnt[e]) for e in ('pe', 'act', 'dve', 'pool')
                if self.cnt[e] > 0]
        deps += list(self.pending.values())
        self._wait(sp, deps)
        self.eng[sp].sem_inc(self.sem[sp], 1)
        self.cnt[sp] += 1
        tok = (self.sem[sp], sp, self.cnt[sp])
        for e in ('pe', 'act', 'dve', 'pool'):
            self._wait(e, [tok])
        self.pending = {}
        for d in self.dbufs:
            d.w = {}
            d.r = {}

    def mm(self, ps, out_ap, lhsT_b, lhsT_ap, rhs_b, rhs_ap, start, stop, attach=False):
        return self.op('pe', lambda e: e.matmul(out_ap, lhsT_ap, rhs_ap, start=start, stop=stop),
                       reads=[lhsT_b, rhs_b], writes=[ps], attach=attach)

    def tr(self, ps, out_ap, in_b, in_ap, ident_b, ident_ap):
        return self.op('pe', lambda e: e.transpose(out_ap, in_ap, ident_ap),
                       reads=[in_b, ident_b], writes=[ps])

    def act(self, out_b, out_ap, in_b, in_ap, func, bias=None, scale=None, extra_reads=(),
            accum=None, accum_b=None, attach=False):
        kw = {}
        if bias is not None:
            kw['bias'] = bias
        if scale is not None:
            kw['scale'] = scale
        if accum is not None:
            kw['accum_out'] = accum
        wr = [out_b] + ([accum_b] if accum_b is not None else [])
        return self.op('act', lambda e: e.activation(out=out_ap, in_=in_ap, func=func, **kw),
                       reads=[in_b] + list(extra_reads), writes=wr, attach=attach)

    def tt(self, e, out_b, out_ap, a_b, a_ap, b_b, b_ap, op):
        return self.op(e, lambda en: en.tensor_tensor(out=out_ap, in0=a_ap, in1=b_ap, op=op),
                       reads=[a_b, b_b], writes=[out_b])

    def ts(self, e, out_b, out_ap, a_b, a_ap, s1, s2, op0, op1=None, extra_reads=()):
        if op1 is None:
            f = lambda en: en.tensor_scalar(out=out_ap, in0=a_ap, scalar1=s1, scalar2=None, op0=op0)
        else:
            f = lambda en: en.tensor_scalar(out=out_ap, in0=a_ap, scalar1=s1, scalar2=s2,
                                            op0=op0, op1=op1)
        return self.op(e, f, reads=[a_b] + list(extra_reads), writes=[out_b])

    def stt(self, e, out_b, out_ap, a_b, a_ap, scalar, b_b, b_ap, op0, op1, extra_reads=()):
        return self.op(e, lambda en: en.scalar_tensor_tensor(out=out_ap, in0=a_ap, scalar=scalar,
                                                             in1=b_ap, op0=op0, op1=op1),
                       reads=[a_b, b_b] + list(extra_reads), writes=[out_b])

    def copy(self, e, out_b, out_ap, in_b, in_ap):
        if e == 'act':
            return self.op('act', lambda en: en.copy(out=out_ap, in_=in_ap),
                           reads=[in_b], writes=[out_b])
        return self.op(e, lambda en: en.tensor_copy(out=out_ap, in_=in_ap),
                       reads=[in_b], writes=[out_b])

    def rsqrt(self, b, ap):
        self.act(b, ap, b, ap, AF.Ln)
        self.act(b, ap, b, ap, AF.Exp, scale=-0.5)

    def memset(self, e, b, ap, val):
        return self.op(e, lambda en: en.memset(ap, val), reads=[], writes=[b])


def bc(ap, shape):
    return ap.broadcast_to(list(shape))


def build_program(NB, depth_run, debug=False):
    nc = bass.Bass("TRN2", target_bir_lowering=False)

    def din(name, shape, dt=F32):
        return nc.dram_tensor(name, list(shape), dt, kind="ExternalInput").ap()

    x_in = din("x", [NB, LAT, D])
    ctx_in = din("ctx", [NB, CTX, D])
    cT_in = din("cT", [128, KC, 5])
    ada_w = din("ada_w", [DEPTH, D, 6 * D])
    ada_b_tm = din("ada_b", [DEPTH, 6 * D])
    ada_b_fm = din("ada_b_fm", [DEPTH, 128, 48])
    w_in = din("w_in", [DEPTH, D, 12864])
    b_gate = din("b_gate", [DEPTH, 3072])
    conv_w = din("conv_w", [DEPTH, 128, 24, 5])
    conv_b = din("conv_b", [DEPTH, 128, 24])
    dt_bias = din("ssm_dt_bias", [DEPTH, 64])
    a_log = din("ssm_a_log", [DEPTH, 64])
    ssm_d = din("ssm_d", [DEPTH, 32])
    ssm_nw = din("ssm_norm_w", [DEPTH, 2048])
    w_sso = din("w_ssm_out", [DEPTH, 2048, D])
    gq_norm = din("gqa_q_norm", [DEPTH, 128])
    gk_norm = din("gqa_k_norm", [DEPTH, 128])
    w_gqo = din("w_gqa_out", [DEPTH, D, D])
    dlam = din("diff_lambda", [DEPTH, 256])
    dnw_fm = din("diff_norm_w", [DEPTH, 128, 1])
    w_dfo = din("w_diff_out", [DEPTH, D, D])
    w_o = din("w_o", [DEPTH, D, D])
    ln1_g = din("ln1_g", [DEPTH, D])
    ln1_b = din("ln1_b", [DEPTH, D])
    ffn_wi = din("ffn_w_in", [DEPTH, D, 2 * FFH])
    ffn_wo = din("ffn_w_out", [DEPTH, FFH, D])
    ln2_g = din("ln2_g", [DEPTH, D])
    ln2_b = din("ln2_b", [DEPTH, D])
    consts_in = din("consts", [128, 6, 128])
    ropeG_in = din("ropeG", [128, 16, 2, 128])
    ropeD_in = din("ropeD", [128, 16, 2, 64])
    out = nc.dram_tensor("out", [NB, LAT, D], F32, kind="ExternalOutput").ap()

    def scratch(name, shape, dt):
        return nc.dram_tensor(name, list(shape), dt, kind="Internal").ap()

    xA_d = scratch("xA", [T, D], F32)
    xB_d = scratch("xB", [T, D], F32)
    ybr_d = scratch("ybr", [32 * 128, T], BF16)
    mod_d = scratch("mod_tm", [DEPTH, 5, 6 * D], F32)

    with ExitStack() as es:
        K = Kx(nc, es)
        X_IN = K.dbuf(x_in)
        CTX_IN = K.dbuf(ctx_in)
        OUT = K.dbuf(out)
        XA = K.dbuf(xA_d)
        XB = K.dbuf(xB_d)
        YBR = K.dbuf(ybr_d)
        MODD = K.dbuf(mod_d)
        WD = K.dbuf(None)

        cst = K.tile('cst', [128, 6, 128], F32, dma=True)
        K.load('sp', cst, cst[:], WD, consts_in)
        IDF, MLE, MGT, MGE, MLT, ONESF = (cst[:, i, :] for i in range(6))
        cstb = K.tile('cstb', [128, 2, 128], BF16)
        K.copy('dve', cstb, cstb[:, 0, :], cst, cst[:, 0, :])
        K.copy('dve', cstb, cstb[:, 1, :], cst, cst[:, 5, :])
        IDB = cstb[:, 0, :]
        ONESB = cstb[:, 1, :]
        modT = K.tile('modT', [128, DEPTH, 48, 5], F32)

        with K.scope('adaLN'):
            cT = K.tile('cT', [128, KC, 5], F32, dma=True)
            K.load('sp', cT, cT[:], WD, cT_in)
            scT = K.tile('scT', [128, KC, 5], BF16)
            K.act(scT, scT[:], cT, cT[:], AF.Silu)
            abf = K.tile('abf', [128, DEPTH, 48], F32, dma=True)
            K.load('sp', abf, abf[:], WD, ada_b_fm.rearrange("l p c -> p l c"))
            wts = [K.tile('adaw%d' % i, [128, KC, 512], BF16, dma=True) for i in range(2)]
            abt = [K.tile('abt%d' % i, [5, 512], F32, dma=True) for i in range(2)]
            mo = [K.tile('mo%d' % i, [5, 512], F32, dma=True) for i in range(2)]
            ps_t = [K.psum('ps0t%d' % i, [128, 512], F32) for i in range(2)]
            ps_f = [K.psum('ps0f%d' % i, [128, 4, 5], F32) for i in range(2)]
            n = 0
            for L in range(depth_run):
                for j in range(12):
                    W = wts[n % 2]
                    K.load('pool', W, W[:], WD,
                           ada_w[L].rearrange("(kc p) f -> p kc f", p=128)[:, :, j * 512:(j + 1) * 512])
                    ab = abt[n % 2]
                    K.load('sp', ab, ab[:], WD,
                           ada_b_tm[L:L + 1, j * 512:(j + 1) * 512].broadcast_to([5, 512]))
                    pt = ps_t[n % 2]
                    for kc in range(KC):
                        K.mm(pt, pt[0:5, :], scT, scT[:, kc, :], W, W[:, kc, :], kc == 0, kc == KC - 1)
                    m = mo[n % 2]
                    K.tt('dve', m, m[:], pt, pt[0:5, :], ab, ab[:], ALU.add)
                    K.store('sp', MODD, mod_d[L, :, j * 512:(j + 1) * 512], m, m[:])
                    pf = ps_f[n % 2]
                    for f in range(4):
                        for kc in range(KC):
                            K.mm(pf, pf[:, f, :], W, W[:, kc, f * 128:(f + 1) * 128], scT, scT[:, kc, :],
                                 kc == 0, kc == KC - 1)
                    K.tt('dve', modT, modT[:, L, j * 4:(j + 1) * 4, :], pf, pf[:],
                         abf, bc(abf[:, L, j * 4:(j + 1) * 4, None], [128, 4, 5]), ALU.add)
                    n += 1

        for b in range(NB):
            for L in range(depth_run):
                last = (L == DEPTH - 1)
                lam_init = 0.8 - 0.6 * math.exp(-0.3 * L)
                if L == 0:
                    def xsrc(t0, nt, b=b):
                        if t0 < 2:
                            return CTX_IN, ctx_in[b, t0 * 128:(t0 + nt) * 128, :]
                        return X_IN, x_in[b, (t0 - 2) * 128:(t0 - 2 + nt) * 128, :]
                else:
                    def xsrc(t0, nt):
                        return XA, xA_d[t0 * 128:(t0 + nt) * 128, :]
                with K.scope():
                    layer(K, nc, locals())
    return nc


def layer(K, nc, G):
    b, L, last, lam_init, xsrc = G['b'], G['L'], G['last'], G['lam_init'], G['xsrc']
    WD, MODD, YBR, XA, XB, OUT = G['WD'], G['MODD'], G['YBR'], G['XA'], G['XB'], G['OUT']
    cst, cstb, modT = G['cst'], G['cstb'], G['modT']
    IDF, MLE, MGT, MGE, MLT, ONESF, IDB, ONESB = (G[k] for k in
                                                  ('IDF', 'MLE', 'MGT', 'MGE', 'MLT', 'ONESF', 'IDB', 'ONESB'))
    w_in = G['w_in']
    mod_d, ybr_d, xA_d, xB_d, out = G['mod_d'], G['ybr_d'], G['xA_d'], G['xB_d'], G['out']

    def wsl(c0, c1):
        return w_in[L].rearrange("(kc p) f -> p kc f", p=128)[:, :, c0:c1]

    sc1 = K.tile('sc1', [128, 2, 8], F32)
    sh1 = K.tile('sh1', [128, 2, 8], F32)
    sc2 = K.tile('sc2', [128, 2, 8], F32)
    sh2 = K.tile('sh2', [128, 2, 8], F32)
    for j, m in enumerate((b, 4)):
        K.copy('dve', sh1, sh1[:, j, :], modT, modT[:, L, 0:8, m])
        K.ts('dve', sc1, sc1[:, j, :], modT, modT[:, L, 8:16, m], 1.0, None, ALU.add)
        K.copy('dve', sh2, sh2[:, j, :], modT, modT[:, L, 24:32, m])
        K.ts('dve', sc2, sc2[:, j, :], modT, modT[:, L, 32:40, m], 1.0, None, ALU.add)
    hT = K.tile('hT', [128, KC, T], BF16)

    phase_A(K, G, hT, sc1, sh1, xsrc)
    phase_ssm(K, G, hT, wsl)
    phase_gqa(K, G, hT, wsl)
    for hh in range(2):
        phase_diff(K, G, hT, wsl, hh)
    for th in range(2):
        phase_merge(K, G, hT, wsl, th)
    phase_A(K, G, hT, sc2, sh2, lambda t0, nt: (XB, xB_d[t0 * 128:(t0 + nt) * 128, :]))
    phase_ffn(K, G, hT)


def load_rows(K, G, kmod, lg, lb):
    L, b = G['L'], G['b']
    rows = K.tile('rows', [128, 4, D], F32, dma=True)
    for j, m in enumerate((b, 4)):
        K.load('sp', rows, rows[:, j, :], G['MODD'],
               G['mod_d'][L, m:m + 1, kmod * D:(kmod + 1) * D].broadcast_to([128, D]), part=(j > 0))
    for j, src in enumerate((lg, lb)):
        K.load('sp', rows, rows[:, 2 + j, :], G['WD'], src[L:L + 1, :].broadcast_to([128, D]), part=True)
    return rows


def phase_A(K, G, hT, sc1, sh1, xsrc):
    IDF = G['IDF']
    cst = G['cst']
    with K.scope('A'):
        ps = [K.psum('psA%d' % i, [128, 512], F32) for i in range(4)]
        xg = [K.tile('xg%d' % i, [128, 4, D], F32, dma=True) for i in range(2)]
        groups = [(0, 2, 1), (2, 4, 0), (6, 4, 0), (10, 4, 0), (14, 4, 0)]
        n = 0
        for gi, (t0, nt, isc) in enumerate(groups):
            xb = xg[gi % 2]
            db, ap = xsrc(t0, nt)
            K.load('sp', xb, xb[:, 0:nt, :], db, ap.rearrange("(n p) d -> p n d", p=128))
            for kc in range(KC):
                p = ps[n % 4]
                for j in range(nt):
                    K.tr(p, p[:, j * 128:(j + 1) * 128], xb, xb[:, j, kc * 128:(kc + 1) * 128], cst, IDF)
                o = hT[:, kc, t0 * 128:(t0 + nt) * 128]
                if n % 2 == 0:
                    K.ts('dve', hT, o, p, p[:, 0:nt * 128], sc1[:, isc, kc:kc + 1], sh1[:, isc, kc:kc + 1],
                         ALU.mult, ALU.add, extra_reads=[sc1, sh1])
                else:
                    K.act(hT, o, p, p[:, 0:nt * 128], AF.Identity, bias=sh1[:, isc, kc:kc + 1],
                          scale=sc1[:, isc, kc:kc + 1], extra_reads=[sc1, sh1])
                n += 1


def proj_tm(K, ps, hT, tt, W, c0, ncols):
    for kc in range(KC):
        K.mm(ps, ps[:, 0:ncols], hT, hT[:, kc, tt * 128:(tt + 1) * 128], W, W[:, kc, c0:c0 + ncols],
             kc == 0, kc == KC - 1)


def phase_ssm(K, G, hT, wsl):
    L = G['L']
    WD, YBR, ybr_d = G['WD'], G['YBR'], G['ybr_d']
    cst, cstb = G['cst'], G['cstb']
    IDB, ONESF = G['IDB'], G['ONESF']
    MLE, MGT, MGE, MLT = G['MLE'], G['MGT'], G['MGE'], G['MLT']
    with K.scope():
        dt = K.tile('dt', [128, NT, 64], F32)
        adt = K.tile('adt', [128, NT, 64], F32)
        prm = K.tile('prm', [128, 3, 64], F32, dma=True)
        K.load('sp', prm, prm[:, 0, :], WD, G['dt_bias'][L:L + 1, :].broadcast_to([128, 64]))
        K.load('sp', prm, prm[:, 1, :], WD, G['a_log'][L:L + 1, :].broadcast_to([128, 64]), part=True)
        K.load('sp', prm, prm[:, 2, 0:32], WD, G['ssm_d'][L:L + 1, :].broadcast_to([128, 32]), part=True)
        nw = K.tile('nw', [128, 2048], F32, dma=True)
        K.load('sp', nw, nw[:], WD, G['ssm_nw'][L:L + 1, :].broadcast_to([128, 2048]))
        cw = K.tile('cw', [128, 24, 5], F32, dma=True)
        K.load('sp', cw, cw[:], WD, G['conv_w'][L])
        cb = K.tile('cb', [128, 24], F32, dma=True)
        K.load('sp', cb, cb[:], WD, G['conv_b'][L])
        aneg = K.tile('aneg', [128, 64], F32)
        K.act(aneg, aneg[:], prm, prm[:, 1, :], AF.Exp)
        K.ts('dve', aneg, aneg[:], aneg, aneg[:], -1.0, None, ALU.mult)
        with K.scope('ssm_dt'):
            Wdt = K.tile('Wdt', [128, KC, 64], BF16, dma=True)
            K.load('pool', Wdt, Wdt[:], WD, wsl(O_DT, O_DT + 64))
            psd = [K.psum('psd%d' % i, [128, 64], F32) for i in range(2)]
            tmp = [K.tile('dtt%d' % i, [128, 64], F32) for i in range(2)]
            for tt in range(NT):
                p = psd[tt % 2]
                proj_tm(K, p, hT, tt, Wdt, 0, 64)
                t1 = tmp[tt % 2]
                K.tt('dve', t1, t1[:], p, p[:], prm, prm[:, 0, :], ALU.add)
                K.act(t1, t1[:], t1, t1[:], AF.Exp)
                K.act(dt, dt[:, tt, :], t1, t1[:], AF.Ln, bias=1.0)
            K.tt('dve', adt, adt[:], dt, dt[:], aneg, bc(aneg[:, None, :], [128, NT, 64]), ALU.mult)

        for g in range(4):
            with K.scope():
                ssm_group(K, G, hT, wsl, g, dt, adt, prm, nw, cw, cb)


def ssm_group(K, G, hT, wsl, g, dt, adt, prm, nw, cw, cb):
    WD, YBR, ybr_d = G['WD'], G['YBR'], G['ybr_d']
    cst, cstb = G['cst'], G['cstb']
    IDB, ONESF = G['IDB'], G['ONESF']
    MLE, MGT, MGE, MLT = G['MLE'], G['MGT'], G['MGE'], G['MLT']
    PADW = 2316
    xs_tm = K.tile('xs_tm', [128, NT, 512], BF16)
    B_tm = K.tile('B_tm', [128, NT, 128], BF16)
    BT = K.tile('BT', [128, T], BF16)
    CT = K.tile('CT', [128, T], BF16)
    Wz = K.tile('Wz', [128, KC, 512], BF16, dma=True)
    with K.scope('ssm_B1'):
        W6 = K.tile('W6', [128, KC, 768], BF16, dma=True)
        K.load('pool', W6, W6[:, :, 0:512], WD, wsl(O_XBC + g * 512, O_XBC + (g + 1) * 512))
        K.load('pool', W6, W6[:, :, 512:640], WD,
               wsl(O_XBC + 2048 + g * 128, O_XBC + 2048 + (g + 1) * 128), part=True)
        K.load('pool', W6, W6[:, :, 640:768], WD,
               wsl(O_XBC + 2560 + g * 128, O_XBC + 2560 + (g + 1) * 128), part=True)
        K.load('pool', Wz, Wz[:], WD, wsl(O_Z + g * 512, O_Z + (g + 1) * 512))
        cchunk = [g * 4 + 0, g * 4 + 1, g * 4 + 2, g * 4 + 3, 16 + g, 20 + g]
        xcT = K.tile('xcT', [128, 4, T], BF16)
        xpad = [K.tile('xpad%d' % i, [128, PADW], BF16) for i in range(2)]
        dg = [K.tile('dg%d' % i, [128, 5, 128], BF16) for i in range(2)]
        for i in range(2):
            K.memset('pool', xpad[i], xpad[i][:], 0.0)
        psp = [K.psum('psp%d' % i, [128, 512], F32) for i in range(3)]
        psc = [K.psum('psc%d' % i, [128, 512], F32) for i in range(2)]
        pst = [K.psum('pst%d' % i, [128, 512], BF16) for i in range(2)]
        n = 0
        m = 0
        for fc in range(6):
            xp = xpad[fc % 2]
            dgm = dg[fc % 2]
            cc = cchunk[fc]
            for j in range(5):
                K.ts('dve', dgm, dgm[:, j, :], cstb, IDB, cw[:, cc, j:j + 1], None, ALU.mult, extra_reads=[cw])
            for (a0, a1) in TB:
                p = psp[n % 3]
                n += 1
                for kc in range(KC):
                    K.mm(p, p[:, 0:a1 - a0], W6, W6[:, kc, fc * 128:(fc + 1) * 128], hT, hT[:, kc, a0:a1],
                         kc == 0, kc == KC - 1)
                off = 2 if a0 < 256 else 6
                K.copy('act', xp, xp[:, a0 + off:a1 + off], p, p[:, 0:a1 - a0])
            for (a0, a1) in TB:
                off = 2 if a0 < 256 else 6
                na = a1 - a0
                pc_ = psc[m % 2]
                m += 1
                for j in range(5):
                    K.mm(pc_, pc_[:, 0:na], dgm, dgm[:, j, :], xp, xp[:, a0 + off + j - 2:a1 + off + j - 2],
                         j == 0, j == 4)
                if fc < 4:
                    ob, o_ = xcT, xcT[:, fc, a0:a1]
                elif fc == 4:
                    ob, o_ = BT, BT[:, a0:a1]
                else:
                    ob, o_ = CT, CT[:, a0:a1]
                K.act(ob, o_, pc_, pc_[:, 0:na], AF.Silu, bias=cb[:, cc:cc + 1], extra_reads=[cb])
        for tt in range(NT):
            p = pst[tt % 2]
            for fc in range(4):
                K.tr(p, p[:, fc * 128:(fc + 1) * 128], xcT, xcT[:, fc, tt * 128:(tt + 1) * 128], cstb, IDB)
            K.copy('dve' if tt % 2 else 'act', xs_tm, xs_tm[:, tt, :], p, p[:])
        for t4 in range(0, NT, 4):
            nt = min(4, NT - t4)
            p = pst[(t4 // 4) % 2]
            for j in range(nt):
                K.tr(p, p[:, j * 128:(j + 1) * 128], BT, BT[:, (t4 + j) * 128:(t4 + j + 1) * 128], cstb, IDB)
            K.copy('dve', B_tm, B_tm[:, t4:t4 + nt, :], p,
                   p[:, 0:nt * 128].rearrange("p (n c) -> p n c", c=128))

    yacc = K.tile('yacc', [128, NT, 512], F32)
    with K.scope('ssm_sweep'):
        S = K.tile('S', [128, 512], F32)
        Sb = K.tile('Sb', [128, 512], BF16)
        Xb = [K.tile('X%d' % i, [128, 8, 128], F32) for i in range(2)]
        Eb = [K.tile('E%d' % i, [128, 8, 128], F32) for i in range(2)]
        MTb = [K.tile('MT%d' % i, [128, 8, 128], BF16) for i in range(2)]
        Gmb = [K.tile('Gm%d' % i, [128, 128], F32) for i in range(2)]
        xdtb = [K.tile('xdt%d' % i, [128, 8, 64], BF16) for i in range(2)]
        xwb = [K.tile('xw%d' % i, [128, 8, 64], BF16) for i in range(2)]
        smb = [K.tile('sm%d' % i, [128, 16], F32) for i in range(2)]
        tmpb = [K.tile('yt%d' % i, [128, 512], F32) for i in range(2)]
        pG = K.psum('pG', [128, 128], F32)
        pD = [K.psum('pD%d' % i, [128, 512], F32) for i in range(2)]
        pY = K.psum('pY', [128, 512], F32)
        pYo = K.psum('pYo', [128, 512], F32)
        pS = K.psum('pS', [128, 512], F32)
        pc = K.psum('pc', [128, 16], F32)
        n = 0
        for d in range(2):
            Ma, Mb, Mg = (MLE, MGT, MLE) if d == 0 else (MGE, MLT, MGE)
            endcol = 127 if d == 0 else 0
            order = list(range(NT)) if d == 0 else [1, 0] + list(range(NT - 1, 1, -1))
            hs = slice(d * 32 + g * 8, d * 32 + g * 8 + 8)
            K.memset('dve', S, S[:], 0.0)
            K.memset('dve', Sb, Sb[:], 0.0)
            def stage1(c, i):
                tok = slice(c * 128, (c + 1) * 128)
                X, E, MT, Gm, xdt, xw, sm = Xb[i], Eb[i], MTb[i], Gmb[i], xdtb[i], xwb[i], smb[i]
                K.mm(pG, pG[:], BT, BT[:, tok], CT, CT[:, tok], True, True)
                K.tt('dve', X, X[:], adt, bc(adt[:, c, hs, None], [128, 8, 128]),
                     cst, bc(Ma[:, None, :], [128, 8, 128]), ALU.mult)
                K.tt('dve', Gm, Gm[:], pG, pG[:], cst, Mg, ALU.mult)
                K.mm(pc, pc[:, 0:8], cst, Ma, adt, adt[:, c, hs], True, True)
                K.mm(pc, pc[:, 8:16], cst, ONESF, adt, adt[:, c, hs], True, True)
                K.act(sm, sm[:], pc, pc[:], AF.Exp)
                K.tt('pool', xdt, xdt[:], xs_tm, xs_tm[:, c, :].rearrange("p (h q) -> p h q", q=64),
                     dt, bc(dt[:, c, hs, None], [128, 8, 64]), ALU.mult)

            def stage1b(c, i):
                X, E, MT, Gm, xdt, xw, sm = Xb[i], Eb[i], MTb[i], Gmb[i], xdtb[i], xwb[i], smb[i]
                for hh in range(2):
                    K.mm(pD[hh], pD[hh][:], cst, Mb, X,
                         X[:, hh * 4:(hh + 1) * 4, :].rearrange("p h l -> p (h l)"), True, True)
                    K.act(E, E[:, hh * 4:(hh + 1) * 4, :].rearrange("p h l -> p (h l)"), pD[hh], pD[hh][:], AF.Exp)
                K.tt('dve', MT, MT[:], E, E[:], Gm, bc(Gm[:, None, :], [128, 8, 128]), ALU.mult)
                K.tt('pool', xw, xw[:], xdt, xdt[:], E, bc(E[:, :, endcol:endcol + 1], [128, 8, 64]), ALU.mult)

            def stage2(c, i):
                tok = slice(c * 128, (c + 1) * 128)
                MT, xdt, xw, sm, ytmp = MTb[i], xdtb[i], xwb[i], smb[i], tmpb[i]
                K.mm(pYo, pYo[:], CT, CT[:, tok], Sb, Sb[:], True, True)
                K.mm(pS, pS[:], B_tm, B_tm[:, c, :], xw, xw[:].rearrange("p h q -> p (h q)"), True, True)
                for h in range(8):
                    K.mm(pY, pY[:, h * 64:(h + 1) * 64], MT, MT[:, h, :], xdt, xdt[:, h, :], True, True)
                K.tt('dve', S, S[:].rearrange("p (h q) -> p h q", q=64),
                     S, S[:].rearrange("p (h q) -> p h q", q=64),
                     sm, bc(sm[:, 8:16, None], [128, 8, 64]), ALU.mult)
                K.tt('dve', S, S[:], S, S[:], pS, pS[:], ALU.add)
                K.copy('act', Sb, Sb[:], S, S[:])
                K.tt('dve', ytmp, ytmp[:].rearrange("p (h q) -> p h q", q=64),
                     pYo, pYo[:].rearrange("p (h q) -> p h q", q=64),
                     sm, bc(sm[:, 0:8, None], [128, 8, 64]), ALU.mult)
                if d == 0:
                    K.tt('dve', yacc, yacc[:, c, :], ytmp, ytmp[:], pY, pY[:], ALU.add)
                else:
                    K.tt('dve', ytmp, ytmp[:], ytmp, ytmp[:], pY, pY[:], ALU.add)
                    K.tt('pool', yacc, yacc[:, c, :], yacc, yacc[:, c, :], ytmp, ytmp[:], ALU.add)

            for t in range(len(order) + 1):
                if t < len(order):
                    stage1(order[t], t % 2)
                if t >= 1:
                    stage2(order[t - 1], (t - 1) % 2)
                if t < len(order):
                    stage1b(order[t], t % 2)

    with K.scope('ssm_post'):
        ysT = K.tile('ysT', [128, 4, T], BF16, dma=True)
        pz = [K.psum('pz%d' % i, [128, 512], F32) for i in range(2)]
        pt = [K.psum('pt%d' % i, [128, 512], BF16) for i in range(2)]
        zs = [K.tile('zs%d' % i, [128, 512], F32) for i in range(2)]
        yb = [K.tile('yb%d' % i, [128, 512], F32) for i in range(2)]
        ynb = [K.tile('yn%d' % i, [128, 512], BF16) for i in range(2)]
        jk = [K.tile('jk%d' % i, [128, 512], F32) for i in range(2)]
        ssb = [K.tile('ss%d' % i, [128, 2], F32) for i in range(2)]
        for tt in range(NT):
            i = tt % 2
            p, z, y = pz[i], zs[i], yb[i]
            proj_tm(K, p, hT, tt, Wz, 0, 512)
            K.act(z, z[:], p, p[:], AF.Silu)
            K.tt('pool', y, y[:].rearrange("p (h q) -> p h q", q=64),
                 xs_tm, xs_tm[:, tt, :].rearrange("p (h q) -> p h q", q=64),
                 prm, bc(prm[:, 2, g * 8:(g + 1) * 8, None], [128, 8, 64]), ALU.mult)
            K.tt('dve', y, y[:], y, y[:], yacc, yacc[:, tt, :], ALU.add)
            K.tt('dve', yacc, yacc[:, tt, :], y, y[:], z, z[:], ALU.mult)
        for tt in range(NT):
            i = tt % 2
            yn, ss = ynb[i], ssb[i]
            K.act(jk[i], jk[i][:], yacc, yacc[:, tt, :], AF.Square, accum=ss[:, 0:1], accum_b=ss)
            K.ts('dve', ss, ss[:, 1:2], ss, ss[:, 0:1], 1.0 / 512, EPS, ALU.mult, ALU.add)
            K.rsqrt(ss, ss[:, 1:2])
            K.stt('dve', yn, yn[:], yacc, yacc[:, tt, :], ss[:, 1:2], nw, nw[:, g * 512:(g + 1) * 512],
                  ALU.mult, ALU.mult, extra_reads=[ss])
            q = pt[i]
            for fc in range(4):
                K.tr(q, q[:, fc * 128:(fc + 1) * 128], yn, yn[:, fc * 128:(fc + 1) * 128], cstb, IDB)
            K.copy('dve' if tt % 2 else 'act', ysT, ysT[:, :, tt * 128:(tt + 1) * 128], q,
                   q[:].rearrange("p (c t) -> p c t", t=128))
        K.store('sp', YBR, ybr_d[g * 512:(g + 1) * 512, :].rearrange("(c p) t -> p c t", p=128),
                ysT, ysT[:])


def rope(K, e, dst, dst_ap, src, src_ap, tab, cos_ap, sin_ap, tmp, tmp_ap, nh, hd, e_first=None):
    q = hd // 4
    K.tt(e_first or e, dst, dst_ap, src, src_ap, tab, bc(cos_ap[:, None, :], [128, nh, hd]), ALU.mult)
    for a in range(2):
        for s in range(2):
            o0 = a * 2 * q + s * q
            i0 = a * 2 * q + (1 - s) * q
            K.tt(e, tmp, tmp_ap[:, :, o0:o0 + q], src, src_ap[:, :, i0:i0 + q],
                 tab, bc(sin_ap[:, None, o0:o0 + q], [128, nh, q]), ALU.mult)
    K.tt(e, dst, dst_ap, dst, dst_ap, tmp, tmp_ap, ALU.add)


def attention_core(K, G, groups, scale, kT, qT, v_tm, pfx):
    cstb, ONESB = G['cstb'], G['ONESB']
    pS = [K.psum(pfx + 'S%d' % i, [128, 2, 512], F32) for i in range(2)]
    pO = [K.psum(pfx + 'O%d' % i, [128, 512], F32) for i in range(2)]
    pZ = [K.psum(pfx + 'Z%d' % i, [128, 512], F32) for i in range(2)]
    PT = [K.tile(pfx + 'PT%d' % i, [128, 2, 512], BF16) for i in range(3)]
    items = []
    for gi, g in enumerate(groups):
        assert (g['k1'] - g['k0']) % 2 == 0
        for kt in range(g['k0'], g['k1'], 2):
            items.append((gi, g, kt))
    LA = 1
    deferred = []
    for idx in range(len(items) + LA):
        while deferred and deferred[0][0] <= idx:
            deferred.pop(0)[1]()
        if idx < len(items):
            gi, g, kt = items[idx]
            nq = g['nq']
            s_, pt = pS[idx % 2], PT[idx % 3]
            for a in range(2):
                K.mm(s_, s_[:, a, 0:nq], kT, g['k'](kt + a), qT, g['q'], True, True, attach=True)
            K.act(pt, pt[:, :, 0:nq], s_, s_[:, :, 0:nq], AF.Exp, scale=scale, attach=True)
        j = idx - LA
        if j >= 0:
            gi, g, kt = items[j]
            nq = g['nq']
            pt = PT[j % 3]
            o, z = pO[gi % 2], pZ[gi % 2]
            for a in range(2):
                first = (kt + a == g['k0'])
                last = (kt + a == g['k1'] - 1)
                K.mm(o, o[:, 0:nq], v_tm, g['v'](kt + a), pt, pt[:, a, 0:nq], first, last, attach=True)
                K.mm(z, z[:, 0:nq], cstb, ONESB, pt, pt[:, a, 0:nq], first, last, attach=True)
            if kt + 2 >= g['k1']:
                cont = g['fin'](o, z)
                if cont is not None:
                    deferred.append((idx + 3, cont))
    for d in deferred:
        d[1]()


QBLOCKS = [(0, 256, 0, 2)] + [(256 + 512 * i, 256 + 512 * (i + 1), 0, NT) for i in range(4)]


def phase_gqa(K, G, hT, wsl):
    L = G['L']
    WD, YBR, ybr_d = G['WD'], G['YBR'], G['ybr_d']
    cst, cstb, IDB = G['cst'], G['cstb'], G['IDB']
    with K.scope():
        ropeG = K.tile('ropeG', [128, 16, 2, 128], F32, dma=True)
        K.load('sp', ropeG, ropeG[:], WD, G['ropeG_in'])
        qT = K.tile('qT', [128, 8, T], BF16)
        kT = K.tile('kT', [128, 2, T], BF16)
        v_tm = K.tile('v_tm', [128, NT, 256], BF16)
        gn = K.tile('gn', [128, 2, 128], F32, dma=True)
        K.load('sp', gn, gn[:, 0, :], WD, G['gq_norm'][L:L + 1, :].broadcast_to([128, 128]))
        K.load('sp', gn, gn[:, 1, :], WD, G['gk_norm'][L:L + 1, :].broadcast_to([128, 128]), part=True)
        with K.scope('gqa_proj'):
            W = K.tile('Wg', [128, KC, 1536], BF16, dma=True)
            K.load('pool', W, W[:, :, 0:512], WD, wsl(O_GQ, O_GQ + 512))
            K.load('pool', W, W[:, :, 512:1024], WD, wsl(O_GQ + 512, O_GQ + 1024), part=True)
            K.load('pool', W, W[:, :, 1024:1536], WD, wsl(O_GK, O_GK + 512), part=True)
            pq = [K.psum('pq%d' % i, [128, 512], F32) for i in range(3)]
            ptr = [K.psum('ptr%d' % i, [128, 512], BF16) for i in range(2)]
            qf = [K.tile('qf%d' % i, [128, 10, 128], F32) for i in range(2)]
            sq = [K.tile('sq%d' % i, [128, 10, 128], F32) for i in range(2)]
            qr = [K.tile('qr%d' % i, [128, 10, 128], F32) for i in range(2)]
            qb_ = [K.tile('qb%d' % i, [128, 10, 128], BF16) for i in range(2)]
            ssq = [K.tile('ssq%d' % i, [128, 10], F32) for i in range(2)]
            def stA(tt):
                i = tt % 2
                f, s2, r, qb16, ss = qf[i], sq[i], qr[i], qb_[i], ssq[i]
                for j in range(3):
                    proj_tm(K, pq[j], hT, tt, W, j * 512, 512)
                K.copy('act', f, f[:, 0:4, :], pq[0], pq[0][:].rearrange("p (h d) -> p h d", d=128))
                K.copy('act', f, f[:, 4:8, :], pq[1], pq[1][:].rearrange("p (h d) -> p h d", d=128))
                K.copy('act', f, f[:, 8:10, :], pq[2], pq[2][:, 0:256].rearrange("p (h d) -> p h d", d=128))
                K.copy('act', v_tm, v_tm[:, tt, :], pq[2], pq[2][:, 256:512])
                K.act(s2, s2[:], f, f[:], AF.Square)
                K.op('dve', lambda e: e.reduce_sum(out=ss[:], in_=s2[:], axis=AX.X), reads=[s2], writes=[ss])
                K.ts('dve', ss, ss[:], ss, ss[:], 1.0 / 128, EPS, ALU.mult, ALU.add)
                K.rsqrt(ss, ss[:])
                K.tt('dve', f, f[:], f, f[:], ss, bc(ss[:, :, None], [128, 10, 128]), ALU.mult)
                K.tt('dve', f, f[:, 0:8, :], f, f[:, 0:8, :], gn, bc(gn[:, 0:1, :], [128, 8, 128]), ALU.mult)
                K.tt('dve', f, f[:, 8:10, :], f, f[:, 8:10, :], gn, bc(gn[:, 1:2, :], [128, 2, 128]), ALU.mult)
                if tt >= 2:
                    rope(K, 'pool', r, r[:], f, f[:], ropeG, ropeG[:, tt - 2, 0, :], ropeG[:, tt - 2, 1, :],
                         s2, s2[:], 10, 128, e_first='dve')
                    K.copy('dve', qb16, qb16[:], r, r[:])
                else:
                    K.copy('dve', qb16, qb16[:], f, f[:])

            def stB(tt):
                qb16 = qb_[tt % 2]
                for half in range(3):
                    hh = [(0, 4), (4, 8), (8, 10)][half]
                    p = ptr[(tt * 3 + half) % 2]
                    for h in range(hh[0], hh[1]):
                        K.tr(p, p[:, (h - hh[0]) * 128:(h - hh[0] + 1) * 128], qb16, qb16[:, h, :], cstb, IDB)
                    nh = hh[1] - hh[0]
                    src = p[:, 0:nh * 128].rearrange("p (h t) -> p h t", t=128)
                    if half < 2:
                        K.copy('act', qT, qT[:, hh[0]:hh[1], tt * 128:(tt + 1) * 128], p, src)
                    else:
                        K.copy('act', kT, kT[:, :, tt * 128:(tt + 1) * 128], p, src)

            for tt in range(NT + 1):
                if tt < NT:
                    stA(tt)
                if tt >= 1:
                    stB(tt - 1)
        with K.scope('gqa_attn'):
            og = [K.tile('og%d' % i, [128, 512], BF16, dma=True) for i in range(3)]
            rz = [K.tile('rz%d' % i, [128, 512], F32) for i in range(2)]
            cnt = [0]

            def mkfin(m, q0, q1):
                def fin(o, z):
                    nq = q1 - q0
                    i = cnt[0]
                    cnt[0] += 1
                    r = rz[i % 2]
                    ob = og[i % 3]
                    K.op('dve', lambda e: e.reciprocal(out=r[:, 0:nq], in_=z[:, 0:nq]), reads=[z], writes=[r])
                    K.tt('dve', ob, ob[:, 0:nq], o, o[:, 0:nq], r, r[:, 0:nq], ALU.mult)
                    K.store('sp', YBR, ybr_d[(16 + m) * 128:(17 + m) * 128, q0:q1], ob, ob[:, 0:nq])
                return fin

            groups = []
            for m in range(8):
                for (q0, q1, k0, k1) in QBLOCKS:
                    groups.append(dict(
                        k=(lambda kt, m=m: kT[:, m // 4, kt * 128:(kt + 1) * 128]),
                        q=qT[:, m, q0:q1],
                        v=(lambda kt, m=m: v_tm[:, kt, (m // 4) * 128:(m // 4 + 1) * 128]),
                        nq=q1 - q0, k0=k0, k1=k1, fin=mkfin(m, q0, q1)))
            attention_core(K, G, groups, 128 ** -0.5, kT, qT, v_tm, 'a')


def phase_diff(K, G, hT, wsl, hh):
    L, lam_init = G['L'], G['lam_init']
    WD, YBR, ybr_d = G['WD'], G['YBR'], G['ybr_d']
    cst, cstb, IDB, ONESF = G['cst'], G['cstb'], G['IDB'], G['ONESF']
    with K.scope():
        ropeD = K.tile('ropeD', [128, 16, 2, 64], F32, dma=True)
        K.load('sp', ropeD, ropeD[:], WD, G['ropeD_in'])
        qT = K.tile('dqT', [128, 4, T], BF16)
        kT = K.tile('dkT', [128, 4, T], BF16)
        v_tm = K.tile('dv_tm', [128, NT, 512], BF16)
        lm = K.tile('lm', [128, 4, 64], F32, dma=True)
        K.load('sp', lm, lm[:].rearrange("p a b -> p (a b)"), WD, G['dlam'][L:L + 1, :].broadcast_to([128, 256]))
        lt = K.tile('lt', [128, 2, 64], F32)
        ls = K.tile('ls', [128, 4], F32)
        K.tt('dve', lt, lt[:, 0, :], lm, lm[:, 0, :], lm, lm[:, 1, :], ALU.mult)
        K.tt('dve', lt, lt[:, 1, :], lm, lm[:, 2, :], lm, lm[:, 3, :], ALU.mult)
        K.op('dve', lambda e: e.reduce_sum(out=ls[:, 0:2], in_=lt[:], axis=AX.X), reads=[lt], writes=[ls])
        K.act(ls, ls[:, 0:2], ls, ls[:, 0:2], AF.Exp)
        K.tt('dve', ls, ls[:, 2:3], ls, ls[:, 0:1], ls, ls[:, 1:2], ALU.subtract)
        K.ts('dve', ls, ls[:, 3:4], ls, ls[:, 2:3], lam_init, -1.0, ALU.add, ALU.mult)
        dnw = K.tile('dnw', [128, 1], F32, dma=True)
        K.load('sp', dnw, dnw[:], WD, G['dnw_fm'][L])
        K.ts('dve', dnw, dnw[:], dnw, dnw[:], 1.0 - lam_init, None, ALU.mult)
        with K.scope('diff_proj'):
            Wb = [K.tile('Wd%d' % i, [128, KC, 512], BF16, dma=True) for i in range(3)]
            for j, c0 in enumerate((O_DQ, O_DK, O_DV)):
                K.load('pool', Wb[j], Wb[j][:], WD, wsl(c0 + hh * 512, c0 + (hh + 1) * 512))
            pq = [K.psum('dpq%d' % i, [128, 512], F32) for i in range(3)]
            ptr = [K.psum('dptr%d' % i, [128, 512], BF16) for i in range(2)]
            qf = [K.tile('dqf%d' % i, [128, 8, 64], F32) for i in range(4)]
            tm = [K.tile('dtm%d' % i, [128, 8, 64], F32) for i in range(4)]
            qr = [K.tile('dqr%d' % i, [128, 8, 64], F32) for i in range(4)]
            q16 = [K.tile('dq16%d' % i, [128, 512], BF16) for i in range(4)]
            nn = [0]

            def stA(tt):
                for j in range(3):
                    p = pq[nn[0] % 3]
                    nn[0] += 1
                    proj_tm(K, p, hT, tt, Wb[j], 0, 512)
                    if j == 2:
                        K.copy('act', v_tm, v_tm[:, tt, :], p, p[:])
                        continue
                    i = (tt % 2) * 2 + j
                    f, t_, r, b16 = qf[i], tm[i], qr[i], q16[i]
                    if tt >= 2:
                        K.copy('act', f, f[:].rearrange("p h d -> p (h d)"), p, p[:])
                        rope(K, 'pool' if j % 2 else 'dve', r, r[:], f, f[:], ropeD, ropeD[:, tt - 2, 0, :],
                             ropeD[:, tt - 2, 1, :], t_, t_[:], 8, 64)
                        K.copy('act', b16, b16[:], r, r[:].rearrange("p h d -> p (h d)"))
                    else:
                        K.copy('act', b16, b16[:], p, p[:])

            def stB(tt):
                for j in range(2):
                    i = (tt % 2) * 2 + j
                    b16 = q16[i]
                    q = ptr[j]
                    for c in range(4):
                        K.tr(q, q[:, c * 128:(c + 1) * 128], b16, b16[:, c * 128:(c + 1) * 128], cstb, IDB)
                    dst = qT if j == 0 else kT
                    K.copy('dve', dst, dst[:, :, tt * 128:(tt + 1) * 128], q,
                           q[:].rearrange("p (c t) -> p c t", t=128))

            for tt in range(NT + 1):
                if tt < NT:
                    stA(tt)
                if tt >= 1:
                    stB(tt - 1)
        with K.scope('diff_attn'):
            o1 = [K.tile('o1_%d' % i, [128, 512], F32) for i in range(2)]
            o2 = [K.tile('o2_%d' % i, [128, 512], F32) for i in range(2)]
            osq = [K.tile('osq%d' % i, [128, 512], F32) for i in range(2)]
            rs = [K.tile('rs%d' % i, [128, 512], F32) for i in range(2)]
            od = [K.tile('od%d' % i, [128, 512], BF16, dma=True) for i in range(3)]
            rz = [K.tile('drz%d' % i, [128, 512], F32) for i in range(2)]
            cnt = [0]

            def mkfin(h, j, q0, q1):
                def fin(o, z):
                    nq = q1 - q0
                    i = cnt[0] % 2
                    r = rz[j]
                    K.op('dve', lambda e: e.reciprocal(out=r[:, 0:nq], in_=z[:, 0:nq]), reads=[z], writes=[r])
                    if j == 0:
                        K.tt('dve', o1[i], o1[i][:, 0:nq], o, o[:, 0:nq], r, r[:, 0:nq], ALU.mult)
                        return
                    K.tt('dve', o2[i], o2[i][:, 0:nq], o, o[:, 0:nq], r, r[:, 0:nq], ALU.mult)
                    K.stt('dve', o1[i], o1[i][:, 0:nq], o2[i], o2[i][:, 0:nq], ls[:, 3:4], o1[i], o1[i][:, 0:nq],
                          ALU.mult, ALU.add, extra_reads=[ls])
                    K.tt('pool', osq[i], osq[i][:, 0:nq], o1[i], o1[i][:, 0:nq], o1[i], o1[i][:, 0:nq], ALU.mult)
                    k = cnt[0] % 3
                    cnt[0] += 1

                    def cont():
                        K.mm(z, z[:, 0:nq], cst, ONESF, osq[i], osq[i][:, 0:nq], True, True)
                        K.ts('dve', rs[i], rs[i][:, 0:nq], z, z[:, 0:nq], 1.0 / 128, EPS, ALU.mult, ALU.add)
                        K.rsqrt(rs[i], rs[i][:, 0:nq])
                        K.stt('dve', od[k], od[k][:, 0:nq], o1[i], o1[i][:, 0:nq], dnw[:, 0:1], rs[i], rs[i][:, 0:nq],
                              ALU.mult, ALU.mult, extra_reads=[dnw])
                        hg = hh * 4 + h
                        K.store('sp', YBR, ybr_d[(24 + hg) * 128:(25 + hg) * 128, q0:q1], od[k], od[k][:, 0:nq])
                    return cont
                return fin

            groups = []
            for h in range(4):
                for (q0, q1, k0, k1) in QBLOCKS:
                    for j in range(2):
                        ps_ = slice(j * 64, (j + 1) * 64)
                        groups.append(dict(
                            k=(lambda kt, h=h, ps_=ps_: kT[ps_, h, kt * 128:(kt + 1) * 128]),
                            q=qT[ps_, h, q0:q1],
                            v=(lambda kt, h=h: v_tm[:, kt, h * 128:(h + 1) * 128]),
                            nq=q1 - q0, k0=k0, k1=k1, fin=mkfin(h, j, q0, q1)))
            attention_core(K, G, groups, 64 ** -0.5, kT, qT, v_tm, 'b')


def layernorm_tile(K, e, xn, xn_ap, src, src_ap, rows, gi, bi, st, mv):
    for c in range(2):
        K.op('dve', lambda en: en.bn_stats(out=st[:, c, :], in_=src_ap[:, c * 512:(c + 1) * 512]),
             reads=[src], writes=[st])
    K.op('dve', lambda en: en.bn_aggr(out=mv[:, 0:2], in_=st[:]), reads=[st], writes=[mv])
    K.ts('dve', mv, mv[:, 2:3], mv, mv[:, 1:2], EPS, None, ALU.add)
    K.rsqrt(mv, mv[:, 2:3])
    K.ts('dve', xn, xn_ap, src, src_ap, mv[:, 0:1], mv[:, 2:3], ALU.subtract, ALU.mult, extra_reads=[mv])
    K.tt(e, xn, xn_ap, xn, xn_ap, rows, rows[:, gi, :], ALU.mult)
    K.tt(e, xn, xn_ap, xn, xn_ap, rows, rows[:, bi, :], ALU.add)


def phase_merge(K, G, hT, wsl, th):
    L, b = G['L'], G['b']
    WD, YBR, ybr_d, XB, xB_d = G['WD'], G['YBR'], G['ybr_d'], G['XB'], G['xB_d']
    cst, cstb, IDB = G['cst'], G['cstb'], G['IDB']
    xsrc = G['xsrc']
    HT = NT // 2
    tiles = list(range(th * HT, (th + 1) * HT))
    with K.scope():
        macc = K.tile('macc', [128, HT, D], F32)
        bg = K.tile('bg', [128, 3072], F32, dma=True)
        K.load('sp', bg, bg[:], WD, G['b_gate'][L:L + 1, :].broadcast_to([128, 3072]))
        branches = [(G['w_sso'], 16, 0), (G['w_gqo'], 8, 16), (G['w_dfo'], 8, 24)]
        for br, (wout, nck, c0) in enumerate(branches):
            with K.scope('merge_br'):
                Wb = K.tile('Wbr', [128, nck, D], BF16, dma=True)
                for c4 in range(0, nck, 4):
                    K.load('pool', Wb, Wb[:, c4:c4 + 4, :], WD,
                           wout[L].rearrange("(c p) f -> p c f", p=128)[:, c4:c4 + 4, :], part=(c4 > 0))
                Wg = K.tile('Wgt', [128, KC, D], BF16, dma=True)
                K.load('pool', Wg, Wg[:, :, 0:512], WD, wsl(O_GATE + br * D, O_GATE + br * D + 512))
                K.load('pool', Wg, Wg[:, :, 512:D], WD, wsl(O_GATE + br * D + 512, O_GATE + (br + 1) * D), part=True)
                yt = [K.tile('ybt%d' % i, [128, nck, 128], BF16, dma=True) for i in range(3)]
                pg = [K.psum('pg%d' % i, [128, 512], F32) for i in range(2)]
                pp = [K.psum('pp%d' % i, [128, 512], F32) for i in range(2)]
                gt = [K.tile('gt%d' % i, [128, 512], F32) for i in range(2)]
                n = 0
                for ti, tt in enumerate(tiles):
                    y = yt[ti % 3]
                    K.load('sp', y, y[:], YBR,
                           ybr_d[c0 * 128:(c0 + nck) * 128, tt * 128:(tt + 1) * 128].rearrange("(c p) t -> p c t", p=128))
                    for hf in range(2):
                        i = n % 2
                        n += 1
                        g_, p_, gg = pg[i], pp[i], gt[i]
                        cs = slice(hf * 512, (hf + 1) * 512)
                        proj_tm(K, g_, hT, tt, Wg, hf * 512, 512)
                        K.tt('dve', gg, gg[:], g_, g_[:], bg, bg[:, br * D + hf * 512:br * D + (hf + 1) * 512], ALU.add)
                        K.act(gg, gg[:], gg, gg[:], AF.Sigmoid)
                        for c in range(nck):
                            K.mm(p_, p_[:], y, y[:, c, :], Wb, Wb[:, c, cs], c == 0, c == nck - 1)
                        if br == 0:
                            K.tt('dve', macc, macc[:, ti, cs], p_, p_[:], gg, gg[:], ALU.mult)
                        else:
                            K.tt('dve', gg, gg[:], p_, p_[:], gg, gg[:], ALU.mult)
                            K.tt('pool', macc, macc[:, ti, cs], macc, macc[:, ti, cs], gg, gg[:], ALU.add)
        with K.scope('merge_wo'):
            rows = load_rows(K, G, 2, G['ln1_g'], G['ln1_b'])
            Wo = K.tile('Wo', [128, KC, D], BF16, dma=True)
            for c4 in range(0, KC, 4):
                K.load('pool', Wo, Wo[:, c4:c4 + 4, :], WD,
                       G['w_o'][L].rearrange("(c p) f -> p c f", p=128)[:, c4:c4 + 4, :], part=(c4 > 0))
            mb = [K.tile('mb%d' % i, [128, D], BF16) for i in range(2)]
            mT = [K.tile('mT%d' % i, [128, KC, 128], BF16) for i in range(2)]
            xt = [K.tile('xt%d' % i, [128, D], F32, dma=True) for i in range(2)]
            xn = [K.tile('xn%d' % i, [128, D], F32, dma=True) for i in range(2)]
            st = [K.tile('st%d' % i, [128, 2, 6], F32) for i in range(2)]
            mv = [K.tile('mv%d' % i, [128, 4], F32) for i in range(2)]
            ptr = [K.psum('mptr%d' % i, [128, 1024], BF16) for i in range(2)]
            py = [K.psum('mpy%d' % i, [128, 512], F32) for i in range(4)]
            for ti, tt in enumerate(tiles):
                i = ti % 2
                isc = 1 if tt < 2 else 0
                x_ = xt[i]
                db, ap = xsrc(tt, 1)
                K.load('sp', x_, x_[:], db, ap)
                K.copy('act', mb[i], mb[i][:], macc, macc[:, ti, :])
                q = ptr[i]
                for c in range(KC):
                    K.tr(q, q[:, c * 128:(c + 1) * 128], mb[i], mb[i][:, c * 128:(c + 1) * 128], cstb, IDB)
                K.copy('act', mT[i], mT[i][:].rearrange("p c t -> p (c t)"), q, q[:])
                xo = xn[i]
                for hf in range(2):
                    p_ = py[(ti * 2 + hf) % 4]
                    cs = slice(hf * 512, (hf + 1) * 512)
                    for c in range(KC):
                        K.mm(p_, p_[:], mT[i], mT[i][:, c, :], Wo, Wo[:, c, cs], c == 0, c == KC - 1)
                    K.tt('dve', xo, xo[:, cs], p_, p_[:], rows, rows[:, isc, cs], ALU.mult)
                K.stt('dve', xo, xo[:], x_, x_[:], ALPHA, xo, xo[:], ALU.mult, ALU.add)
                layernorm_tile(K, 'pool', xo, xo[:], xo, xo[:], rows, 2, 3, st[i], mv[i])
                K.store('sp', XB, xB_d[tt * 128:(tt + 1) * 128, :], xo, xo[:])


def phase_ffn(K, G, hT):
    L, b = G['L'], G['b']
    WD, XB, xB_d, XA, xA_d, OUT, out = G['WD'], G['XB'], G['xB_d'], G['XA'], G['xA_d'], G['OUT'], G['out']
    NF = FFH // 128
    HTOK = T // 2
    with K.scope():
        rows = load_rows(K, G, 5, G['ln2_g'], G['ln2_b'])
        Wo = K.tile('Wfo', [128, NF, D], BF16, dma=True)
        wvo = G['ffn_wo'][L].rearrange("(c p) f -> p c f", p=128)
        for c0 in range(0, NF, 4):
            c1 = min(NF, c0 + 4)
            K.load('pool', Wo, Wo[:, c0:c1, :], WD, wvo[:, c0:c1, :], part=(c0 > 0))
        uT = K.tile('uT', [128, NF, HTOK], BF16)
        for th in range(2):
            base = th * HTOK
            blocks = [(0, 256), (256, 768), (768, 1152)] if th == 0 else [(0, 512), (512, 1024), (1024, 1152)]
            with K.scope('ffn_in'):
                Wa = [K.tile('Wa%d' % i, [128, KC, 256], BF16, dma=True) for i in range(3)]
                pa = [K.psum('pa%d' % i, [128, 512], F32) for i in range(3)]
                pb = [K.psum('pb%d' % i, [128, 512], F32) for i in range(3)]
                sa = [K.tile('sa%d' % i, [128, 512], F32) for i in range(2)]
                wv = G['ffn_wi'][L].rearrange("(kc p) f -> p kc f", p=128)
                n = 0
                for fc in range(NF):
                    W = Wa[fc % 3]
                    K.load('pool', W, W[:, :, 0:128], WD, wv[:, :, fc * 128:(fc + 1) * 128])
                    K.load('pool', W, W[:, :, 128:256], WD, wv[:, :, FFH + fc * 128:FFH + (fc + 1) * 128], part=True)
                    for (a0, a1) in blocks:
                        na = a1 - a0
                        A, B_, s_ = pa[n % 3], pb[n % 3], sa[n % 2]
                        n += 1
                        for kc in range(KC):
                            K.mm(A, A[:, 0:na], W, W[:, kc, 0:128], hT, hT[:, kc, base + a0:base + a1],
                                 kc == 0, kc == KC - 1)
                        for kc in range(KC):
                            K.mm(B_, B_[:, 0:na], W, W[:, kc, 128:256], hT, hT[:, kc, base + a0:base + a1],
                                 kc == 0, kc == KC - 1)
                        K.act(s_, s_[:, 0:na], A, A[:, 0:na], AF.Silu)
                        K.tt('dve', uT, uT[:, fc, a0:a1], s_, s_[:, 0:na], B_, B_[:, 0:na], ALU.mult)
            with K.scope('ffn_out'):
                xt = [K.tile('fxt%d' % i, [128, D], F32, dma=True) for i in range(2)]
                xn = [K.tile('fxn%d' % i, [128, D], F32, dma=True) for i in range(2)]
                st = [K.tile('fst%d' % i, [128, 2, 6], F32) for i in range(2)]
                mv = [K.tile('fmv%d' % i, [128, 4], F32) for i in range(2)]
                py = [K.psum('fpy%d' % i, [128, 512], F32) for i in range(4)]
                for ti in range(NT // 2):
                    tt = th * (NT // 2) + ti
                    i = ti % 2
                    isc = 1 if tt < 2 else 0
                    x_ = xt[i]
                    K.load('sp', x_, x_[:], XB, xB_d[tt * 128:(tt + 1) * 128, :])
                    xo = xn[i]
                    for hf in range(2):
                        p_ = py[(ti * 2 + hf) % 4]
                        cs = slice(hf * 512, (hf + 1) * 512)
                        for c in range(NF):
                            K.mm(p_, p_[:], uT, uT[:, c, ti * 128:(ti + 1) * 128], Wo, Wo[:, c, cs],
                                 c == 0, c == NF - 1)
                        K.tt('dve', xo, xo[:, cs], p_, p_[:], rows, rows[:, isc, cs], ALU.mult)
                    K.stt('dve', xo, xo[:], x_, x_[:], ALPHA, xo, xo[:], ALU.mult, ALU.add)
                    layernorm_tile(K, 'pool', xo, xo[:], xo, xo[:], rows, 2, 3, st[i], mv[i])
                    if G['L'] == G['depth_run'] - 1:
                        if tt >= 2:
                            K.store('sp', OUT, out[b, (tt - 2) * 128:(tt - 1) * 128, :], xo, xo[:])
                    else:
                        K.store('sp', XA, xA_d[tt * 128:(tt + 1) * 128, :], xo, xo[:])


def host_consts():
    p = np.arange(128)[:, None]
    f = np.arange(128)[None, :]
    c = np.zeros((128, 6, 128), np.float32)
    c[:, 0] = (p == f)
    c[:, 1] = (p <= f)
    c[:, 2] = (p > f)
    c[:, 3] = (p >= f)
    c[:, 4] = (p < f)
    c[:, 5] = 1.0
    return c


def host_rope(hd):
    t = np.arange(LAT)
    pos_row = (t // 64).astype(np.float32)
    pos_col = (t % 64).astype(np.float32)
    d_axis = hd // 2
    inv = (10000.0 ** (-np.arange(0, d_axis, 2, dtype=np.float32) / d_axis)).astype(np.float32)
    ar = pos_row[:, None] * inv
    ac = pos_col[:, None] * inv
    ang = np.concatenate([ar, ar, ac, ac], axis=-1).astype(np.float32)
    cos = np.cos(ang).astype(np.float32)
    sin = np.sin(ang).astype(np.float32)
    q = hd // 4
    sign = np.concatenate([-np.ones(q), np.ones(q), -np.ones(q), np.ones(q)]).astype(np.float32)
    tab = np.stack([cos, sin * sign], axis=1)
    return np.ascontiguousarray(tab.reshape(16, 128, 2, hd).transpose(1, 0, 2, 3))


_CACHE = {}


def run(inputs, NB, depth_run, ncores=8, trace=False):
    key = (NB, depth_run)
    if key not in _CACHE:
        _CACHE[key] = build_program(NB, depth_run)
    nc = _CACHE[key]
    f = lambda k: np.ascontiguousarray(np.asarray(inputs[k], dtype=np.float32))
    shared = {
        "ada_w": f('ada_w'), "ada_b": f('ada_b'),
        "ada_b_fm": np.ascontiguousarray(f('ada_b').reshape(DEPTH, 48, 128).transpose(0, 2, 1)),
        "w_in": f('w_in'), "b_gate": f('b_gate'),
        "conv_w": np.ascontiguousarray(f('ssm_conv_w').reshape(DEPTH, 5, 24, 128).transpose(0, 3, 2, 1)),
        "conv_b": np.ascontiguousarray(f('ssm_conv_b').reshape(DEPTH, 24, 128).transpose(0, 2, 1)),
        "ssm_dt_bias": f('ssm_dt_bias').reshape(DEPTH, 64), "ssm_a_log": f('ssm_a_log').reshape(DEPTH, 64),
        "ssm_d": f('ssm_d'), "ssm_norm_w": f('ssm_norm_w'), "w_ssm_out": f('w_ssm_out'),
        "gqa_q_norm": f('gqa_q_norm'), "gqa_k_norm": f('gqa_k_norm'), "w_gqa_out": f('w_gqa_out'),
        "diff_lambda": f('diff_lambda').reshape(DEPTH, 256),
        "diff_norm_w": f('diff_norm_w').reshape(DEPTH, 128, 1),
        "w_diff_out": f('w_diff_out'), "w_o": f('w_o'), "ln1_g": f('ln1_g'), "ln1_b": f('ln1_b'),
        "ffn_w_in": f('ffn_w_in'), "ffn_w_out": f('ffn_w_out'), "ln2_g": f('ln2_g'), "ln2_b": f('ln2_b'),
        "consts": host_consts(), "ropeG": host_rope(128), "ropeD": host_rope(64),
    }
    x, c, ctx, c_ctx = f('x'), f('c'), f('ctx'), f('c_ctx')
    in_maps = []
    for i in range(ncores):
        sl = slice(i * NB, (i + 1) * NB)
        c5 = np.zeros((5, D), np.float32)
        c5[:NB] = c[sl]
        c5[4] = c_ctx
        cT = np.ascontiguousarray(c5.reshape(5, KC, 128).transpose(2, 1, 0))
        m = dict(shared)
        m.update({"x": x[sl], "ctx": ctx[sl], "cT": cT})
        in_maps.append(m)
    if trace:
        res = run_bass_kernel_spmd(nc, in_maps, core_ids=list(range(ncores)), trace=True)
        print("exec_time_ns", res.exec_time_ns)
    else:
        res = run_bass_kernel_spmd(nc, in_maps, core_ids=list(range(ncores)))
    return np.concatenate([r["out"] for r in res.results], axis=0)


def kernel(**inputs):
    return run(inputs, 4, DEPTH).astype(np.float32)
```

```python
import math
from contextlib import ExitStack, contextmanager

import numpy as np
import concourse.bass as bass
import concourse.mybir as mybir
from concourse.bass_utils import run_bass_kernel_spmd

F32 = mybir.dt.float32
BF16 = mybir.dt.bfloat16
AF = mybir.ActivationFunctionType
ALU = mybir.AluOpType
AX = mybir.AxisListType

D = 1024
KC = 8
T = 2304
NT = 18
LAT = 2048
CTX = 256
DEPTH = 4
ALPHA = (2 * DEPTH) ** 0.25
EPS = 1e-6
FFH = 2816
O_Z, O_XBC, O_DT, O_GQ, O_GK, O_GV, O_DQ, O_DK, O_DV, O_GATE = (
    0, 2048, 5120, 5184, 6208, 6464, 6720, 7744, 8768, 9792)
TB = [(0, 256), (256, 768), (768, 1280), (1280, 1792), (1792, 2304)]
NSEM = 40
ATTACH = True
MARKS = []


class Buf:
    def __init__(self, t, sem=None):
        self.t = t
        self.w = {}
        self.r = {}
        self.sem = sem

    def __getitem__(self, idx):
        return self.t[idx]


class DBuf:
    def __init__(self, ap):
        self.ap = ap
        self.w = {}
        self.r = {}


class Kx:
    def __init__(self, nc, es):
        self.nc = nc
        self.eng = {'pe': nc.tensor, 'act': nc.scalar, 'dve': nc.vector,
                    'pool': nc.gpsimd, 'sp': nc.sync}
        self.sem = {k: es.enter_context(nc.semaphore('sem_' + k)) for k in self.eng}
        self.cnt = {k: 0 for k in self.eng}
        self.seen = {k: {} for k in self.eng}
        self.sempool = [[es.enter_context(nc.semaphore('dsem%d' % i)), 'd%d' % i, 0]
                        for i in range(NSEM)]
        self.pending = {}
        self.dbufs = []
        self.stacks = [es]
        self.scope_sems = [[]]
        self.uid = 0

    def tile(self, name, shape, dtype, dma=False):
        self.uid += 1
        t = self.stacks[-1].enter_context(
            self.nc.sbuf_tensor('%s_%d' % (name, self.uid), list(shape), dtype))
        sem = None
        if dma:
            sem = self.sempool.pop()
            self.scope_sems[-1].append(sem)
        return Buf(t, sem)

    def psum(self, name, shape, dtype):
        self.uid += 1
        t = self.stacks[-1].enter_context(
            self.nc.psum_tensor('%s_%d' % (name, self.uid), list(shape), dtype))
        return Buf(t)

    def dbuf(self, ap):
        d = DBuf(ap)
        self.dbufs.append(d)
        return d

    @contextmanager
    def scope(self, name=None):
        es = ExitStack()
        self.stacks.append(es)
        self.scope_sems.append([])
        c0 = self.cnt['pe']
        try:
            yield
        finally:
            if name is not None:
                MARKS.append((name, c0, self.cnt['pe']))
            self.barrier()
            for s in self.scope_sems.pop():
                self.sempool.append(s)
            self.stacks.pop()
            es.close()

    def _need(self, e, deps):
        need = {}
        for (sem, key, val) in deps:
            if key == e and e == 'pe':
                continue
            if self.seen[e].get(key, 0) >= val:
                continue
            if key not in need or need[key][2] < val:
                need[key] = (sem, key, val)
        return list(need.values())

    def _wait(self, e, deps, attach=False):
        need = self._need(e, deps)
        last = None
        if attach and need:
            last = need.pop()
        for (sem, key, val) in need:
            self.eng[e].wait_ge(sem, val)
            self.seen[e][key] = val
        if last is not None:
            self.seen[e][last[1]] = last[2]
        return last

    def op(self, e, fn, reads=(), writes=(), attach=False):
        deps = []
        for b in reads:
            deps += list(b.w.values())
        for b in writes:
            deps += [t for t in b.w.values() if t[1] != e]
            deps += [t for t in b.r.values() if t[1] != e]
        last = self._wait(e, deps, attach=(attach and ATTACH))
        ins = fn(self.eng[e])
        if last is not None:
            ins._wait_ge(last[0], last[2])
        self.cnt[e] += 1
        ins.then_inc(self.sem[e], 1)
        tok = (self.sem[e], e, self.cnt[e])
        for b in reads:
            b.r[e] = tok
        for b in writes:
            b.w = {e: tok}
            b.r = {}
        return ins

    def load(self, q, sb, out_ap, dr, in_ap, part=False):
        deps = list(dr.w.values()) + list(sb.r.values())
        deps += [t for t in sb.w.values() if not (part and t[1] == sb.sem[1])]
        self._wait(q, deps)
        ins = self.eng[q].dma_start(out=out_ap, in_=in_ap)
        sb.sem[2] += 16
        ins.then_inc(sb.sem[0], 16)
        tok = (sb.sem[0], sb.sem[1], sb.sem[2])
        self.pending[tok[1]] = tok
        dr.r[tok[1]] = tok
        if part:
            sb.w[tok[1]] = tok
        else:
            sb.w = {tok[1]: tok}
        sb.r = {}

    def store(self, q, dr, out_ap, sb, in_ap):
        deps = list(sb.w.values()) + list(dr.r.values())
        self._wait(q, deps)
        ins = self.eng[q].dma_start(out=out_ap, in_=in_ap)
        sb.sem[2] += 16
        ins.then_inc(sb.sem[0], 16)
        tok = (sb.sem[0], sb.sem[1], sb.sem[2])
        self.pending[tok[1]] = tok
        sb.r[tok[1]] = tok
        dr.w[tok[1]] = tok

    def barrier(self):
        sp = 'sp'
        deps = [(self.sem[e], e, self.cnt[e]) for e in ('pe', 'act', 'dve', 'pool')
                if self.cnt[e] > 0]
        deps += list(self.pending.values())
        self._wait(sp, deps)
        self.eng[sp].sem_inc(self.sem[sp], 1)
        self.cnt[sp] += 1
        tok = (self.sem[sp], sp, self.cnt[sp])
        for e in ('pe', 'act', 'dve', 'pool'):
            self._wait(e, [tok])
        self.pending = {}
        for d in self.dbufs:
            d.w = {}
            d.r = {}

    def mm(self, ps, out_ap, lhsT_b, lhsT_ap, rhs_b, rhs_ap, start, stop, attach=False):
        if not attach:
            try:
                attach = (lhsT_ap.dtype == BF16 and rhs_ap.dtype == BF16)
            except Exception:
                attach = False
        return self.op('pe', lambda e: e.matmul(out_ap, lhsT_ap, rhs_ap, start=start, stop=stop),
                       reads=[lhsT_b, rhs_b], writes=[ps], attach=attach)

    def tr(self, ps, out_ap, in_b, in_ap, ident_b, ident_ap):
        return self.op('pe', lambda e: e.transpose(out_ap, in_ap, ident_ap),
                       reads=[in_b, ident_b], writes=[ps])

    def act(self, out_b, out_ap, in_b, in_ap, func, bias=None, scale=None, extra_reads=(),
            accum=None, accum_b=None, attach=False):
        kw = {}
        if bias is not None:
            kw['bias'] = bias
        if scale is not None:
            kw['scale'] = scale
        if accum is not None:
            kw['accum_out'] = accum
        wr = [out_b] + ([accum_b] if accum_b is not None else [])
        return self.op('act', lambda e: e.activation(out=out_ap, in_=in_ap, func=func, **kw),
                       reads=[in_b] + list(extra_reads), writes=wr, attach=True)

    def tt(self, e, out_b, out_ap, a_b, a_ap, b_b, b_ap, op):
        return self.op(e, lambda en: en.tensor_tensor(out=out_ap, in0=a_ap, in1=b_ap, op=op),
                       reads=[a_b, b_b], writes=[out_b])

    def ts(self, e, out_b, out_ap, a_b, a_ap, s1, s2, op0, op1=None, extra_reads=()):
        if op1 is None:
            f = lambda en: en.tensor_scalar(out=out_ap, in0=a_ap, scalar1=s1, scalar2=None, op0=op0)
        else:
            f = lambda en: en.tensor_scalar(out=out_ap, in0=a_ap, scalar1=s1, scalar2=s2,
                                            op0=op0, op1=op1)
        return self.op(e, f, reads=[a_b] + list(extra_reads), writes=[out_b])

    def stt(self, e, out_b, out_ap, a_b, a_ap, scalar, b_b, b_ap, op0, op1, extra_reads=()):
        return self.op(e, lambda en: en.scalar_tensor_tensor(out=out_ap, in0=a_ap, scalar=scalar,
                                                             in1=b_ap, op0=op0, op1=op1),
                       reads=[a_b, b_b] + list(extra_reads), writes=[out_b])

    def copy(self, e, out_b, out_ap, in_b, in_ap):
        if e == 'act':
            return self.op('act', lambda en: en.copy(out=out_ap, in_=in_ap),
                           reads=[in_b], writes=[out_b])
        return self.op(e, lambda en: en.tensor_copy(out=out_ap, in_=in_ap),
                       reads=[in_b], writes=[out_b])

    def rsqrt(self, b, ap):
        self.act(b, ap, b, ap, AF.Ln)
        self.act(b, ap, b, ap, AF.Exp, scale=-0.5)

    def memset(self, e, b, ap, val):
        return self.op(e, lambda en: en.memset(ap, val), reads=[], writes=[b])


def bc(ap, shape):
    return ap.broadcast_to(list(shape))


def build_program(NB, depth_run, debug=False):
    nc = bass.Bass("TRN2", target_bir_lowering=False)

    def din(name, shape, dt=F32):
        return nc.dram_tensor(name, list(shape), dt, kind="ExternalInput").ap()

    x_in = din("x", [NB, LAT, D])
    ctx_in = din("ctx", [NB, CTX, D])
    cT_in = din("cT", [128, KC, 5])
    ada_w = din("ada_w", [DEPTH, D, 6 * D])
    ada_b_tm = din("ada_b", [DEPTH, 6 * D])
    ada_b_fm = din("ada_b_fm", [DEPTH, 128, 48])
    w_in = din("w_in", [DEPTH, D, 12864])
    b_gate = din("b_gate", [DEPTH, 3072])
    conv_w = din("conv_w", [DEPTH, 128, 24, 5])
    conv_b = din("conv_b", [DEPTH, 128, 24])
    dt_bias = din("ssm_dt_bias", [DEPTH, 64])
    a_log = din("ssm_a_log", [DEPTH, 64])
    ssm_d = din("ssm_d", [DEPTH, 32])
    ssm_nw = din("ssm_norm_w", [DEPTH, 2048])
    w_sso = din("w_ssm_out", [DEPTH, 2048, D])
    gq_norm = din("gqa_q_norm", [DEPTH, 128])
    gk_norm = din("gqa_k_norm", [DEPTH, 128])
    w_gqo = din("w_gqa_out", [DEPTH, D, D])
    dlam = din("diff_lambda", [DEPTH, 256])
    dnw_fm = din("diff_norm_w", [DEPTH, 128, 1])
    w_dfo = din("w_diff_out", [DEPTH, D, D])
    w_o = din("w_o", [DEPTH, D, D])
    ln1_g = din("ln1_g", [DEPTH, D])
    ln1_b = din("ln1_b", [DEPTH, D])
    ffn_wi = din("ffn_w_in", [DEPTH, D, 2 * FFH])
    ffn_wo = din("ffn_w_out", [DEPTH, FFH, D])
    ln2_g = din("ln2_g", [DEPTH, D])
    ln2_b = din("ln2_b", [DEPTH, D])
    consts_in = din("consts", [128, 6, 128])
    ropeG_in = din("ropeG", [128, 16, 2, 128])
    ropeD_in = din("ropeD", [128, 16, 2, 64])
    out = nc.dram_tensor("out", [NB, LAT, D], F32, kind="ExternalOutput").ap()

    def scratch(name, shape, dt):
        return nc.dram_tensor(name, list(shape), dt, kind="Internal").ap()

    xA_d = scratch("xA", [T, D], F32)
    xB_d = scratch("xB", [T, D], F32)
    ybr_d = scratch("ybr", [32 * 128, T], BF16)
    mod_d = scratch("mod_tm", [DEPTH, 5, 6 * D], F32)

    with ExitStack() as es:
        K = Kx(nc, es)
        X_IN = K.dbuf(x_in)
        CTX_IN = K.dbuf(ctx_in)
        OUT = K.dbuf(out)
        XA = K.dbuf(xA_d)
        XB = K.dbuf(xB_d)
        YBR = K.dbuf(ybr_d)
        MODD = K.dbuf(mod_d)
        WD = K.dbuf(None)

        cst = K.tile('cst', [128, 6, 128], F32, dma=True)
        K.load('sp', cst, cst[:], WD, consts_in)
        IDF, MLE, MGT, MGE, MLT, ONESF = (cst[:, i, :] for i in range(6))
        cstb = K.tile('cstb', [128, 2, 128], BF16)
        K.copy('dve', cstb, cstb[:, 0, :], cst, cst[:, 0, :])
        K.copy('dve', cstb, cstb[:, 1, :], cst, cst[:, 5, :])
        IDB = cstb[:, 0, :]
        ONESB = cstb[:, 1, :]
        modT = K.tile('modT', [128, DEPTH, 48, 5], F32)

        with K.scope('adaLN'):
            cT = K.tile('cT', [128, KC, 5], F32, dma=True)
            K.load('sp', cT, cT[:], WD, cT_in)
            scT = K.tile('scT', [128, KC, 5], BF16)
            K.act(scT, scT[:], cT, cT[:], AF.Silu)
            abf = K.tile('abf', [128, DEPTH, 48], F32, dma=True)
            K.load('sp', abf, abf[:], WD, ada_b_fm.rearrange("l p c -> p l c"))
            wts = [K.tile('adaw%d' % i, [128, KC, 512], BF16, dma=True) for i in range(2)]
            abt = [K.tile('abt%d' % i, [5, 512], F32, dma=True) for i in range(2)]
            mo = [K.tile('mo%d' % i, [5, 512], F32, dma=True) for i in range(2)]
            ps_t = [K.psum('ps0t%d' % i, [128, 512], F32) for i in range(2)]
            ps_f = [K.psum('ps0f%d' % i, [128, 4, 5], F32) for i in range(2)]
            n = 0
            for L in range(depth_run):
                for j in range(12):
                    W = wts[n % 2]
                    K.load('pool', W, W[:], WD,
                           ada_w[L].rearrange("(kc p) f -> p kc f", p=128)[:, :, j * 512:(j + 1) * 512])
                    ab = abt[n % 2]
                    K.load('sp', ab, ab[:], WD,
                           ada_b_tm[L:L + 1, j * 512:(j + 1) * 512].broadcast_to([5, 512]))
                    pt = ps_t[n % 2]
                    for kc in range(KC):
                        K.mm(pt, pt[0:5, :], scT, scT[:, kc, :], W, W[:, kc, :], kc == 0, kc == KC - 1)
                    m = mo[n % 2]
                    K.tt('dve', m, m[:], pt, pt[0:5, :], ab, ab[:], ALU.add)
                    K.store('sp', MODD, mod_d[L, :, j * 512:(j + 1) * 512], m, m[:])
                    pf = ps_f[n % 2]
                    for f in range(4):
                        for kc in range(KC):
                            K.mm(pf, pf[:, f, :], W, W[:, kc, f * 128:(f + 1) * 128], scT, scT[:, kc, :],
                                 kc == 0, kc == KC - 1)
                    K.tt('dve', modT, modT[:, L, j * 4:(j + 1) * 4, :], pf, pf[:],
                         abf, bc(abf[:, L, j * 4:(j + 1) * 4, None], [128, 4, 5]), ALU.add)
                    n += 1

        for b in range(NB):
            for L in range(depth_run):
                last = (L == DEPTH - 1)
                lam_init = 0.8 - 0.6 * math.exp(-0.3 * L)
                if L == 0:
                    def xsrc(t0, nt, b=b):
                        if t0 < 2:
                            return CTX_IN, ctx_in[b, t0 * 128:(t0 + nt) * 128, :]
                        return X_IN, x_in[b, (t0 - 2) * 128:(t0 - 2 + nt) * 128, :]
                else:
                    def xsrc(t0, nt):
                        return XA, xA_d[t0 * 128:(t0 + nt) * 128, :]
                with K.scope():
                    layer(K, nc, locals())
    return nc


def layer(K, nc, G):
    b, L, last, lam_init, xsrc = G['b'], G['L'], G['last'], G['lam_init'], G['xsrc']
    WD, MODD, YBR, XA, XB, OUT = G['WD'], G['MODD'], G['YBR'], G['XA'], G['XB'], G['OUT']
    cst, cstb, modT = G['cst'], G['cstb'], G['modT']
    IDF, MLE, MGT, MGE, MLT, ONESF, IDB, ONESB = (G[k] for k in
                                                  ('IDF', 'MLE', 'MGT', 'MGE', 'MLT', 'ONESF', 'IDB', 'ONESB'))
    w_in = G['w_in']
    mod_d, ybr_d, xA_d, xB_d, out = G['mod_d'], G['ybr_d'], G['xA_d'], G['xB_d'], G['out']

    def wsl(c0, c1):
        return w_in[L].rearrange("(kc p) f -> p kc f", p=128)[:, :, c0:c1]

    sc1 = K.tile('sc1', [128, 2, 8], F32)
    sh1 = K.tile('sh1', [128, 2, 8], F32)
    sc2 = K.tile('sc2', [128, 2, 8], F32)
    sh2 = K.tile('sh2', [128, 2, 8], F32)
    for j, m in enumerate((b, 4)):
        K.copy('dve', sh1, sh1[:, j, :], modT, modT[:, L, 0:8, m])
        K.ts('dve', sc1, sc1[:, j, :], modT, modT[:, L, 8:16, m], 1.0, None, ALU.add)
        K.copy('dve', sh2, sh2[:, j, :], modT, modT[:, L, 24:32, m])
        K.ts('dve', sc2, sc2[:, j, :], modT, modT[:, L, 32:40, m], 1.0, None, ALU.add)
    hT = K.tile('hT', [128, KC, T], BF16)

    phase_A(K, G, hT, sc1, sh1, xsrc)
    phase_ssm(K, G, hT, wsl)
    phase_gqa(K, G, hT, wsl)
    for hh in range(2):
        phase_diff(K, G, hT, wsl, hh)
    for th in range(2):
        phase_merge(K, G, hT, wsl, th)
    phase_A(K, G, hT, sc2, sh2, lambda t0, nt: (XB, xB_d[t0 * 128:(t0 + nt) * 128, :]))
    phase_ffn(K, G, hT)


def load_rows(K, G, kmod, lg, lb):
    L, b = G['L'], G['b']
    rows = K.tile('rows', [128, 4, D], F32, dma=True)
    for j, m in enumerate((b, 4)):
        K.load('sp', rows, rows[:, j, :], G['MODD'],
               G['mod_d'][L, m:m + 1, kmod * D:(kmod + 1) * D].broadcast_to([128, D]), part=(j > 0))
    for j, src in enumerate((lg, lb)):
        K.load('sp', rows, rows[:, 2 + j, :], G['WD'], src[L:L + 1, :].broadcast_to([128, D]), part=True)
    return rows


def phase_A(K, G, hT, sc1, sh1, xsrc):
    IDF = G['IDF']
    cst = G['cst']
    with K.scope('A'):
        ps = [K.psum('psA%d' % i, [128, 512], F32) for i in range(4)]
        xg = [K.tile('xg%d' % i, [128, 4, D], F32, dma=True) for i in range(2)]
        groups = [(0, 2, 1), (2, 4, 0), (6, 4, 0), (10, 4, 0), (14, 4, 0)]
        n = 0
        for gi, (t0, nt, isc) in enumerate(groups):
            xb = xg[gi % 2]
            db, ap = xsrc(t0, nt)
            K.load('sp', xb, xb[:, 0:nt, :], db, ap.rearrange("(n p) d -> p n d", p=128))
            for kc in range(KC):
                p = ps[n % 4]
                for j in range(nt):
                    K.tr(p, p[:, j * 128:(j + 1) * 128], xb, xb[:, j, kc * 128:(kc + 1) * 128], cst, IDF)
                o = hT[:, kc, t0 * 128:(t0 + nt) * 128]
                if n % 2 == 0:
                    K.ts('dve', hT, o, p, p[:, 0:nt * 128], sc1[:, isc, kc:kc + 1], sh1[:, isc, kc:kc + 1],
                         ALU.mult, ALU.add, extra_reads=[sc1, sh1])
                else:
                    K.act(hT, o, p, p[:, 0:nt * 128], AF.Identity, bias=sh1[:, isc, kc:kc + 1],
                          scale=sc1[:, isc, kc:kc + 1], extra_reads=[sc1, sh1])
                n += 1


def proj_tm(K, ps, hT, tt, W, c0, ncols):
    for kc in range(KC):
        K.mm(ps, ps[:, 0:ncols], hT, hT[:, kc, tt * 128:(tt + 1) * 128], W, W[:, kc, c0:c0 + ncols],
             kc == 0, kc == KC - 1)


def phase_ssm(K, G, hT, wsl):
    L = G['L']
    WD, YBR, ybr_d = G['WD'], G['YBR'], G['ybr_d']
    cst, cstb = G['cst'], G['cstb']
    IDB, ONESF = G['IDB'], G['ONESF']
    MLE, MGT, MGE, MLT = G['MLE'], G['MGT'], G['MGE'], G['MLT']
    with K.scope():
        dt = K.tile('dt', [128, NT, 64], F32)
        adt = K.tile('adt', [128, NT, 64], F32)
        prm = K.tile('prm', [128, 3, 64], F32, dma=True)
        K.load('sp', prm, prm[:, 0, :], WD, G['dt_bias'][L:L + 1, :].broadcast_to([128, 64]))
        K.load('sp', prm, prm[:, 1, :], WD, G['a_log'][L:L + 1, :].broadcast_to([128, 64]), part=True)
        K.load('sp', prm, prm[:, 2, 0:32], WD, G['ssm_d'][L:L + 1, :].broadcast_to([128, 32]), part=True)
        nw = K.tile('nw', [128, 2048], F32, dma=True)
        K.load('sp', nw, nw[:], WD, G['ssm_nw'][L:L + 1, :].broadcast_to([128, 2048]))
        cw = K.tile('cw', [128, 24, 5], F32, dma=True)
        K.load('sp', cw, cw[:], WD, G['conv_w'][L])
        cb = K.tile('cb', [128, 24], F32, dma=True)
        K.load('sp', cb, cb[:], WD, G['conv_b'][L])
        aneg = K.tile('aneg', [128, 64], F32)
        K.act(aneg, aneg[:], prm, prm[:, 1, :], AF.Exp)
        K.ts('dve', aneg, aneg[:], aneg, aneg[:], -1.0, None, ALU.mult)
        with K.scope('ssm_dt'):
            Wdt = K.tile('Wdt', [128, KC, 64], BF16, dma=True)
            K.load('pool', Wdt, Wdt[:], WD, wsl(O_DT, O_DT + 64))
            psd = [K.psum('psd%d' % i, [128, 64], F32) for i in range(2)]
            tmp = [K.tile('dtt%d' % i, [128, 64], F32) for i in range(2)]
            for tt in range(NT):
                p = psd[tt % 2]
                proj_tm(K, p, hT, tt, Wdt, 0, 64)
                t1 = tmp[tt % 2]
                K.tt('dve', t1, t1[:], p, p[:], prm, prm[:, 0, :], ALU.add)
                K.act(t1, t1[:], t1, t1[:], AF.Exp)
                K.act(dt, dt[:, tt, :], t1, t1[:], AF.Ln, bias=1.0)
            K.tt('dve', adt, adt[:], dt, dt[:], aneg, bc(aneg[:, None, :], [128, NT, 64]), ALU.mult)

        for g in range(4):
            with K.scope():
                ssm_group(K, G, hT, wsl, g, dt, adt, prm, nw, cw, cb)


def ssm_group(K, G, hT, wsl, g, dt, adt, prm, nw, cw, cb):
    WD, YBR, ybr_d = G['WD'], G['YBR'], G['ybr_d']
    cst, cstb = G['cst'], G['cstb']
    IDB, ONESF = G['IDB'], G['ONESF']
    MLE, MGT, MGE, MLT = G['MLE'], G['MGT'], G['MGE'], G['MLT']
    PADW = 2316
    xs_tm = K.tile('xs_tm', [128, NT, 512], BF16)
    B_tm = K.tile('B_tm', [128, NT, 128], BF16)
    BT = K.tile('BT', [128, T], BF16)
    CT = K.tile('CT', [128, T], BF16)
    Wz = K.tile('Wz', [128, KC, 512], BF16, dma=True)
    with K.scope('ssm_B1'):
        W6 = K.tile('W6', [128, KC, 768], BF16, dma=True)
        K.load('pool', W6, W6[:, :, 0:512], WD, wsl(O_XBC + g * 512, O_XBC + (g + 1) * 512))
        K.load('pool', W6, W6[:, :, 512:640], WD,
               wsl(O_XBC + 2048 + g * 128, O_XBC + 2048 + (g + 1) * 128), part=True)
        K.load('pool', W6, W6[:, :, 640:768], WD,
               wsl(O_XBC + 2560 + g * 128, O_XBC + 2560 + (g + 1) * 128), part=True)
        K.load('pool', Wz, Wz[:], WD, wsl(O_Z + g * 512, O_Z + (g + 1) * 512))
        cchunk = [g * 4 + 0, g * 4 + 1, g * 4 + 2, g * 4 + 3, 16 + g, 20 + g]
        xcT = K.tile('xcT', [128, 4, T], BF16)
        xpad = [K.tile('xpad%d' % i, [128, PADW], BF16) for i in range(2)]
        dg = [K.tile('dg%d' % i, [128, 5, 128], BF16) for i in range(2)]
        for i in range(2):
            K.memset('pool', xpad[i], xpad[i][:], 0.0)
        psp = [K.psum('psp%d' % i, [128, 512], F32) for i in range(3)]
        psc = [K.psum('psc%d' % i, [128, 512], F32) for i in range(2)]
        pst = [K.psum('pst%d' % i, [128, 512], BF16) for i in range(2)]
        n = 0
        m = 0
        for fc in range(6):
            xp = xpad[fc % 2]
            dgm = dg[fc % 2]
            cc = cchunk[fc]
            for j in range(5):
                K.ts('dve', dgm, dgm[:, j, :], cstb, IDB, cw[:, cc, j:j + 1], None, ALU.mult, extra_reads=[cw])
            for (a0, a1) in TB:
                p = psp[n % 3]
                n += 1
                for kc in range(KC):
                    K.mm(p, p[:, 0:a1 - a0], W6, W6[:, kc, fc * 128:(fc + 1) * 128], hT, hT[:, kc, a0:a1],
                         kc == 0, kc == KC - 1)
                off = 2 if a0 < 256 else 6
                K.copy('act', xp, xp[:, a0 + off:a1 + off], p, p[:, 0:a1 - a0])
            for (a0, a1) in TB:
                off = 2 if a0 < 256 else 6
                na = a1 - a0
                pc_ = psc[m % 2]
                m += 1
                for j in range(5):
                    K.mm(pc_, pc_[:, 0:na], dgm, dgm[:, j, :], xp, xp[:, a0 + off + j - 2:a1 + off + j - 2],
                         j == 0, j == 4)
                if fc < 4:
                    ob, o_ = xcT, xcT[:, fc, a0:a1]
                elif fc == 4:
                    ob, o_ = BT, BT[:, a0:a1]
                else:
                    ob, o_ = CT, CT[:, a0:a1]
                K.act(ob, o_, pc_, pc_[:, 0:na], AF.Silu, bias=cb[:, cc:cc + 1], extra_reads=[cb])
        for tt in range(NT):
            p = pst[tt % 2]
            for fc in range(4):
                K.tr(p, p[:, fc * 128:(fc + 1) * 128], xcT, xcT[:, fc, tt * 128:(tt + 1) * 128], cstb, IDB)
            K.copy('dve' if tt % 2 else 'act', xs_tm, xs_tm[:, tt, :], p, p[:])
        for t4 in range(0, NT, 4):
            nt = min(4, NT - t4)
            p = pst[(t4 // 4) % 2]
            for j in range(nt):
                K.tr(p, p[:, j * 128:(j + 1) * 128], BT, BT[:, (t4 + j) * 128:(t4 + j + 1) * 128], cstb, IDB)
            K.copy('dve', B_tm, B_tm[:, t4:t4 + nt, :], p,
                   p[:, 0:nt * 128].rearrange("p (n c) -> p n c", c=128))

    yacc = K.tile('yacc', [128, NT, 512], F32)
    with K.scope('ssm_sweep'):
        S = K.tile('S', [128, 512], F32)
        Sb = K.tile('Sb', [128, 512], BF16)
        Xb = [K.tile('X%d' % i, [128, 8, 128], F32) for i in range(2)]
        Eb = [K.tile('E%d' % i, [128, 8, 128], F32) for i in range(2)]
        MTb = [K.tile('MT%d' % i, [128, 8, 128], BF16) for i in range(2)]
        Gmb = [K.tile('Gm%d' % i, [128, 128], F32) for i in range(2)]
        xdtb = [K.tile('xdt%d' % i, [128, 8, 64], BF16) for i in range(2)]
        xwb = [K.tile('xw%d' % i, [128, 8, 64], BF16) for i in range(2)]
        smb = [K.tile('sm%d' % i, [128, 16], F32) for i in range(2)]
        tmpb = [K.tile('yt%d' % i, [128, 512], F32) for i in range(2)]
        pG = K.psum('pG', [128, 128], F32)
        pD = [K.psum('pD%d' % i, [128, 512], F32) for i in range(2)]
        pY = K.psum('pY', [128, 512], F32)
        pYo = K.psum('pYo', [128, 512], F32)
        pS = K.psum('pS', [128, 512], F32)
        pc = K.psum('pc', [128, 16], F32)
        n = 0
        for d in range(2):
            Ma, Mb, Mg = (MLE, MGT, MLE) if d == 0 else (MGE, MLT, MGE)
            endcol = 127 if d == 0 else 0
            order = list(range(NT)) if d == 0 else [1, 0] + list(range(NT - 1, 1, -1))
            hs = slice(d * 32 + g * 8, d * 32 + g * 8 + 8)
            K.memset('dve', S, S[:], 0.0)
            K.memset('dve', Sb, Sb[:], 0.0)
            def stage1(c, i):
                tok = slice(c * 128, (c + 1) * 128)
                X, E, MT, Gm, xdt, xw, sm = Xb[i], Eb[i], MTb[i], Gmb[i], xdtb[i], xwb[i], smb[i]
                K.mm(pG, pG[:], BT, BT[:, tok], CT, CT[:, tok], True, True)
                K.tt('dve', X, X[:], adt, bc(adt[:, c, hs, None], [128, 8, 128]),
                     cst, bc(Ma[:, None, :], [128, 8, 128]), ALU.mult)
                K.tt('dve', Gm, Gm[:], pG, pG[:], cst, Mg, ALU.mult)
                K.mm(pc, pc[:, 0:8], cst, Ma, adt, adt[:, c, hs], True, True)
                K.mm(pc, pc[:, 8:16], cst, ONESF, adt, adt[:, c, hs], True, True)
                K.act(sm, sm[:], pc, pc[:], AF.Exp)
                K.tt('pool', xdt, xdt[:], xs_tm, xs_tm[:, c, :].rearrange("p (h q) -> p h q", q=64),
                     dt, bc(dt[:, c, hs, None], [128, 8, 64]), ALU.mult)

            def stage1b(c, i):
                X, E, MT, Gm, xdt, xw, sm = Xb[i], Eb[i], MTb[i], Gmb[i], xdtb[i], xwb[i], smb[i]
                for hh in range(2):
                    K.mm(pD[hh], pD[hh][:], cst, Mb, X,
                         X[:, hh * 4:(hh + 1) * 4, :].rearrange("p h l -> p (h l)"), True, True)
                    K.act(E, E[:, hh * 4:(hh + 1) * 4, :].rearrange("p h l -> p (h l)"), pD[hh], pD[hh][:], AF.Exp)
                K.tt('dve', MT, MT[:], E, E[:], Gm, bc(Gm[:, None, :], [128, 8, 128]), ALU.mult)
                K.tt('pool', xw, xw[:], xdt, xdt[:], E, bc(E[:, :, endcol:endcol + 1], [128, 8, 64]), ALU.mult)

            def stage2(c, i):
                tok = slice(c * 128, (c + 1) * 128)
                MT, xdt, xw, sm, ytmp = MTb[i], xdtb[i], xwb[i], smb[i], tmpb[i]
                K.mm(pYo, pYo[:], CT, CT[:, tok], Sb, Sb[:], True, True)
                K.mm(pS, pS[:], B_tm, B_tm[:, c, :], xw, xw[:].rearrange("p h q -> p (h q)"), True, True)
                for h in range(8):
                    K.mm(pY, pY[:, h * 64:(h + 1) * 64], MT, MT[:, h, :], xdt, xdt[:, h, :], True, True)
                K.tt('dve', S, S[:].rearrange("p (h q) -> p h q", q=64),
                     S, S[:].rearrange("p (h q) -> p h q", q=64),
                     sm, bc(sm[:, 8:16, None], [128, 8, 64]), ALU.mult)
                K.tt('dve', S, S[:], S, S[:], pS, pS[:], ALU.add)
                K.copy('act', Sb, Sb[:], S, S[:])
                K.tt('dve', ytmp, ytmp[:].rearrange("p (h q) -> p h q", q=64),
                     pYo, pYo[:].rearrange("p (h q) -> p h q", q=64),
                     sm, bc(sm[:, 0:8, None], [128, 8, 64]), ALU.mult)
                if d == 0:
                    K.tt('dve', yacc, yacc[:, c, :], ytmp, ytmp[:], pY, pY[:], ALU.add)
                else:
                    K.tt('dve', ytmp, ytmp[:], ytmp, ytmp[:], pY, pY[:], ALU.add)
                    K.tt('pool', yacc, yacc[:, c, :], yacc, yacc[:, c, :], ytmp, ytmp[:], ALU.add)

            for t in range(len(order) + 1):
                if t < len(order):
                    stage1(order[t], t % 2)
                if t >= 1:
                    stage2(order[t - 1], (t - 1) % 2)
                if t < len(order):
                    stage1b(order[t], t % 2)

    with K.scope('ssm_post'):
        ysT = K.tile('ysT', [128, 4, T], BF16, dma=True)
        pz = [K.psum('pz%d' % i, [128, 512], F32) for i in range(2)]
        pt = [K.psum('pt%d' % i, [128, 512], BF16) for i in range(2)]
        zs = [K.tile('zs%d' % i, [128, 512], F32) for i in range(2)]
        yb = [K.tile('yb%d' % i, [128, 512], F32) for i in range(2)]
        ynb = [K.tile('yn%d' % i, [128, 512], BF16) for i in range(2)]
        jk = [K.tile('jk%d' % i, [128, 512], F32) for i in range(2)]
        ssb = [K.tile('ss%d' % i, [128, 2], F32) for i in range(2)]
        for tt in range(NT):
            i = tt % 2
            p, z, y = pz[i], zs[i], yb[i]
            proj_tm(K, p, hT, tt, Wz, 0, 512)
            K.act(z, z[:], p, p[:], AF.Silu)
            K.tt('pool', y, y[:].rearrange("p (h q) -> p h q", q=64),
                 xs_tm, xs_tm[:, tt, :].rearrange("p (h q) -> p h q", q=64),
                 prm, bc(prm[:, 2, g * 8:(g + 1) * 8, None], [128, 8, 64]), ALU.mult)
            K.tt('dve', y, y[:], y, y[:], yacc, yacc[:, tt, :], ALU.add)
            K.tt('dve', yacc, yacc[:, tt, :], y, y[:], z, z[:], ALU.mult)
        for tt in range(NT):
            i = tt % 2
            yn, ss = ynb[i], ssb[i]
            K.act(jk[i], jk[i][:], yacc, yacc[:, tt, :], AF.Square, accum=ss[:, 0:1], accum_b=ss)
            K.ts('dve', ss, ss[:, 1:2], ss, ss[:, 0:1], 1.0 / 512, EPS, ALU.mult, ALU.add)
            K.rsqrt(ss, ss[:, 1:2])
            K.stt('dve', yn, yn[:], yacc, yacc[:, tt, :], ss[:, 1:2], nw, nw[:, g * 512:(g + 1) * 512],
                  ALU.mult, ALU.mult, extra_reads=[ss])
            q = pt[i]
            for fc in range(4):
                K.tr(q, q[:, fc * 128:(fc + 1) * 128], yn, yn[:, fc * 128:(fc + 1) * 128], cstb, IDB)
            K.copy('dve' if tt % 2 else 'act', ysT, ysT[:, :, tt * 128:(tt + 1) * 128], q,
                   q[:].rearrange("p (c t) -> p c t", t=128))
        K.store('sp', YBR, ybr_d[g * 512:(g + 1) * 512, :].rearrange("(c p) t -> p c t", p=128),
                ysT, ysT[:])


def rope(K, e, dst, dst_ap, src, src_ap, tab, cos_ap, sin_ap, tmp, tmp_ap, nh, hd, e_first=None):
    q = hd // 4
    K.tt(e_first or e, dst, dst_ap, src, src_ap, tab, bc(cos_ap[:, None, :], [128, nh, hd]), ALU.mult)
    for a in range(2):
        for s in range(2):
            o0 = a * 2 * q + s * q
            i0 = a * 2 * q + (1 - s) * q
            K.tt(e, tmp, tmp_ap[:, :, o0:o0 + q], src, src_ap[:, :, i0:i0 + q],
                 tab, bc(sin_ap[:, None, o0:o0 + q], [128, nh, q]), ALU.mult)
    K.tt(e, dst, dst_ap, dst, dst_ap, tmp, tmp_ap, ALU.add)


def attention_core(K, G, groups, scale, kT, qT, v_tm, pfx):
    cstb, ONESB = G['cstb'], G['ONESB']
    pS = [K.psum(pfx + 'S%d' % i, [128, 2, 512], F32) for i in range(2)]
    pO = [K.psum(pfx + 'O%d' % i, [128, 512], F32) for i in range(2)]
    pZ = [K.psum(pfx + 'Z%d' % i, [128, 512], F32) for i in range(2)]
    PT = [K.tile(pfx + 'PT%d' % i, [128, 2, 512], BF16) for i in range(3)]
    items = []
    for gi, g in enumerate(groups):
        assert (g['k1'] - g['k0']) % 2 == 0
        for kt in range(g['k0'], g['k1'], 2):
            items.append((gi, g, kt))
    LA = 1
    deferred = []
    for idx in range(len(items) + LA):
        while deferred and deferred[0][0] <= idx:
            deferred.pop(0)[1]()
        if idx < len(items):
            gi, g, kt = items[idx]
            nq = g['nq']
            s_, pt = pS[idx % 2], PT[idx % 3]
            for a in range(2):
                K.mm(s_, s_[:, a, 0:nq], kT, g['k'](kt + a), qT, g['q'], True, True, attach=True)
            K.act(pt, pt[:, :, 0:nq], s_, s_[:, :, 0:nq], AF.Exp, scale=scale, attach=True)
        j = idx - LA
        if j >= 0:
            gi, g, kt = items[j]
            nq = g['nq']
            pt = PT[j % 3]
            o, z = pO[gi % 2], pZ[gi % 2]
            for a in range(2):
                first = (kt + a == g['k0'])
                last = (kt + a == g['k1'] - 1)
                K.mm(o, o[:, 0:nq], v_tm, g['v'](kt + a), pt, pt[:, a, 0:nq], first, last, attach=True)
                K.mm(z, z[:, 0:nq], cstb, ONESB, pt, pt[:, a, 0:nq], first, last, attach=True)
            if kt + 2 >= g['k1']:
                cont = g['fin'](o, z)
                if cont is not None:
                    deferred.append((idx + 3, cont))
    for d in deferred:
        d[1]()


QBLOCKS = [(0, 256, 0, 2)] + [(256 + 512 * i, 256 + 512 * (i + 1), 0, NT) for i in range(4)]


def phase_gqa(K, G, hT, wsl):
    L = G['L']
    WD, YBR, ybr_d = G['WD'], G['YBR'], G['ybr_d']
    cst, cstb, IDB = G['cst'], G['cstb'], G['IDB']
    with K.scope():
        ropeG = K.tile('ropeG', [128, 16, 2, 128], F32, dma=True)
        K.load('sp', ropeG, ropeG[:], WD, G['ropeG_in'])
        qT = K.tile('qT', [128, 8, T], BF16)
        kT = K.tile('kT', [128, 2, T], BF16)
        v_tm = K.tile('v_tm', [128, NT, 256], BF16)
        gn = K.tile('gn', [128, 2, 128], F32, dma=True)
        K.load('sp', gn, gn[:, 0, :], WD, G['gq_norm'][L:L + 1, :].broadcast_to([128, 128]))
        K.load('sp', gn, gn[:, 1, :], WD, G['gk_norm'][L:L + 1, :].broadcast_to([128, 128]), part=True)
        with K.scope('gqa_proj'):
            W = K.tile('Wg', [128, KC, 1536], BF16, dma=True)
            K.load('pool', W, W[:, :, 0:512], WD, wsl(O_GQ, O_GQ + 512))
            K.load('pool', W, W[:, :, 512:1024], WD, wsl(O_GQ + 512, O_GQ + 1024), part=True)
            K.load('pool', W, W[:, :, 1024:1536], WD, wsl(O_GK, O_GK + 512), part=True)
            pq = [K.psum('pq%d' % i, [128, 512], F32) for i in range(3)]
            ptr = [K.psum('ptr%d' % i, [128, 512], BF16) for i in range(2)]
            qf = [K.tile('qf%d' % i, [128, 10, 128], F32) for i in range(2)]
            sq = [K.tile('sq%d' % i, [128, 10, 128], F32) for i in range(2)]
            qr = [K.tile('qr%d' % i, [128, 10, 128], F32) for i in range(2)]
            qb_ = [K.tile('qb%d' % i, [128, 10, 128], BF16) for i in range(2)]
            ssq = [K.tile('ssq%d' % i, [128, 10], F32) for i in range(2)]
            def stA(tt):
                i = tt % 2
                f, s2, r, qb16, ss = qf[i], sq[i], qr[i], qb_[i], ssq[i]
                for j in range(3):
                    proj_tm(K, pq[j], hT, tt, W, j * 512, 512)
                K.copy('act', f, f[:, 0:4, :], pq[0], pq[0][:].rearrange("p (h d) -> p h d", d=128))
                K.copy('act', f, f[:, 4:8, :], pq[1], pq[1][:].rearrange("p (h d) -> p h d", d=128))
                K.copy('act', f, f[:, 8:10, :], pq[2], pq[2][:, 0:256].rearrange("p (h d) -> p h d", d=128))
                K.copy('act', v_tm, v_tm[:, tt, :], pq[2], pq[2][:, 256:512])
                K.act(s2, s2[:], f, f[:], AF.Square)
                K.op('dve', lambda e: e.reduce_sum(out=ss[:], in_=s2[:], axis=AX.X), reads=[s2], writes=[ss])
                K.ts('dve', ss, ss[:], ss, ss[:], 1.0 / 128, EPS, ALU.mult, ALU.add)
                K.rsqrt(ss, ss[:])
                K.tt('dve', f, f[:], f, f[:], ss, bc(ss[:, :, None], [128, 10, 128]), ALU.mult)
                K.tt('dve', f, f[:, 0:8, :], f, f[:, 0:8, :], gn, bc(gn[:, 0:1, :], [128, 8, 128]), ALU.mult)
                K.tt('dve', f, f[:, 8:10, :], f, f[:, 8:10, :], gn, bc(gn[:, 1:2, :], [128, 2, 128]), ALU.mult)
                if tt >= 2:
                    rope(K, 'pool', r, r[:], f, f[:], ropeG, ropeG[:, tt - 2, 0, :], ropeG[:, tt - 2, 1, :],
                         s2, s2[:], 10, 128, e_first='dve')
                    K.copy('dve', qb16, qb16[:], r, r[:])
                else:
                    K.copy('dve', qb16, qb16[:], f, f[:])

            def stB(tt):
                qb16 = qb_[tt % 2]
                for half in range(3):
                    hh = [(0, 4), (4, 8), (8, 10)][half]
                    p = ptr[(tt * 3 + half) % 2]
                    for h in range(hh[0], hh[1]):
                        K.tr(p, p[:, (h - hh[0]) * 128:(h - hh[0] + 1) * 128], qb16, qb16[:, h, :], cstb, IDB)
                    nh = hh[1] - hh[0]
                    src = p[:, 0:nh * 128].rearrange("p (h t) -> p h t", t=128)
                    if half < 2:
                        K.copy('act', qT, qT[:, hh[0]:hh[1], tt * 128:(tt + 1) * 128], p, src)
                    else:
                        K.copy('act', kT, kT[:, :, tt * 128:(tt + 1) * 128], p, src)

            for tt in range(NT + 1):
                if tt < NT:
                    stA(tt)
                if tt >= 1:
                    stB(tt - 1)
        with K.scope('gqa_attn'):
            og = [K.tile('og%d' % i, [128, 512], BF16, dma=True) for i in range(3)]
            rz = [K.tile('rz%d' % i, [128, 512], F32) for i in range(2)]
            cnt = [0]

            def mkfin(m, q0, q1):
                def fin(o, z):
                    nq = q1 - q0
                    i = cnt[0]
                    cnt[0] += 1
                    r = rz[i % 2]
                    ob = og[i % 3]
                    K.op('dve', lambda e: e.reciprocal(out=r[:, 0:nq], in_=z[:, 0:nq]), reads=[z], writes=[r])
                    K.tt('dve', ob, ob[:, 0:nq], o, o[:, 0:nq], r, r[:, 0:nq], ALU.mult)
                    K.store('sp', YBR, ybr_d[(16 + m) * 128:(17 + m) * 128, q0:q1], ob, ob[:, 0:nq])
                return fin

            groups = []
            for m in range(8):
                for (q0, q1, k0, k1) in QBLOCKS:
                    groups.append(dict(
                        k=(lambda kt, m=m: kT[:, m // 4, kt * 128:(kt + 1) * 128]),
                        q=qT[:, m, q0:q1],
                        v=(lambda kt, m=m: v_tm[:, kt, (m // 4) * 128:(m // 4 + 1) * 128]),
                        nq=q1 - q0, k0=k0, k1=k1, fin=mkfin(m, q0, q1)))
            attention_core(K, G, groups, 128 ** -0.5, kT, qT, v_tm, 'a')


def phase_diff(K, G, hT, wsl, hh):
    L, lam_init = G['L'], G['lam_init']
    WD, YBR, ybr_d = G['WD'], G['YBR'], G['ybr_d']
    cst, cstb, IDB, ONESF = G['cst'], G['cstb'], G['IDB'], G['ONESF']
    with K.scope():
        ropeD = K.tile('ropeD', [128, 16, 2, 64], F32, dma=True)
        K.load('sp', ropeD, ropeD[:], WD, G['ropeD_in'])
        qT = K.tile('dqT', [128, 4, T], BF16)
        kT = K.tile('dkT', [128, 4, T], BF16)
        v_tm = K.tile('dv_tm', [128, NT, 512], BF16)
        lm = K.tile('lm', [128, 4, 64], F32, dma=True)
        K.load('sp', lm, lm[:].rearrange("p a b -> p (a b)"), WD, G['dlam'][L:L + 1, :].broadcast_to([128, 256]))
        lt = K.tile('lt', [128, 2, 64], F32)
        ls = K.tile('ls', [128, 4], F32)
        K.tt('dve', lt, lt[:, 0, :], lm, lm[:, 0, :], lm, lm[:, 1, :], ALU.mult)
        K.tt('dve', lt, lt[:, 1, :], lm, lm[:, 2, :], lm, lm[:, 3, :], ALU.mult)
        K.op('dve', lambda e: e.reduce_sum(out=ls[:, 0:2], in_=lt[:], axis=AX.X), reads=[lt], writes=[ls])
        K.act(ls, ls[:, 0:2], ls, ls[:, 0:2], AF.Exp)
        K.tt('dve', ls, ls[:, 2:3], ls, ls[:, 0:1], ls, ls[:, 1:2], ALU.subtract)
        K.ts('dve', ls, ls[:, 3:4], ls, ls[:, 2:3], lam_init, -1.0, ALU.add, ALU.mult)
        dnw = K.tile('dnw', [128, 1], F32, dma=True)
        K.load('sp', dnw, dnw[:], WD, G['dnw_fm'][L])
        K.ts('dve', dnw, dnw[:], dnw, dnw[:], 1.0 - lam_init, None, ALU.mult)
        with K.scope('diff_proj'):
            Wb = [K.tile('Wd%d' % i, [128, KC, 512], BF16, dma=True) for i in range(3)]
            for j, c0 in enumerate((O_DQ, O_DK, O_DV)):
                K.load('pool', Wb[j], Wb[j][:], WD, wsl(c0 + hh * 512, c0 + (hh + 1) * 512))
            pq = [K.psum('dpq%d' % i, [128, 512], F32) for i in range(3)]
            ptr = [K.psum('dptr%d' % i, [128, 512], BF16) for i in range(2)]
            qf = [K.tile('dqf%d' % i, [128, 8, 64], F32) for i in range(4)]
            tm = [K.tile('dtm%d' % i, [128, 8, 64], F32) for i in range(4)]
            qr = [K.tile('dqr%d' % i, [128, 8, 64], F32) for i in range(4)]
            q16 = [K.tile('dq16%d' % i, [128, 512], BF16) for i in range(4)]
            nn = [0]

            def stA(tt):
                for j in range(3):
                    p = pq[nn[0] % 3]
                    nn[0] += 1
                    proj_tm(K, p, hT, tt, Wb[j], 0, 512)
                    if j == 2:
                        K.copy('act', v_tm, v_tm[:, tt, :], p, p[:])
                        continue
                    i = (tt % 2) * 2 + j
                    f, t_, r, b16 = qf[i], tm[i], qr[i], q16[i]
                    if tt >= 2:
                        K.copy('act', f, f[:].rearrange("p h d -> p (h d)"), p, p[:])
                        rope(K, 'pool' if j % 2 else 'dve', r, r[:], f, f[:], ropeD, ropeD[:, tt - 2, 0, :],
                             ropeD[:, tt - 2, 1, :], t_, t_[:], 8, 64)
                        K.copy('act', b16, b16[:], r, r[:].rearrange("p h d -> p (h d)"))
                    else:
                        K.copy('act', b16, b16[:], p, p[:])

            def stB(tt):
                for j in range(2):
                    i = (tt % 2) * 2 + j
                    b16 = q16[i]
                    q = ptr[j]
                    for c in range(4):
                        K.tr(q, q[:, c * 128:(c + 1) * 128], b16, b16[:, c * 128:(c + 1) * 128], cstb, IDB)
                    dst = qT if j == 0 else kT
                    K.copy('dve', dst, dst[:, :, tt * 128:(tt + 1) * 128], q,
                           q[:].rearrange("p (c t) -> p c t", t=128))

            for tt in range(NT + 1):
                if tt < NT:
                    stA(tt)
                if tt >= 1:
                    stB(tt - 1)
        with K.scope('diff_attn'):
            o1 = [K.tile('o1_%d' % i, [128, 512], F32) for i in range(2)]
            o2 = [K.tile('o2_%d' % i, [128, 512], F32) for i in range(2)]
            osq = [K.tile('osq%d' % i, [128, 512], F32) for i in range(2)]
            rs = [K.tile('rs%d' % i, [128, 512], F32) for i in range(2)]
            od = [K.tile('od%d' % i, [128, 512], BF16, dma=True) for i in range(3)]
            rz = [K.tile('drz%d' % i, [128, 512], F32) for i in range(2)]
            cnt = [0]

            def mkfin(h, j, q0, q1):
                def fin(o, z):
                    nq = q1 - q0
                    i = cnt[0] % 2
                    r = rz[j]
                    K.op('dve', lambda e: e.reciprocal(out=r[:, 0:nq], in_=z[:, 0:nq]), reads=[z], writes=[r])
                    if j == 0:
                        K.tt('dve', o1[i], o1[i][:, 0:nq], o, o[:, 0:nq], r, r[:, 0:nq], ALU.mult)
                        return
                    K.tt('dve', o2[i], o2[i][:, 0:nq], o, o[:, 0:nq], r, r[:, 0:nq], ALU.mult)
                    K.stt('dve', o1[i], o1[i][:, 0:nq], o2[i], o2[i][:, 0:nq], ls[:, 3:4], o1[i], o1[i][:, 0:nq],
                          ALU.mult, ALU.add, extra_reads=[ls])
                    K.tt('pool', osq[i], osq[i][:, 0:nq], o1[i], o1[i][:, 0:nq], o1[i], o1[i][:, 0:nq], ALU.mult)
                    k = cnt[0] % 3
                    cnt[0] += 1

                    def cont():
                        K.mm(z, z[:, 0:nq], cst, ONESF, osq[i], osq[i][:, 0:nq], True, True)
                        K.ts('dve', rs[i], rs[i][:, 0:nq], z, z[:, 0:nq], 1.0 / 128, EPS, ALU.mult, ALU.add)
                        K.rsqrt(rs[i], rs[i][:, 0:nq])
                        K.stt('dve', od[k], od[k][:, 0:nq], o1[i], o1[i][:, 0:nq], dnw[:, 0:1], rs[i], rs[i][:, 0:nq],
                              ALU.mult, ALU.mult, extra_reads=[dnw])
                        hg = hh * 4 + h
                        K.store('sp', YBR, ybr_d[(24 + hg) * 128:(25 + hg) * 128, q0:q1], od[k], od[k][:, 0:nq])
                    return cont
                return fin

            groups = []
            for h in range(4):
                for (q0, q1, k0, k1) in QBLOCKS:
                    for j in range(2):
                        ps_ = slice(j * 64, (j + 1) * 64)
                        groups.append(dict(
                            k=(lambda kt, h=h, ps_=ps_: kT[ps_, h, kt * 128:(kt + 1) * 128]),
                            q=qT[ps_, h, q0:q1],
                            v=(lambda kt, h=h: v_tm[:, kt, h * 128:(h + 1) * 128]),
                            nq=q1 - q0, k0=k0, k1=k1, fin=mkfin(h, j, q0, q1)))
            attention_core(K, G, groups, 64 ** -0.5, kT, qT, v_tm, 'b')


def layernorm_tile(K, e, xn, xn_ap, src, src_ap, rows, gi, bi, st, mv):
    for c in range(2):
        K.op('dve', lambda en: en.bn_stats(out=st[:, c, :], in_=src_ap[:, c * 512:(c + 1) * 512]),
             reads=[src], writes=[st])
    K.op('dve', lambda en: en.bn_aggr(out=mv[:, 0:2], in_=st[:]), reads=[st], writes=[mv])
    K.ts('dve', mv, mv[:, 2:3], mv, mv[:, 1:2], EPS, None, ALU.add)
    K.rsqrt(mv, mv[:, 2:3])
    K.ts('dve', xn, xn_ap, src, src_ap, mv[:, 0:1], mv[:, 2:3], ALU.subtract, ALU.mult, extra_reads=[mv])
    K.tt(e, xn, xn_ap, xn, xn_ap, rows, rows[:, gi, :], ALU.mult)
    K.tt(e, xn, xn_ap, xn, xn_ap, rows, rows[:, bi, :], ALU.add)


def phase_merge(K, G, hT, wsl, th):
    L, b = G['L'], G['b']
    WD, YBR, ybr_d, XB, xB_d = G['WD'], G['YBR'], G['ybr_d'], G['XB'], G['xB_d']
    cst, cstb, IDB = G['cst'], G['cstb'], G['IDB']
    xsrc = G['xsrc']
    HT = NT // 2
    tiles = list(range(th * HT, (th + 1) * HT))
    with K.scope():
        macc = K.tile('macc', [128, HT, D], F32)
        bg = K.tile('bg', [128, 3072], F32, dma=True)
        K.load('sp', bg, bg[:], WD, G['b_gate'][L:L + 1, :].broadcast_to([128, 3072]))
        branches = [(G['w_sso'], 16, 0), (G['w_gqo'], 8, 16), (G['w_dfo'], 8, 24)]
        for br, (wout, nck, c0) in enumerate(branches):
            with K.scope('merge_br'):
                Wb = K.tile('Wbr', [128, nck, D], BF16, dma=True)
                for c4 in range(0, nck, 4):
                    K.load('pool', Wb, Wb[:, c4:c4 + 4, :], WD,
                           wout[L].rearrange("(c p) f -> p c f", p=128)[:, c4:c4 + 4, :], part=(c4 > 0))
                Wg = K.tile('Wgt', [128, KC, D], BF16, dma=True)
                K.load('pool', Wg, Wg[:, :, 0:512], WD, wsl(O_GATE + br * D, O_GATE + br * D + 512))
                K.load('pool', Wg, Wg[:, :, 512:D], WD, wsl(O_GATE + br * D + 512, O_GATE + (br + 1) * D), part=True)
                yt = [K.tile('ybt%d' % i, [128, nck, 128], BF16, dma=True) for i in range(3)]
                pg = [K.psum('pg%d' % i, [128, 512], F32) for i in range(2)]
                pp = [K.psum('pp%d' % i, [128, 512], F32) for i in range(2)]
                gt = [K.tile('gt%d' % i, [128, 512], F32) for i in range(2)]
                n = 0
                for ti, tt in enumerate(tiles):
                    y = yt[ti % 3]
                    K.load('sp', y, y[:], YBR,
                           ybr_d[c0 * 128:(c0 + nck) * 128, tt * 128:(tt + 1) * 128].rearrange("(c p) t -> p c t", p=128))
                    for hf in range(2):
                        i = n % 2
                        n += 1
                        g_, p_, gg = pg[i], pp[i], gt[i]
                        cs = slice(hf * 512, (hf + 1) * 512)
                        proj_tm(K, g_, hT, tt, Wg, hf * 512, 512)
                        K.tt('dve', gg, gg[:], g_, g_[:], bg, bg[:, br * D + hf * 512:br * D + (hf + 1) * 512], ALU.add)
                        K.act(gg, gg[:], gg, gg[:], AF.Sigmoid)
                        for c in range(nck):
                            K.mm(p_, p_[:], y, y[:, c, :], Wb, Wb[:, c, cs], c == 0, c == nck - 1)
                        if br == 0:
                            K.tt('dve', macc, macc[:, ti, cs], p_, p_[:], gg, gg[:], ALU.mult)
                        else:
                            K.tt('dve', gg, gg[:], p_, p_[:], gg, gg[:], ALU.mult)
                            K.tt('pool', macc, macc[:, ti, cs], macc, macc[:, ti, cs], gg, gg[:], ALU.add)
        with K.scope('merge_wo'):
            rows = load_rows(K, G, 2, G['ln1_g'], G['ln1_b'])
            Wo = K.tile('Wo', [128, KC, D], BF16, dma=True)
            for c4 in range(0, KC, 4):
                K.load('pool', Wo, Wo[:, c4:c4 + 4, :], WD,
                       G['w_o'][L].rearrange("(c p) f -> p c f", p=128)[:, c4:c4 + 4, :], part=(c4 > 0))
            mb = [K.tile('mb%d' % i, [128, D], BF16) for i in range(2)]
            mT = [K.tile('mT%d' % i, [128, KC, 128], BF16) for i in range(2)]
            xt = [K.tile('xt%d' % i, [128, D], F32, dma=True) for i in range(2)]
            xn = [K.tile('xn%d' % i, [128, D], F32, dma=True) for i in range(2)]
            st = [K.tile('st%d' % i, [128, 2, 6], F32) for i in range(2)]
            mv = [K.tile('mv%d' % i, [128, 4], F32) for i in range(2)]
            ptr = [K.psum('mptr%d' % i, [128, 1024], BF16) for i in range(2)]
            py = [K.psum('mpy%d' % i, [128, 512], F32) for i in range(4)]
            for ti, tt in enumerate(tiles):
                i = ti % 2
                isc = 1 if tt < 2 else 0
                x_ = xt[i]
                db, ap = xsrc(tt, 1)
                K.load('sp', x_, x_[:], db, ap)
                K.copy('act', mb[i], mb[i][:], macc, macc[:, ti, :])
                q = ptr[i]
                for c in range(KC):
                    K.tr(q, q[:, c * 128:(c + 1) * 128], mb[i], mb[i][:, c * 128:(c + 1) * 128], cstb, IDB)
                K.copy('act', mT[i], mT[i][:].rearrange("p c t -> p (c t)"), q, q[:])
                xo = xn[i]
                for hf in range(2):
                    p_ = py[(ti * 2 + hf) % 4]
                    cs = slice(hf * 512, (hf + 1) * 512)
                    for c in range(KC):
                        K.mm(p_, p_[:], mT[i], mT[i][:, c, :], Wo, Wo[:, c, cs], c == 0, c == KC - 1)
                    K.tt('dve', xo, xo[:, cs], p_, p_[:], rows, rows[:, isc, cs], ALU.mult)
                K.stt('dve', xo, xo[:], x_, x_[:], ALPHA, xo, xo[:], ALU.mult, ALU.add)
                layernorm_tile(K, 'pool', xo, xo[:], xo, xo[:], rows, 2, 3, st[i], mv[i])
                K.store('sp', XB, xB_d[tt * 128:(tt + 1) * 128, :], xo, xo[:])


def phase_ffn(K, G, hT):
    L, b = G['L'], G['b']
    WD, XB, xB_d, XA, xA_d, OUT, out = G['WD'], G['XB'], G['xB_d'], G['XA'], G['xA_d'], G['OUT'], G['out']
    NF = FFH // 128
    HTOK = T // 2
    with K.scope():
        rows = load_rows(K, G, 5, G['ln2_g'], G['ln2_b'])
        Wo = K.tile('Wfo', [128, NF, D], BF16, dma=True)
        wvo = G['ffn_wo'][L].rearrange("(c p) f -> p c f", p=128)
        for c0 in range(0, NF, 4):
            c1 = min(NF, c0 + 4)
            K.load('pool', Wo, Wo[:, c0:c1, :], WD, wvo[:, c0:c1, :], part=(c0 > 0))
        uT = K.tile('uT', [128, NF, HTOK], BF16)
        for th in range(2):
            base = th * HTOK
            blocks = [(0, 256), (256, 768), (768, 1152)] if th == 0 else [(0, 512), (512, 1024), (1024, 1152)]
            with K.scope('ffn_in'):
                Wa = [K.tile('Wa%d' % i, [128, KC, 256], BF16, dma=True) for i in range(3)]
                pa = [K.psum('pa%d' % i, [128, 512], F32) for i in range(3)]
                pb = [K.psum('pb%d' % i, [128, 512], F32) for i in range(3)]
                sa = [K.tile('sa%d' % i, [128, 512], F32) for i in range(2)]
                wv = G['ffn_wi'][L].rearrange("(kc p) f -> p kc f", p=128)
                n = 0
                for fc in range(NF):
                    W = Wa[fc % 3]
                    K.load('pool', W, W[:, :, 0:128], WD, wv[:, :, fc * 128:(fc + 1) * 128])
                    K.load('pool', W, W[:, :, 128:256], WD, wv[:, :, FFH + fc * 128:FFH + (fc + 1) * 128], part=True)
                    for (a0, a1) in blocks:
                        na = a1 - a0
                        A, B_, s_ = pa[n % 3], pb[n % 3], sa[n % 2]
                        n += 1
                        for kc in range(KC):
                            K.mm(A, A[:, 0:na], W, W[:, kc, 0:128], hT, hT[:, kc, base + a0:base + a1],
                                 kc == 0, kc == KC - 1)
                        for kc in range(KC):
                            K.mm(B_, B_[:, 0:na], W, W[:, kc, 128:256], hT, hT[:, kc, base + a0:base + a1],
                                 kc == 0, kc == KC - 1)
                        K.act(s_, s_[:, 0:na], A, A[:, 0:na], AF.Silu)
                        K.tt('dve', uT, uT[:, fc, a0:a1], s_, s_[:, 0:na], B_, B_[:, 0:na], ALU.mult)
            with K.scope('ffn_out'):
                xt = [K.tile('fxt%d' % i, [128, D], F32, dma=True) for i in range(2)]
                xn = [K.tile('fxn%d' % i, [128, D], F32, dma=True) for i in range(2)]
                st = [K.tile('fst%d' % i, [128, 2, 6], F32) for i in range(2)]
                mv = [K.tile('fmv%d' % i, [128, 4], F32) for i in range(2)]
                py = [K.psum('fpy%d' % i, [128, 512], F32) for i in range(4)]
                for ti in range(NT // 2):
                    tt = th * (NT // 2) + ti
                    i = ti % 2
                    isc = 1 if tt < 2 else 0
                    x_ = xt[i]
                    K.load('sp', x_, x_[:], XB, xB_d[tt * 128:(tt + 1) * 128, :])
                    xo = xn[i]
                    for hf in range(2):
                        p_ = py[(ti * 2 + hf) % 4]
                        cs = slice(hf * 512, (hf + 1) * 512)
                        for c in range(NF):
                            K.mm(p_, p_[:], uT, uT[:, c, ti * 128:(ti + 1) * 128], Wo, Wo[:, c, cs],
                                 c == 0, c == NF - 1)
                        K.tt('dve', xo, xo[:, cs], p_, p_[:], rows, rows[:, isc, cs], ALU.mult)
                    K.stt('dve', xo, xo[:], x_, x_[:], ALPHA, xo, xo[:], ALU.mult, ALU.add)
                    layernorm_tile(K, 'pool', xo, xo[:], xo, xo[:], rows, 2, 3, st[i], mv[i])
                    if G['L'] == G['depth_run'] - 1:
                        if tt >= 2:
                            K.store('sp', OUT, out[b, (tt - 2) * 128:(tt - 1) * 128, :], xo, xo[:])
                    else:
                        K.store('sp', XA, xA_d[tt * 128:(tt + 1) * 128, :], xo, xo[:])


def host_consts():
    p = np.arange(128)[:, None]
    f = np.arange(128)[None, :]
    c = np.zeros((128, 6, 128), np.float32)
    c[:, 0] = (p == f)
    c[:, 1] = (p <= f)
    c[:, 2] = (p > f)
    c[:, 3] = (p >= f)
    c[:, 4] = (p < f)
    c[:, 5] = 1.0
    return c


def host_rope(hd):
    t = np.arange(LAT)
    pos_row = (t // 64).astype(np.float32)
    pos_col = (t % 64).astype(np.float32)
    d_axis = hd // 2
    inv = (10000.0 ** (-np.arange(0, d_axis, 2, dtype=np.float32) / d_axis)).astype(np.float32)
    ar = pos_row[:, None] * inv
    ac = pos_col[:, None] * inv
    ang = np.concatenate([ar, ar, ac, ac], axis=-1).astype(np.float32)
    cos = np.cos(ang).astype(np.float32)
    sin = np.sin(ang).astype(np.float32)
    q = hd // 4
    sign = np.concatenate([-np.ones(q), np.ones(q), -np.ones(q), np.ones(q)]).astype(np.float32)
    tab = np.stack([cos, sin * sign], axis=1)
    return np.ascontiguousarray(tab.reshape(16, 128, 2, hd).transpose(1, 0, 2, 3))


_CACHE = {}


def run(inputs, NB, depth_run, ncores=8, trace=False):
    key = (NB, depth_run)
    if key not in _CACHE:
        _CACHE[key] = build_program(NB, depth_run)
    nc = _CACHE[key]
    f = lambda k: np.ascontiguousarray(np.asarray(inputs[k], dtype=np.float32))
    shared = {
        "ada_w": f('ada_w'), "ada_b": f('ada_b'),
        "ada_b_fm": np.ascontiguousarray(f('ada_b').reshape(DEPTH, 48, 128).transpose(0, 2, 1)),
        "w_in": f('w_in'), "b_gate": f('b_gate'),
        "conv_w": np.ascontiguousarray(f('ssm_conv_w').reshape(DEPTH, 5, 24, 128).transpose(0, 3, 2, 1)),
        "conv_b": np.ascontiguousarray(f('ssm_conv_b').reshape(DEPTH, 24, 128).transpose(0, 2, 1)),
        "ssm_dt_bias": f('ssm_dt_bias').reshape(DEPTH, 64), "ssm_a_log": f('ssm_a_log').reshape(DEPTH, 64),
        "ssm_d": f('ssm_d'), "ssm_norm_w": f('ssm_norm_w'), "w_ssm_out": f('w_ssm_out'),
        "gqa_q_norm": f('gqa_q_norm'), "gqa_k_norm": f('gqa_k_norm'), "w_gqa_out": f('w_gqa_out'),
        "diff_lambda": f('diff_lambda').reshape(DEPTH, 256),
        "diff_norm_w": f('diff_norm_w').reshape(DEPTH, 128, 1),
        "w_diff_out": f('w_diff_out'), "w_o": f('w_o'), "ln1_g": f('ln1_g'), "ln1_b": f('ln1_b'),
        "ffn_w_in": f('ffn_w_in'), "ffn_w_out": f('ffn_w_out'), "ln2_g": f('ln2_g'), "ln2_b": f('ln2_b'),
        "consts": host_consts(), "ropeG": host_rope(128), "ropeD": host_rope(64),
    }
    x, c, ctx, c_ctx = f('x'), f('c'), f('ctx'), f('c_ctx')
    in_maps = []
    for i in range(ncores):
        sl = slice(i * NB, (i + 1) * NB)
        c5 = np.zeros((5, D), np.float32)
        c5[:NB] = c[sl]
        c5[4] = c_ctx
        cT = np.ascontiguousarray(c5.reshape(5, KC, 128).transpose(2, 1, 0))
        m = dict(shared)
        m.update({"x": x[sl], "ctx": ctx[sl], "cT": cT})
        in_maps.append(m)
    if trace:
        res = run_bass_kernel_spmd(nc, in_maps, core_ids=list(range(ncores)), trace=True)
        print("exec_time_ns", res.exec_time_ns)
    else:
        res = run_bass_kernel_spmd(nc, in_maps, core_ids=list(range(ncores)))
    return np.concatenate([r["out"] for r in res.results], axis=0)


def kernel(**inputs):
    return run(inputs, 4, DEPTH).astype(np.float32)
```
